# Optimizing a Trainium2 kernel written in Bass

```python
import math
import jax, jax.numpy as jnp
from jax import lax
import numpy as np

D_MODEL = 1024
BATCH = 4
SEQ = 4096
DEPTH = 2
DEC_BATCH = 32
DEC_SEQ = 4
PAST_LEN = 8192
PAGE_SIZE = 128

N_AB_LAYERS = (DEPTH + 1) // 2
N_C_LAYERS = DEPTH // 2
POOL_WIDTH = D_MODEL
POOL_WINDOWS = (2, 4, 8, 16)
POOL_GROUPS = len(POOL_WINDOWS)
POOL_GROUP_DIM = POOL_WIDTH // POOL_GROUPS
POOL_BUF = max(POOL_WINDOWS) - 1
CONV_WIDTH = D_MODEL
CONV_K = 31
CONV_BUF = CONV_K - 1
AB_IN = 2 * POOL_WIDTH + 3 * CONV_WIDTH
AB_MIX = POOL_WIDTH + CONV_WIDTH
C_HEADS = 8
C_HEAD_DIM = 64
C_V_DIM = 2 * C_HEAD_DIM
C_WIDTH = C_HEADS * C_V_DIM
C_IN = 4 * C_WIDTH
ATTN_SCALE = C_HEAD_DIM ** -0.5
Q_BLOCK = 128
EPS = 1e-6
NEG_INF = -1e30

kernel_name = "pool_conformer_diffattn_hybrid_step"


def rmsnorm(x, w):
    xf = x.astype(jnp.float32)
    y = xf * lax.rsqrt(jnp.mean(xf * xf, axis=-1, keepdims=True) + EPS)
    return (y * w.astype(jnp.float32)).astype(x.dtype)


def layernorm(x, w, b):
    xf = x.astype(jnp.float32)
    mu = jnp.mean(xf, axis=-1, keepdims=True)
    xc = xf - mu
    y = xc * lax.rsqrt(jnp.mean(xc * xc, axis=-1, keepdims=True) + EPS)
    return (y * w.astype(jnp.float32) + b.astype(jnp.float32)).astype(x.dtype)


def causal_multiscale_pool(full, n_valid):
    B, L, _ = full.shape
    T = L - POOL_BUF
    cs = jnp.cumsum(full.astype(jnp.float32), axis=1)
    cs = jnp.concatenate([jnp.zeros((B, 1, POOL_WIDTH), jnp.float32), cs], axis=1)
    hi = cs[:, POOL_BUF + 1:]
    pos = jnp.arange(T)
    means = []
    for g, w in enumerate(POOL_WINDOWS):
        sl = slice(g * POOL_GROUP_DIM, (g + 1) * POOL_GROUP_DIM)
        lo = cs[:, POOL_BUF + 1 - w: POOL_BUF + 1 - w + T, sl]
        cnt = jnp.minimum(w, pos + 1 + n_valid).astype(jnp.float32)
        means.append((hi[..., sl] - lo) / cnt[None, :, None])
    return jnp.concatenate(means, axis=-1)


def pool_conv_layer(x, pool_buf, conv_buf, n_valid, norm_w, w_in, pool_w, pool_scale,
                    conv_w, conv_b, ln_w, ln_b, w_out):
    B, T, _ = x.shape
    h = rmsnorm(x, norm_w)
    z = h @ w_in
    xa, ga, u, v, gb = jnp.split(
        z, [POOL_WIDTH, 2 * POOL_WIDTH, 2 * POOL_WIDTH + CONV_WIDTH, 2 * POOL_WIDTH + 2 * CONV_WIDTH], axis=-1)
    pool_full = jnp.concatenate([pool_buf, xa], axis=1)
    mean = causal_multiscale_pool(pool_full, n_valid)
    d = (mean - xa.astype(jnp.float32)).astype(x.dtype).reshape(B, T, POOL_GROUPS, POOL_GROUP_DIM)
    a = jnp.einsum('btgc,gce->btge', d, pool_w).reshape(B, T, POOL_WIDTH) * pool_scale
    glu = u * jax.nn.sigmoid(v)
    conv_full = jnp.concatenate([conv_buf, glu], axis=1)
    c = lax.conv_general_dilated(conv_full, conv_w[:, None, :], window_strides=(1,), padding='VALID',
                                 dimension_numbers=('NWC', 'WIO', 'NWC'),
                                 feature_group_count=CONV_WIDTH) + conv_b
    c = jax.nn.silu(layernorm(c, ln_w, ln_b))
    mix = jnp.concatenate([a * jax.nn.silu(ga), c * jax.nn.silu(gb)], axis=-1)
    y = x + mix @ w_out
    return y, pool_full[:, -POOL_BUF:], conv_full[:, -CONV_BUF:]


def diff_qkv(x, norm_w, w_in, qn_w, kn_w):
    B, T, _ = x.shape
    h = rmsnorm(x, norm_w)
    q, k, v, g = jnp.split(h @ w_in, 4, axis=-1)
    q = rmsnorm(q.reshape(B, T, C_HEADS, 2, C_HEAD_DIM), qn_w)
    k = rmsnorm(k.reshape(B, T, C_HEADS, 2, C_HEAD_DIM), kn_w)
    v = v.reshape(B, T, C_HEADS, C_V_DIM)
    return q, k, v, g


def diff_lambda(lq1, lk1, lq2, lk2, lambda_init):
    f32 = jnp.float32
    return (jnp.exp(jnp.sum(lq1.astype(f32) * lk1.astype(f32)))
            - jnp.exp(jnp.sum(lq2.astype(f32) * lk2.astype(f32))) + lambda_init)


def diff_attention(q, k, v, mask, lam):
    s = jnp.einsum('bqhcd,bkhcd->bhcqk', q, k, preferred_element_type=jnp.float32) * ATTN_SCALE
    s = jnp.where(mask[None, None, None], s, NEG_INF)
    p = jax.nn.softmax(s, axis=-1)
    p = p[:, :, 0] - lam * p[:, :, 1]
    return jnp.einsum('bhqk,bkhe->bqhe', p, v.astype(jnp.float32))


def prompt_diff_attention(q, k, v, lam):
    B, T = q.shape[:2]
    nb = T // Q_BLOCK
    qb = q.reshape(B, nb, Q_BLOCK, C_HEADS, 2, C_HEAD_DIM).swapaxes(0, 1)
    kpos = jnp.arange(T)

    def block(args):
        i, qi = args
        qpos = i * Q_BLOCK + jnp.arange(Q_BLOCK)
        return diff_attention(qi, k, v, kpos[None, :] <= qpos[:, None], lam)

    o = lax.map(block, (jnp.arange(nb), qb))
    return o.swapaxes(0, 1).reshape(B, T, C_HEADS, C_V_DIM)


def sample_diff_attention(q, k_new, v_new, cache_k, cache_v, page_table, lam):
    Bd, Tn = q.shape[:2]
    past = page_table.shape[1] * PAGE_SIZE
    k_past = cache_k[page_table].reshape(Bd, past, C_HEADS, 2, C_HEAD_DIM)
    v_past = cache_v[page_table].reshape(Bd, past, C_HEADS, C_V_DIM)
    k_all = jnp.concatenate([k_past, k_new], axis=1)
    v_all = jnp.concatenate([v_past, v_new], axis=1)
    kpos = jnp.arange(past + Tn)
    qpos = past + jnp.arange(Tn)
    return diff_attention(q, k_all, v_all, kpos[None, :] <= qpos[:, None], lam)


def diff_output(x, o, g, subln_w, lambda_init, w_out):
    B, T = x.shape[:2]
    o = rmsnorm(o, subln_w) * (1.0 - lambda_init)
    o = o.reshape(B, T, C_WIDTH).astype(x.dtype)
    return x + (o * jax.nn.silu(g)) @ w_out


def setup_inputs(seed: int = 0) -> dict:
    key = jax.random.key(seed)
    ks = jax.random.split(key, 32)
    f32 = jnp.float32
    n_pages = PAST_LEN // PAGE_SIZE
    n_used = DEC_BATCH * n_pages
    n_pool = (5 * n_used + 3) // 4

    def nrm(k, shape, s=1.0):
        return s * jax.random.normal(k, shape, f32)

    page_table = jax.random.permutation(ks[6], n_pool)[:n_used].reshape(DEC_BATCH, n_pages).astype(jnp.int32)
    return {
        "x_prompt": nrm(ks[0], (BATCH, SEQ, D_MODEL)),
        "x_sample": nrm(ks[1], (DEC_BATCH, DEC_SEQ, D_MODEL)),
        "state_pool": nrm(ks[2], (N_AB_LAYERS, DEC_BATCH, POOL_BUF, POOL_WIDTH)),
        "state_conv": nrm(ks[3], (N_AB_LAYERS, DEC_BATCH, CONV_BUF, CONV_WIDTH), 0.5),
        "cache_k": nrm(ks[4], (N_C_LAYERS, n_pool, PAGE_SIZE, C_HEADS, 2 * C_HEAD_DIM)),
        "cache_v": nrm(ks[5], (N_C_LAYERS, n_pool, PAGE_SIZE, C_HEADS, C_V_DIM)),
        "page_table": page_table,
        "norm_w_ab": 1.0 + nrm(ks[7], (N_AB_LAYERS, D_MODEL), 0.1),
        "w_in_ab": nrm(ks[8], (N_AB_LAYERS, D_MODEL, AB_IN), D_MODEL ** -0.5),
        "pool_w": nrm(ks[9], (N_AB_LAYERS, POOL_GROUPS, POOL_GROUP_DIM, POOL_GROUP_DIM), POOL_GROUP_DIM ** -0.5),
        "pool_scale": 1.0 + nrm(ks[10], (N_AB_LAYERS, POOL_WIDTH), 0.1),
        "conv_w": nrm(ks[11], (N_AB_LAYERS, CONV_K, CONV_WIDTH), CONV_K ** -0.5),
        "conv_b": nrm(ks[12], (N_AB_LAYERS, CONV_WIDTH), 0.01),
        "conv_ln_w": 1.0 + nrm(ks[13], (N_AB_LAYERS, CONV_WIDTH), 0.1),
        "conv_ln_b": nrm(ks[14], (N_AB_LAYERS, CONV_WIDTH), 0.01),
        "w_out_ab": nrm(ks[15], (N_AB_LAYERS, AB_MIX, D_MODEL), AB_MIX ** -0.5),
        "norm_w_c": 1.0 + nrm(ks[16], (N_C_LAYERS, D_MODEL), 0.1),
        "w_in_c": nrm(ks[17], (N_C_LAYERS, D_MODEL, C_IN), D_MODEL ** -0.5),
        "q_norm_w": 1.0 + nrm(ks[18], (N_C_LAYERS, C_HEAD_DIM), 0.1),
        "k_norm_w": 1.0 + nrm(ks[19], (N_C_LAYERS, C_HEAD_DIM), 0.1),
        "lambda_q1": nrm(ks[20], (N_C_LAYERS, C_HEAD_DIM), 0.1),
        "lambda_k1": nrm(ks[21], (N_C_LAYERS, C_HEAD_DIM), 0.1),
        "lambda_q2": nrm(ks[22], (N_C_LAYERS, C_HEAD_DIM), 0.1),
        "lambda_k2": nrm(ks[23], (N_C_LAYERS, C_HEAD_DIM), 0.1),
        "subln_w": 1.0 + nrm(ks[24], (N_C_LAYERS, C_V_DIM), 0.1),
        "w_out_c": nrm(ks[25], (N_C_LAYERS, C_WIDTH, D_MODEL), C_WIDTH ** -0.5),
    }


def reference(x_prompt, x_sample, state_pool, state_conv, cache_k, cache_v, page_table,
              norm_w_ab, w_in_ab, pool_w, pool_scale, conv_w, conv_b, conv_ln_w, conv_ln_b, w_out_ab,
              norm_w_c, w_in_c, q_norm_w, k_norm_w, lambda_q1, lambda_k1, lambda_q2, lambda_k2,
              subln_w, w_out_c):
    yp, ys = x_prompt, x_sample
    B, T = yp.shape[:2]
    Bd, Tn = ys.shape[:2]
    n_valid_sample = min(POOL_BUF, PAST_LEN)
    pool_p, conv_p, k_p, v_p = [], [], [], []
    pool_s, conv_s, k_s, v_s = [], [], [], []
    for l in range(DEPTH):
        j = l // 2
        if l % 2 == 0:
            params = (norm_w_ab[j], w_in_ab[j], pool_w[j], pool_scale[j], conv_w[j], conv_b[j],
                      conv_ln_w[j], conv_ln_b[j], w_out_ab[j])
            zero_pool = jnp.zeros((B, POOL_BUF, POOL_WIDTH), yp.dtype)
            zero_conv = jnp.zeros((B, CONV_BUF, CONV_WIDTH), yp.dtype)
            yp, sp, sc = pool_conv_layer(yp, zero_pool, zero_conv, 0, *params)
            ys, tp, tc = pool_conv_layer(ys, state_pool[j], state_conv[j], n_valid_sample, *params)
            pool_p.append(sp)
            conv_p.append(sc)
            pool_s.append(tp)
            conv_s.append(tc)
        else:
            lambda_init = 0.8 - 0.6 * math.exp(-0.3 * l)
            lam = diff_lambda(lambda_q1[j], lambda_k1[j], lambda_q2[j], lambda_k2[j], lambda_init)
            q, k, v, g = diff_qkv(yp, norm_w_c[j], w_in_c[j], q_norm_w[j], k_norm_w[j])
            o = prompt_diff_attention(q, k, v, lam)
            yp = diff_output(yp, o, g, subln_w[j], lambda_init, w_out_c[j])
            k_p.append(k.reshape(B, T, C_HEADS, 2 * C_HEAD_DIM))
            v_p.append(v)
            q, k, v, g = diff_qkv(ys, norm_w_c[j], w_in_c[j], q_norm_w[j], k_norm_w[j])
            o = sample_diff_attention(q, k, v, cache_k[j], cache_v[j], page_table, lam)
            ys = diff_output(ys, o, g, subln_w[j], lambda_init, w_out_c[j])
            k_s.append(k.reshape(Bd, Tn, C_HEADS, 2 * C_HEAD_DIM))
            v_s.append(v)
    return (yp, ys, jnp.stack(pool_p), jnp.stack(conv_p), jnp.stack(k_p), jnp.stack(v_p),
            jnp.stack(pool_s), jnp.stack(conv_s), jnp.stack(k_s), jnp.stack(v_s))
```

```python
import contextlib
import math
import os
import numpy as np
import concourse.bass as bass
import concourse.mybir as mybir
from concourse.bass_utils import run_bass_kernel_spmd

F32 = mybir.dt.float32
BF16 = mybir.dt.bfloat16
I32 = mybir.dt.int32
U8 = mybir.dt.uint8
ALU = mybir.AluOpType
AF = mybir.ActivationFunctionType
AX = mybir.AxisListType

NDSEM = 12

D = 1024
SEQ = 4096
NBATCH = 4
TT = 256
NT = SEQ // TT
NB = TT // 128
EPS = 1e-6
LAMBDA_INIT = 0.8 - 0.6 * math.exp(-0.3 * 1)
OWN = {0: [0, 15, 2, 13, 4, 11, 6, 9], 1: [1, 14, 3, 12, 5, 10, 7, 8]}
POOL_W = (2, 4, 8, 16)
NPOOL = int(os.environ.get("K_NPOOL", 2560))
NPAGE = 64
import os
DBG = {"nt1": int(os.environ.get("K_NT1", NT)), "nt2": int(os.environ.get("K_NT2", NT)), "attn": int(os.environ.get("K_ATTN", 1)), "st": int(os.environ.get("K_ST", 1)), "npg": int(os.environ.get("K_NPG", NPAGE))}


class Res:
    __slots__ = ("name", "w", "r", "excl")

    def __init__(self, name="", excl=False):
        self.name = name
        self.w = None
        self.r = []
        self.excl = excl


class Sched:
    ENGS = ("pe", "act", "dve", "pool", "sp")

    def __init__(self):
        self.ops = {e: [] for e in self.ENGS}
        self.ndma = {e: 0 for e in self.ENGS}
        self.allres = []
        self.pending = {e: set() for e in self.ENGS}

    def res(self, name="", excl=False):
        r = Res(name, excl)
        self.allres.append(r)
        return r

    def barrier(self):
        b = set()
        for r in self.allres:
            if r.w is not None:
                b.add(r.w)
            b.update(r.r)
        for e in self.ENGS:
            self.pending[e] |= b

    def _deps(self, eng, reads, writes, is_dma):
        deps = set()
        for r in reads:
            if r.w is not None:
                w = r.w
                if w[0] == "c" and w[1] == eng and eng == "pe" and not is_dma:
                    continue
                deps.add(w)
        for wr in writes:
            cands = list(wr.r)
            if wr.w is not None:
                cands.append(wr.w)
            for w in cands:
                if w[0] == "c" and w[1] == eng and not is_dma:
                    continue
                deps.add(w)
        if self.pending[eng]:
            for w in self.pending[eng]:
                if w[0] == "c" and w[1] == eng and not is_dma:
                    continue
                deps.add(w)
            self.pending[eng] = set()
        return deps

    def _commit(self, me, reads, writes):
        for r in reads:
            r.r.append(me)
        for w in writes:
            w.w = me
            w.r = []

    def op(self, eng, fn, reads=(), writes=()):
        writes = list(writes) + [r for r in reads if r.excl]
        reads = [r for r in reads if not r.excl]
        deps = self._deps(eng, reads, writes, False)
        idx = len(self.ops[eng])
        self.ops[eng].append({"fn": fn, "deps": deps, "dma": None})
        me = ("c", eng, idx)
        self._commit(me, reads, writes)
        return me

    def dma(self, eng, fn, reads=(), writes=()):
        deps = self._deps(eng, reads, writes, True)
        k = self.ndma[eng]
        self.ndma[eng] += 1
        if k >= NDSEM:
            deps.add(("d", eng, k - NDSEM))
        self.ops[eng].append({"fn": fn, "deps": deps, "dma": k})
        me = ("d", eng, k)
        self._commit(me, reads, writes)
        return me

    def emit(self, nc, final_waits=()):
        need = {e: set() for e in self.ENGS}
        for e in self.ENGS:
            for o in self.ops[e]:
                for d in o["deps"]:
                    if d[0] == "c":
                        need[d[1]].add(d[2])
        for d in final_waits:
            if d[0] == "c":
                need[d[1]].add(d[2])
        cnt = {}
        for e in self.ENGS:
            c = 0
            m = {}
            for i in sorted(need[e]):
                c += 1
                m[i] = c
            cnt[e] = m
        with contextlib.ExitStack() as st:
            csem = {e: st.enter_context(nc.semaphore("c_" + e)) for e in self.ENGS}
            dsem = {e: [st.enter_context(nc.semaphore("d_%s_%d" % (e, i))) for i in range(NDSEM)]
                    for e in self.ENGS if self.ndma[e] > 0}
            block = st.enter_context(nc.Block())

            def body(e, engine):
                waited = {}

                def do_wait(d):
                    if d[0] == "c":
                        key = ("c", d[1]); sem = csem[d[1]]; val = cnt[d[1]][d[2]]
                    else:
                        key = ("d", d[1], d[2] % NDSEM); sem = dsem[d[1]][d[2] % NDSEM]
                        val = 16 * (d[2] // NDSEM + 1)
                    if waited.get(key, 0) >= val:
                        return
                    waited[key] = val
                    engine.wait_ge(sem, val)

                for i, o in enumerate(self.ops[e]):
                    for d in sorted(o["deps"]):
                        do_wait(d)
                    ins = o["fn"](engine)
                    if o["dma"] is not None:
                        ins.then_inc(dsem[e][o["dma"] % NDSEM], 16)
                    elif i in cnt[e]:
                        ins.then_inc(csem[e], 1)
                if e == "sp":
                    for d in final_waits:
                        do_wait(d)

            block.tensor(lambda eng: body("pe", eng))
            block.scalar(lambda eng: body("act", eng))
            block.vector(lambda eng: body("dve", eng))
            block.gpsimd(lambda eng: body("pool", eng))
            block.sync(lambda eng: body("sp", eng))


class Arena:
    def __init__(self, nc, st, nbytes):
        self.t = st.enter_context(nc.sbuf_tensor("arena", [128, nbytes], U8))
        self.n = nbytes
        self.off = 0
        self.peak = 0

    def alloc(self, shape, dt):
        esz = 4 if dt in (F32, I32) else 2
        n = esz
        for s in shape:
            n *= s
        n = (n + 31) // 32 * 32
        assert self.off + n <= self.n, "arena overflow: %d + %d > %d" % (self.off, n, self.n)
        v = self.t[:, self.off:self.off + n].bitcast(dt)
        tot = 1
        for s in shape:
            tot *= s
        v = v[:, 0:tot]
        if len(shape) == 2:
            v = v.rearrange("p (a b) -> p a b", a=shape[0])
        elif len(shape) == 3:
            v = v.rearrange("p (a b c) -> p a b c", a=shape[0], b=shape[1])
        self.off += n
        self.peak = max(self.peak, self.off)
        return v

    def mark(self):
        return self.off

    def reset(self, m):
        self.off = m


def build_nc():
    nc = bass.Bass("TRN2", target_bir_lowering=False)
    S = Sched()
    dt_in = lambda name, shape, dt=F32: nc.dram_tensor(name, shape, dt, kind="ExternalInput").ap()
    dt_out = lambda name, shape, dt=F32: nc.dram_tensor(name, shape, dt, kind="ExternalOutput").ap()
    dt_int = lambda name, shape, dt: nc.dram_tensor(name, shape, dt, kind="Internal").ap()

    xp = dt_in("xp", [SEQ, D])
    own_idx_unused = None
    norm_w_ab = dt_in("norm_w_ab", [1, D])
    w_in_ab = dt_in("w_in_ab", [D, 5 * D])
    pool_w = dt_in("pool_w", [4, 256, 256])
    pvec_src = dt_in("pvec_src", [35, D])
    w_out_ab = dt_in("w_out_ab", [2 * D, D])
    norm_w_c = dt_in("norm_w_c", [1, D])
    w_in_c = dt_in("w_in_c", [D, 4 * D])
    qk_w = dt_in("qk_w", [1, 128])
    lam_v = dt_in("lam_v", [1, 256])
    subln_w = dt_in("subln_w", [1, 128])
    w_out_c = dt_in("w_out_c", [D, D])
    own_tiles = None

    xs = dt_in("xs", [16, D])
    spool = dt_in("spool", [4, 15, D])
    sconv = dt_in("sconv", [4, 30, D])
    pt = dt_in("pt", [1, 256], I32)
    smask = dt_in("smask", [16, 16])
    cache_k = dt_in("cache_k", [NPOOL * 128, D])
    cache_v = dt_in("cache_v", [NPOOL * 128, D])
    ys_out = dt_out("ys_out", [16, D])
    pool_s_out = dt_out("pool_s_out", [4, 15, D])
    conv_s_out = dt_out("conv_s_out", [4, 30, D])
    ks_out = dt_out("ks_out", [16, D])
    vs_out = dt_out("vs_out", [16, D])
    own_rows = dt_in("own_rows", [128, 16], I32)
    amask = dt_in("amask", [128, 4 * TT])
    y_own = dt_out("y_own", [SEQ // 2, D])
    k_all = dt_out("k_all", [SEQ, D])
    v_all = dt_out("v_all", [SEQ, D])
    pool_st = dt_out("pool_st", [15, D])
    conv_st = dt_out("conv_st", [30, D])

    if os.environ.get("K_DUMP"):
        d_h0T = dt_out("d_h0T", [128, 8 * TT], BF16)
        d_win = dt_out("d_win", [128, 1024], BF16)
        d_h0 = dt_out("d_h0", [128, NB * D], BF16)
        d_xae = dt_out("d_xae", [128, 2 * (15 + TT)], F32)
    x1_scr = dt_int("x1_scr", [SEQ, D], F32)
    kt_scr = dt_int("kt_scr", [8, 128, SEQ], BF16)
    v_scr = dt_int("v_scr", [8, 128, SEQ // 128, 129], BF16)

    return nc, S, locals()


def emit_program(nc, S, T):
    xp = T["xp"]
    outs = []
    st = contextlib.ExitStack()
    A = Arena(nc, st, 211968)
    psum_all = st.enter_context(nc.psum_tensor("ps_all", [128, 4096], F32))
    psum = [psum_all[:, i * 512:(i + 1) * 512] for i in range(8)]
    rps = [S.res("ps%d" % i, excl=True) for i in range(8)]

    def ps_bf(i):
        return psum[i][:, :].bitcast(BF16)

    ident_f = A.alloc([128], F32); r_identf = S.res()
    ident_b = A.alloc([128], BF16); r_identb = S.res()
    mask_b = A.alloc([128], BF16); r_mask = S.res()
    ones_b = A.alloc([128], BF16); r_ones = S.res()
    iot = A.alloc([128], F32); r_iot = S.res()
    pvec = A.alloc([8, 35], F32); r_pvec = S.res()
    nw = A.alloc([D], F32); r_nw = S.res()
    qkw = A.alloc([128], F32); r_qkw = S.res()
    sublnw = A.alloc([128], F32); r_subw = S.res()
    lamv = A.alloc([256], F32); r_lamv = S.res()
    lam = A.alloc([4], F32); r_lam = S.res()
    invc = A.alloc([4, 16], F32); r_invc = S.res()
    small = A.alloc([64], F32); r_small = S.res()
    junk = A.alloc([D], BF16); r_junk = S.res()

    S.op("pool", lambda e: e.iota(out=iot, pattern=[[1, 128]], base=0, channel_multiplier=-1,
                                  allow_small_or_imprecise_dtypes=True), [], [r_iot])
    S.op("dve", lambda e: e.tensor_single_scalar(out=ident_f, in_=iot, scalar=0.0, op=ALU.is_equal), [r_iot], [r_identf])
    S.op("dve", lambda e: e.tensor_copy(out=ident_b, in_=ident_f), [r_identf], [r_identb])
    S.op("dve", lambda e: e.tensor_single_scalar(out=mask_b, in_=iot, scalar=0.0, op=ALU.is_ge), [r_iot], [r_mask])
    S.op("dve", lambda e: e.memset(ones_b, 1.0), [], [r_ones])
    iot2 = A.alloc([16], F32); r_iot2 = S.res()
    S.op("pool", lambda e: e.iota(out=iot2, pattern=[[1, 16]], base=1, channel_multiplier=0,
                                  allow_small_or_imprecise_dtypes=True), [], [r_iot2])
    for g, w in enumerate(POOL_W):
        S.op("dve", lambda e, g=g, w=w: e.tensor_scalar(out=invc[:, g, :], in0=iot2, scalar1=float(w), scalar2=None, op0=ALU.min),
             [r_iot2], [r_invc])
    S.op("dve", lambda e: e.reciprocal(out=invc, in_=invc), [r_invc], [r_invc])

    m0 = A.mark()
    t35 = A.alloc([D], F32); r_t35 = S.res()
    S.dma("sp", lambda e: e.dma_start(out=t35[0:35, :], in_=T["pvec_src"][:, :]), [], [r_t35])
    for c in range(8):
        S.op("pe", lambda e, c=c: e.transpose(out=psum[0][:, c * 35:(c + 1) * 35], in_=t35[0:35, c * 128:(c + 1) * 128],
                                              identity=ident_f[0:35, 0:35]), [r_t35, r_identf], [rps[0]])
    S.op("dve", lambda e: e.tensor_copy(out=pvec, in_=psum[0][:, 0:280].rearrange("p (c k) -> p c k", c=8)), [rps[0]], [r_pvec])

    S.dma("sp", lambda e: e.dma_start(out=qkw, in_=T["qk_w"][0:1, :].partition_broadcast(128)), [], [r_qkw])
    S.dma("sp", lambda e: e.dma_start(out=sublnw, in_=T["subln_w"][0:1, :].partition_broadcast(128)), [], [r_subw])
    S.dma("sp", lambda e: e.dma_start(out=lamv, in_=T["lam_v"][0:1, :].partition_broadcast(128)), [], [r_lamv])
    S.op("dve", lambda e: e.tensor_scalar(out=sublnw, in0=sublnw, scalar1=float(1.0 - LAMBDA_INIT), scalar2=None, op0=ALU.mult),
         [r_subw], [r_subw])
    S.op("dve", lambda e: e.tensor_tensor(out=small[:, 0:64], in0=lamv[:, 0:64], in1=lamv[:, 64:128], op=ALU.mult), [r_lamv], [r_small])
    S.op("dve", lambda e: e.tensor_reduce(out=lam[:, 2:3], in_=small[:, 0:64], axis=AX.X, op=ALU.add), [r_small], [r_lam])
    S.op("dve", lambda e: e.tensor_tensor(out=small[:, 0:64], in0=lamv[:, 128:192], in1=lamv[:, 192:256], op=ALU.mult), [r_lamv, r_lam], [r_small])
    S.op("dve", lambda e: e.tensor_reduce(out=lam[:, 3:4], in_=small[:, 0:64], axis=AX.X, op=ALU.add), [r_small], [r_lam])
    S.op("act", lambda e: e.activation(out=lam[:, 2:4], in_=lam[:, 2:4], func=AF.Exp), [r_lam], [r_lam])
    S.op("dve", lambda e: e.scalar_tensor_tensor(out=lam[:, 0:1], in0=lam[:, 3:4], scalar=float(-LAMBDA_INIT), in1=lam[:, 2:3],
                                                 op0=ALU.add, op1=ALU.subtract), [r_lam], [r_lam])
    S.op("dve", lambda e: e.tensor_reduce(out=lam[:, 2:3], in_=qkw[:, 0:64], axis=AX.X, op=ALU.max, apply_absolute_value=True), [r_qkw, r_lam], [r_lam])
    S.op("dve", lambda e: e.tensor_reduce(out=lam[:, 3:4], in_=qkw[:, 64:128], axis=AX.X, op=ALU.max, apply_absolute_value=True), [r_qkw, r_lam], [r_lam])
    S.op("dve", lambda e: e.scalar_tensor_tensor(out=lam[:, 1:2], in0=lam[:, 2:3], scalar=-8.0, in1=lam[:, 3:4],
                                                 op0=ALU.mult, op1=ALU.mult), [r_lam], [r_lam])

    def rms_rstd(ss_ap, n, res_list):
        S.op("dve", lambda e: e.tensor_scalar(out=ss_ap, in0=ss_ap, scalar1=1.0 / n, scalar2=EPS, op0=ALU.mult, op1=ALU.add), res_list, res_list)
        S.op("act", lambda e: e.activation(out=ss_ap, in_=ss_ap, func=AF.Sqrt), res_list, res_list)
        S.op("dve", lambda e: e.reciprocal(out=ss_ap, in_=ss_ap), res_list, res_list)

    def rmsnorm_tile(xt, r_xt, h, r_h, hT, r_hT, nblk, ncols):
        for blk in range(nblk):
            ss = small[:, blk:blk + 1]
            S.op("act", lambda e, blk=blk, ss=ss: e.activation(out=junk, in_=xt[:, blk, :], func=AF.Square, accum_out=ss),
                 [r_xt], [r_junk, r_small])
            rms_rstd(ss, D, [r_small])
            S.op("dve", lambda e, blk=blk, ss=ss: e.scalar_tensor_tensor(out=h[:, blk, :], in0=xt[:, blk, :], scalar=ss, in1=nw,
                                                                         op0=ALU.mult, op1=ALU.mult), [r_xt, r_small, r_nw], [r_h])
            pb = ps_bf(7)
            for c in range(8):
                S.op("pe", lambda e, blk=blk, c=c, pb=pb: e.transpose(out=pb[:, c * 128:(c + 1) * 128], in_=h[:, blk, c * 128:(c + 1) * 128],
                                                                      identity=ident_b), [r_h, r_identb], [rps[7]])
            S.op("act", lambda e, blk=blk, pb=pb: e.copy(out=hT[:, :, blk * 128:(blk + 1) * 128],
                                                         in_=pb.rearrange("p (c t) -> p c t", c=8)), [rps[7]], [r_hT])

    S.dma("sp", lambda e: e.dma_start(out=nw, in_=T["norm_w_ab"][0:1, :].partition_broadcast(128)), [], [r_nw])
    x1s = A.alloc([D], F32); r_x1s = S.res()
    mP = A.mark()
    win = A.alloc([8, 5 * D], BF16); r_win = S.res()
    wout = A.alloc([16, D], BF16); r_wout = S.res()
    wpool = A.alloc([4, 2, 256], BF16); r_wpool = S.res()
    def cast_load(dst, r_dst, src, nchunk, ncols):
        for c in range(nchunk):
            for k in range(ncols // 1024):
                S.dma("pool", lambda e, c=c, k=k: e.dma_start(out=dst[:, c, k * 1024:(k + 1) * 1024],
                                                            in_=src[c * 128:(c + 1) * 128, k * 1024:(k + 1) * 1024]), [], [r_dst])

    cast_load(win, r_win, T["w_in_ab"], 8, 5 * D)
    cast_load(wout, r_wout, T["w_out_ab"], 16, D)
    for g in range(4):
        for cc in range(2):
            S.dma("pool", lambda e, g=g, cc=cc: e.dma_start(out=wpool[:, g, cc, :], in_=T["pool_w"][g, cc * 128:(cc + 1) * 128, :]), [], [r_wpool])

    def sample_l0():
        NS = 16
        xs_t = A.alloc([D], F32); r_xs = S.res()
        hs = A.alloc([D], BF16); r_hs = S.res()
        hTs = A.alloc([8, NS], BF16); r_hTs = S.res()
        stp = [A.alloc([D], F32) for _ in range(2)]; r_stp = [S.res() for _ in range(2)]
        xae = A.alloc([8, 4, 19], F32); r_xaes = S.res()
        sab = [A.alloc([8, 4, 19], F32) for _ in range(2)]; r_sab = [S.res() for _ in range(2)]
        glue = A.alloc([8, 4, 34], F32); r_glues = S.res()
        xac = A.alloc([8, NS], F32); r_xac = S.res()
        gluc = A.alloc([8, NS], F32); r_gluc = S.res()
        tok = A.alloc([D], F32); r_tok = S.res()
        tok2 = A.alloc([D], F32); r_tok2 = S.res()
        sgas = A.alloc([8, NS], BF16); r_sgas = S.res()
        sgbs = A.alloc([8, NS], BF16); r_sgbs = S.res()
        sigs = A.alloc([8, NS], F32); r_sigs = S.res()
        dTs = A.alloc([8, NS], BF16); r_dTs = S.res()
        cs = A.alloc([8, NS], F32); r_cs = S.res()
        cs16 = A.alloc([8, NS], BF16); r_cs16 = S.res()
        csq = A.alloc([8, NS], BF16); r_csq = S.res()
        lnt8 = A.alloc([8, NS], F32); r_lnt8 = S.res()
        mixs = A.alloc([16, NS], BF16); r_mixs = S.res()
        lnms = A.alloc([NS], F32); r_lnms = S.res()
        lnrs = A.alloc([NS], F32); r_lnrs = S.res()

        outs.append(S.dma("sp", lambda e: e.dma_start(out=T["pool_s_out"][:, 0:11, :], in_=T["spool"][:, 4:15, :]), [], []))
        outs.append(S.dma("sp", lambda e: e.dma_start(out=T["conv_s_out"][:, 0:26, :], in_=T["sconv"][:, 4:30, :]), [], []))
        S.dma("sp", lambda e: e.dma_start(out=xs_t[0:NS, :], in_=T["xs"][:, :]), [], [r_xs])
        ss = small[0:NS, 8:9]
        S.op("act", lambda e: e.activation(out=junk[0:NS, :], in_=xs_t[0:NS, :], func=AF.Square, accum_out=ss), [r_xs], [r_junk, r_small])
        rms_rstd(ss, D, [r_small])
        S.op("dve", lambda e: e.scalar_tensor_tensor(out=hs[0:NS, :], in0=xs_t[0:NS, :], scalar=ss, in1=nw[0:NS, :], op0=ALU.mult, op1=ALU.mult),
             [r_xs, r_small, r_nw], [r_hs])
        pb = ps_bf(7)
        for c in range(8):
            S.op("pe", lambda e, c=c: e.transpose(out=pb[:, c * NS:(c + 1) * NS], in_=hs[0:NS, c * 128:(c + 1) * 128], identity=ident_b[0:NS, 0:NS]),
                 [r_hs, r_identb], [rps[7]])
        S.op("act", lambda e: e.copy(out=hTs, in_=pb[:, 0:8 * NS].rearrange("p (c t) -> p c t", c=8)), [rps[7]], [r_hTs])
        for e_ in range(4):
            S.dma("sp", lambda e, e_=e_: e.dma_start(out=stp[0][0:15, :], in_=T["spool"][e_, :, :]), [], [r_stp[0]])
            for c in range(8):
                S.op("pe", lambda e, c=c: e.transpose(out=psum[5][:, c * 15:(c + 1) * 15], in_=stp[0][0:15, c * 128:(c + 1) * 128], identity=ident_f[0:15, 0:15]),
                     [r_stp[0], r_identf], [rps[5]])
            S.op("dve", lambda e, e_=e_: e.tensor_copy(out=xae[:, :, e_, 0:15], in_=psum[5][:, 0:120].rearrange("p (c k) -> p c k", c=8)), [rps[5]], [r_xaes])
            S.dma("sp", lambda e, e_=e_: e.dma_start(out=stp[1][0:30, :], in_=T["sconv"][e_, :, :]), [], [r_stp[1]])
            for c in range(8):
                S.op("pe", lambda e, c=c: e.transpose(out=psum[6][:, c * 30:(c + 1) * 30], in_=stp[1][0:30, c * 128:(c + 1) * 128], identity=ident_f[0:30, 0:30]),
                     [r_stp[1], r_identf], [rps[6]])
            S.op("act", lambda e, e_=e_: e.copy(out=glue[:, :, e_, 0:30], in_=psum[6][:, 0:240].rearrange("p (c k) -> p c k", c=8)), [rps[6]], [r_glues])
        for f_ in range(40):
            bank, col = f_ // 8, (f_ % 8) * NS
            for c in range(8):
                S.op("pe", lambda e, c=c, f_=f_, bank=bank, col=col: e.matmul(psum[bank][:, col:col + NS], lhsT=win[:, c, f_ * 128:(f_ + 1) * 128], rhs=hTs[:, c, :],
                                                                            start=(c == 0), stop=(c == 7)), [r_win, r_hTs], [rps[bank]])
        v816 = lambda ap: ap[:, 0:8 * NS].rearrange("p (c t) -> p c t", c=8)
        S.op("act", lambda e: e.copy(out=xae[:, :, :, 15:19], in_=psum[0][:, 0:8 * NS].rearrange("p (c a t) -> p c a t", c=8, a=4)), [rps[0]], [r_xaes])
        S.op("dve", lambda e: e.tensor_copy(out=xac, in_=v816(psum[0])), [rps[0]], [r_xac])
        S.op("act", lambda e: e.activation(out=sgas, in_=v816(psum[1]), func=AF.Silu), [rps[1]], [r_sgas])
        S.op("act", lambda e: e.activation(out=sigs, in_=v816(psum[3]), func=AF.Sigmoid), [rps[3]], [r_sigs])
        S.op("dve", lambda e: e.tensor_tensor(out=gluc, in0=v816(psum[2]), in1=sigs, op=ALU.mult), [rps[2], r_sigs], [r_gluc])
        S.op("pool", lambda e: e.tensor_copy(out=glue[:, :, :, 30:34], in_=gluc.rearrange("p c (a t) -> p c a t", a=4)), [r_gluc], [r_glues])
        S.op("act", lambda e: e.activation(out=sgbs, in_=v816(psum[4]), func=AF.Silu), [rps[4]], [r_sgbs])
        for (srcc, r_srcc, tk, r_tk, dst, row0) in ((xac, r_xac, tok, r_tok, "pool_s_out", 11), (gluc, r_gluc, tok2, r_tok2, "conv_s_out", 26)):
            for c in range(8):
                bk = 5 + c // 4
                S.op("pe", lambda e, c=c, bk=bk, srcc=srcc: e.transpose(out=psum[bk][0:NS, (c % 4) * 128:(c % 4 + 1) * 128], in_=srcc[:, c, :], identity=ident_f),
                     [r_srcc, r_identf], [rps[bk]])
            S.op("dve", lambda e, tk=tk: e.tensor_copy(out=tk[0:NS, 0:512], in_=psum[5][0:NS, :]), [rps[5]], [r_tk])
            S.op("dve", lambda e, tk=tk: e.tensor_copy(out=tk[0:NS, 512:1024], in_=psum[6][0:NS, :]), [rps[6]], [r_tk])
            for e_ in range(4):
                outs.append(S.dma("sp", lambda e, e_=e_, tk=tk, dst=dst, row0=row0: e.dma_start(out=T[dst][e_, row0:row0 + 4, :], in_=tk[e_ * 4:(e_ + 1) * 4, :]),
                                  [r_tk], []))
        for g in range(4):
            w = POOL_W[g]
            Xg = xae[:, 2 * g:2 * g + 2]
            src, rsrc = Xg, r_xaes
            for k in range(g + 1):
                sh = 1 << k
                dst_, rdst = sab[k % 2][:, 2 * g:2 * g + 2], r_sab[k % 2]
                S.op("dve", lambda e, src=src, dst_=dst_, sh=sh: e.tensor_tensor(out=dst_[:, :, :, sh:19], in0=src[:, :, :, sh:19], in1=src[:, :, :, 0:19 - sh], op=ALU.add),
                     [rsrc], [rdst])
                src, rsrc = dst_, rdst
            S.op("dve", lambda e, src=src, Xg=Xg, g=g, w=w: e.scalar_tensor_tensor(out=dTs[:, 2 * g:2 * g + 2, :].rearrange("p c (a t) -> p c a t", a=4),
                                                                                 in0=src[:, :, :, 15:19], scalar=1.0 / w, in1=Xg[:, :, :, 15:19],
                                                                                 op0=ALU.mult, op1=ALU.subtract), [rsrc, r_xaes], [r_dTs])
            for ec in range(2):
                for cc in range(2):
                    S.op("pe", lambda e, ec=ec, cc=cc, g=g: e.matmul(psum[7][:, (2 * g + ec) * NS:(2 * g + ec + 1) * NS], lhsT=wpool[:, g, cc, ec * 128:(ec + 1) * 128],
                                                                    rhs=dTs[:, 2 * g + cc, :], start=(cc == 0), stop=(cc == 1)), [r_wpool, r_dTs], [rps[7]])
        for c in range(8):
            S.op("dve", lambda e, c=c: e.scalar_tensor_tensor(out=mixs[:, c, :], in0=psum[7][:, c * NS:(c + 1) * NS], scalar=pvec[:, c, 34:35], in1=sgas[:, c, :],
                                                             op0=ALU.mult, op1=ALU.mult), [rps[7], r_pvec, r_sgas], [r_mixs])
        for j in range(8):
            csj = cs[:, j, :].rearrange("p (a t) -> p a t", a=4)
            S.op("dve", lambda e, j=j, csj=csj: e.tensor_scalar(out=csj, in0=glue[:, j, :, 0:4], scalar1=pvec[:, j, 0:1], scalar2=pvec[:, j, 31:32],
                                                                op0=ALU.mult, op1=ALU.add), [r_glues, r_pvec], [r_cs])
            for k in range(1, 31):
                S.op("dve", lambda e, j=j, k=k, csj=csj: e.scalar_tensor_tensor(out=csj, in0=glue[:, j, :, k:k + 4], scalar=pvec[:, j, k:k + 1], in1=csj,
                                                                              op0=ALU.mult, op1=ALU.add), [r_glues, r_pvec, r_cs], [r_cs])
        S.op("act", lambda e: e.activation(out=csq, in_=cs, func=AF.Square), [r_cs], [r_csq])
        S.op("pool", lambda e: e.tensor_copy(out=cs16, in_=cs), [r_cs], [r_cs16])
        for j in range(8):
            S.op("pe", lambda e, j=j: e.matmul(psum[5][:, 0:NS], lhsT=ones_b, rhs=cs16[:, j, :], start=(j == 0), stop=(j == 7)), [r_ones, r_cs16], [rps[5]])
        for j in range(8):
            S.op("pe", lambda e, j=j: e.matmul(psum[6][:, 0:NS], lhsT=ones_b, rhs=csq[:, j, :], start=(j == 0), stop=(j == 7)), [r_ones, r_csq], [rps[6]])
        S.op("dve", lambda e: e.tensor_scalar(out=lnms, in0=psum[5][:, 0:NS], scalar1=1.0 / D, scalar2=None, op0=ALU.mult), [rps[5]], [r_lnms])
        S.op("dve", lambda e: e.tensor_tensor(out=lnrs, in0=lnms, in1=lnms, op=ALU.mult), [r_lnms], [r_lnrs])
        S.op("dve", lambda e: e.scalar_tensor_tensor(out=lnrs, in0=psum[6][:, 0:NS], scalar=1.0 / D, in1=lnrs, op0=ALU.mult, op1=ALU.subtract), [rps[6], r_lnrs], [r_lnrs])
        S.op("dve", lambda e: e.tensor_scalar(out=lnrs, in0=lnrs, scalar1=EPS, scalar2=None, op0=ALU.add), [r_lnrs], [r_lnrs])
        S.op("act", lambda e: e.activation(out=lnrs, in_=lnrs, func=AF.Sqrt), [r_lnrs], [r_lnrs])
        S.op("dve", lambda e: e.reciprocal(out=lnrs, in_=lnrs), [r_lnrs], [r_lnrs])
        S.op("dve", lambda e: e.tensor_tensor(out=lnt8, in0=cs, in1=lnms.unsqueeze(1).broadcast_to([128, 8, NS]), op=ALU.subtract), [r_cs, r_lnms], [r_lnt8])
        S.op("dve", lambda e: e.tensor_tensor(out=lnt8, in0=lnt8, in1=lnrs.unsqueeze(1).broadcast_to([128, 8, NS]), op=ALU.mult), [r_lnt8, r_lnrs], [r_lnt8])
        for j in range(8):
            S.op("act", lambda e, j=j: e.activation(out=lnt8[:, j, :], in_=lnt8[:, j, :], func=AF.Silu, scale=pvec[:, j, 32:33], bias=pvec[:, j, 33:34]),
                 [r_lnt8, r_pvec], [r_lnt8])
        S.op("dve", lambda e: e.tensor_tensor(out=mixs[:, 8:16, :], in0=lnt8, in1=sgbs, op=ALU.mult), [r_lnt8, r_sgbs], [r_mixs])
        for half in range(2):
            bo = 5 + half
            for kc in range(16):
                S.op("pe", lambda e, half=half, kc=kc, bo=bo: e.matmul(psum[bo][0:NS, :], lhsT=mixs[:, kc, :], rhs=wout[:, kc, half * 512:(half + 1) * 512],
                                                                      start=(kc == 0), stop=(kc == 15)), [r_mixs, r_wout], [rps[bo]])
            S.op("dve", lambda e, half=half, bo=bo: e.tensor_tensor(out=x1s[0:NS, half * 512:(half + 1) * 512], in0=psum[bo][0:NS, :],
                                                                   in1=xs_t[0:NS, half * 512:(half + 1) * 512], op=ALU.add), [rps[bo], r_xs], [r_x1s])

    mS = A.mark()
    if DBG["st"]:
        sample_l0()
        print("sample-L0 arena bytes:", A.off)
        S.barrier()
    A.reset(mS)

    xt = A.alloc([NB, D], F32); r_xt = S.res()
    h0 = A.alloc([NB, D], BF16); r_h0 = S.res()
    h0T = A.alloc([8, TT], BF16); r_h0T = S.res()
    xa_h = A.alloc([8, 15], F32); r_xah = S.res()
    glu_h = A.alloc([8, 30], F32); r_gluh = S.res()
    xa_e = [A.alloc([2, 15 + TT], F32) for _ in range(2)]; r_xae = [S.res() for _ in range(2)]
    s_tmp = [A.alloc([2, 15 + TT], F32) for _ in range(2)]; r_stmp = [S.res() for _ in range(2)]
    sga = [A.alloc([2, TT], BF16) for _ in range(2)]; r_sga = [S.res() for _ in range(2)]
    dT = [A.alloc([2, TT], BF16) for _ in range(2)]; r_dT = [S.res() for _ in range(2)]
    glu_e = [A.alloc([30 + TT], F32) for _ in range(2)]; r_glue = [S.res() for _ in range(2)]
    sig = [A.alloc([TT], F32) for _ in range(2)]; r_sig = [S.res() for _ in range(2)]
    cacc2 = [A.alloc([TT], F32) for _ in range(2)]; r_cacc2 = [S.res() for _ in range(2)]
    ptmp = A.alloc([TT], F32); r_ptmp = S.res()
    cbuf = A.alloc([8, TT], F32); r_c = [S.res() for _ in range(8)]
    cb16 = A.alloc([8, TT], BF16); r_cb16 = S.res()
    csq16 = A.alloc([8, TT], BF16); r_csq16 = S.res()
    sgb = A.alloc([8, TT], BF16); r_sgb = [S.res() for _ in range(8)]
    mixT = A.alloc([16, TT], BF16); r_mix = [S.res() for _ in range(16)]
    lnm = A.alloc([TT], F32); r_lnm = S.res()
    lnr = A.alloc([TT], F32); r_lnr = S.res()
    lnt = [A.alloc([TT], F32) for _ in range(2)]; r_lnt = [S.res() for _ in range(2)]
    print("phase1 arena bytes:", A.off)

    S.op("dve", lambda e: e.memset(xa_h, 0.0), [], [r_xah])
    S.op("dve", lambda e: e.memset(glu_h, 0.0), [], [r_gluh])

    r_x1 = [S.res() for _ in range(NT)]
    pi = [0]

    def nxt_bank():
        b = pi[0] % 4
        pi[0] += 1
        return b

    for t in range(DBG["nt1"]):
        tok0 = t * TT
        S.dma("sp", lambda e, tok0=tok0: e.dma_start(out=xt, in_=xp[tok0:tok0 + TT, :].rearrange("(k p) d -> p k d", p=128)), [], [r_xt])
        rmsnorm_tile(xt, r_xt, h0, r_h0, h0T, r_h0T, NB, TT)

        def inproj(bank, col, half, fchunk):
            for c in range(8):
                S.op("pe", lambda e, c=c: e.matmul(psum[bank][:, half * TT:(half + 1) * TT], lhsT=win[:, c, fchunk * 128:(fchunk + 1) * 128],
                                                   rhs=h0T[:, c, :], start=(c == 0), stop=(c == 7)), [r_win, r_h0T], [rps[bank]])

        if t == 0 and os.environ.get("K_DUMP"):
            outs.append(S.dma("sp", lambda e: e.dma_start(out=T["d_h0T"][:, :], in_=h0T.rearrange("p c t -> p (c t)")), [r_h0T], []))
            outs.append(S.dma("sp", lambda e: e.dma_start(out=T["d_h0"][:, :], in_=h0.rearrange("p c t -> p (c t)")), [r_h0], []))
            outs.append(S.dma("sp", lambda e: e.dma_start(out=T["d_win"][:, :], in_=win[:, 0, 0:1024]), [r_win], []))
        for g in range(4):
            w = POOL_W[g]
            i2 = g % 2
            ba = nxt_bank(); bb = nxt_bank()
            inproj(ba, 0, 0, 2 * g); inproj(ba, 0, 1, 2 * g + 1)
            inproj(bb, 0, 0, 8 + 2 * g); inproj(bb, 0, 1, 8 + 2 * g + 1)
            xe = xa_e[i2]; rxe = r_xae[i2]
            S.op("act", lambda e, bb=bb, i2=i2: e.activation(out=sga[i2], in_=psum[bb][:, :].rearrange("p (a t) -> p a t", a=2), func=AF.Silu),
                 [rps[bb]], [r_sga[i2]])
            S.op("act", lambda e, ba=ba, xe=xe: e.copy(out=xe[:, :, 15:15 + TT], in_=psum[ba][:, :].rearrange("p (a t) -> p a t", a=2)),
                 [rps[ba]], [rxe])
            S.op("pool", lambda e, g=g, xe=xe: e.tensor_copy(out=xe[:, :, 0:15], in_=xa_h[:, 2 * g:2 * g + 2, :]), [r_xah], [rxe])
            if t == 0 and g == 0 and os.environ.get("K_DUMP"):
                outs.append(S.dma("sp", lambda e, xe=xe: e.dma_start(out=T["d_xae"][:, :], in_=xe.rearrange("p c t -> p (c t)")), [rxe], []))
            src, rsrc = xe, rxe
            L = 15 + TT
            for k in range(g + 1):
                sh = 1 << k
                dst, rdst = s_tmp[k % 2], r_stmp[k % 2]
                S.op("dve", lambda e, src=src, dst=dst, sh=sh: e.tensor_tensor(out=dst[:, :, sh:L], in0=src[:, :, sh:L], in1=src[:, :, 0:L - sh], op=ALU.add),
                     [rsrc], [rdst])
                src, rsrc = dst, rdst
            S.op("dve", lambda e, src=src, xe=xe, i2=i2, w=w: e.scalar_tensor_tensor(out=dT[i2], in0=src[:, :, 15:L], scalar=1.0 / w, in1=xe[:, :, 15:L],
                                                                                   op0=ALU.mult, op1=ALU.subtract), [rsrc, rxe], [r_dT[i2]])
            if t == 0:
                S.op("dve", lambda e, src=src, g=g: e.tensor_tensor(out=src[:, :, 15:30], in0=src[:, :, 15:30],
                                                                     in1=invc[:, g:g + 1, 0:15].broadcast_to([128, 2, 15]), op=ALU.mult), [rsrc, r_invc], [rsrc])
                S.op("dve", lambda e, src=src, xe=xe, i2=i2: e.tensor_tensor(out=dT[i2][:, :, 0:15], in0=src[:, :, 15:30], in1=xe[:, :, 15:30], op=ALU.subtract),
                     [rsrc, rxe], [r_dT[i2]])
            S.op("pool", lambda e, g=g, xe=xe: e.tensor_copy(out=xa_h[:, 2 * g:2 * g + 2, :], in_=xe[:, :, TT:TT + 15]), [rxe], [r_xah])
            bc = nxt_bank()
            for ec in range(2):
                for cc in range(2):
                    S.op("pe", lambda e, ec=ec, cc=cc, g=g, i2=i2, bc=bc: e.matmul(psum[bc][:, ec * TT:(ec + 1) * TT], lhsT=wpool[:, g, cc, ec * 128:(ec + 1) * 128],
                                                                                  rhs=dT[i2][:, cc, :], start=(cc == 0), stop=(cc == 1)),
                         [r_wpool, r_dT[i2]], [rps[bc]])
            for ec in range(2):
                S.op("dve", lambda e, ec=ec, g=g, i2=i2, bc=bc: e.scalar_tensor_tensor(out=mixT[:, 2 * g + ec, :], in0=psum[bc][:, ec * TT:(ec + 1) * TT],
                                                                                      scalar=pvec[:, 2 * g + ec, 34:35], in1=sga[i2][:, ec, :],
                                                                                      op0=ALU.mult, op1=ALU.mult),
                     [rps[bc], r_pvec, r_sga[i2]], [r_mix[2 * g + ec]])

        for j in range(8):
            i2 = j % 2
            bu = nxt_bank(); bg = nxt_bank()
            inproj(bu, 0, 0, 16 + j); inproj(bu, 0, 1, 24 + j)
            inproj(bg, 0, 0, 32 + j)
            ge = glu_e[i2]; rge = r_glue[i2]
            S.op("act", lambda e, bu=bu, i2=i2: e.activation(out=sig[i2], in_=psum[bu][:, TT:2 * TT], func=AF.Sigmoid), [rps[bu]], [r_sig[i2]])
            S.op("dve", lambda e, bu=bu, i2=i2, ge=ge: e.tensor_tensor(out=ge[:, 30:30 + TT], in0=psum[bu][:, 0:TT], in1=sig[i2], op=ALU.mult),
                 [rps[bu], r_sig[i2]], [rge])
            S.op("pool", lambda e, j=j, ge=ge: e.tensor_copy(out=ge[:, 0:30], in_=glu_h[:, j, :]), [r_gluh], [rge])
            S.op("act", lambda e, bg=bg, j=j: e.activation(out=sgb[:, j, :], in_=psum[bg][:, 0:TT], func=AF.Silu), [rps[bg]], [r_sgb[j]])
            S.op("dve", lambda e, j=j, ge=ge: e.tensor_scalar(out=cbuf[:, j, :], in0=ge[:, 0:TT], scalar1=pvec[:, j, 0:1], scalar2=pvec[:, j, 31:32],
                                                              op0=ALU.mult, op1=ALU.add), [rge, r_pvec], [r_c[j]])
            for k in range(1, 20):
                S.op("dve", lambda e, j=j, ge=ge, k=k: e.scalar_tensor_tensor(out=cbuf[:, j, :], in0=ge[:, k:k + TT], scalar=pvec[:, j, k:k + 1], in1=cbuf[:, j, :],
                                                                            op0=ALU.mult, op1=ALU.add), [rge, r_pvec, r_c[j]], [r_c[j]])
            S.op("pool", lambda e, j=j, ge=ge, i2=i2: e.tensor_scalar(out=cacc2[i2], in0=ge[:, 20:20 + TT], scalar1=pvec[:, j, 20:21], scalar2=None, op0=ALU.mult),
                 [rge, r_pvec], [r_cacc2[i2]])
            for k in range(21, 31):
                S.op("pool", lambda e, j=j, ge=ge, k=k: e.tensor_scalar(out=ptmp, in0=ge[:, k:k + TT], scalar1=pvec[:, j, k:k + 1], scalar2=None, op0=ALU.mult),
                     [rge, r_pvec], [r_ptmp])
                S.op("pool", lambda e, i2=i2: e.tensor_tensor(out=cacc2[i2], in0=cacc2[i2], in1=ptmp, op=ALU.add), [r_cacc2[i2], r_ptmp], [r_cacc2[i2]])
            S.op("dve", lambda e, j=j, i2=i2: e.tensor_tensor(out=cbuf[:, j, :], in0=cbuf[:, j, :], in1=cacc2[i2], op=ALU.add), [r_c[j], r_cacc2[i2]], [r_c[j]])
            S.op("pool", lambda e, j=j, ge=ge: e.tensor_copy(out=glu_h[:, j, :], in_=ge[:, TT:TT + 30]), [rge], [r_gluh])
            S.op("act", lambda e, j=j: e.activation(out=csq16[:, j, :], in_=cbuf[:, j, :], func=AF.Square), [r_c[j]], [r_csq16])
            S.op("pool", lambda e, j=j: e.tensor_copy(out=cb16[:, j, :], in_=cbuf[:, j, :]), [r_c[j]], [r_cb16])
        bs = 4
        for j in range(8):
            S.op("pe", lambda e, j=j: e.matmul(psum[bs][:, 0:TT], lhsT=ones_b, rhs=cb16[:, j, :], start=(j == 0), stop=(j == 7)), [r_ones, r_cb16], [rps[bs]])
        for j in range(8):
            S.op("pe", lambda e, j=j: e.matmul(psum[5][:, 0:TT], lhsT=ones_b, rhs=csq16[:, j, :], start=(j == 0), stop=(j == 7)), [r_ones, r_csq16], [rps[5]])
        S.op("dve", lambda e: e.tensor_scalar(out=lnm, in0=psum[bs][:, 0:TT], scalar1=1.0 / D, scalar2=None, op0=ALU.mult), [rps[bs]], [r_lnm])
        S.op("dve", lambda e: e.tensor_tensor(out=lnr, in0=lnm, in1=lnm, op=ALU.mult), [r_lnm], [r_lnr])
        S.op("dve", lambda e: e.scalar_tensor_tensor(out=lnr, in0=psum[5][:, 0:TT], scalar=1.0 / D, in1=lnr, op0=ALU.mult, op1=ALU.subtract),
             [rps[5], r_lnr], [r_lnr])
        S.op("dve", lambda e: e.tensor_scalar(out=lnr, in0=lnr, scalar1=EPS, scalar2=None, op0=ALU.add), [r_lnr], [r_lnr])
        S.op("act", lambda e: e.activation(out=lnr, in_=lnr, func=AF.Sqrt), [r_lnr], [r_lnr])
        S.op("dve", lambda e: e.reciprocal(out=lnr, in_=lnr), [r_lnr], [r_lnr])
        for j in range(8):
            i2 = j % 2
            S.op("dve", lambda e, j=j, i2=i2: e.tensor_tensor(out=lnt[i2], in0=cbuf[:, j, :], in1=lnm, op=ALU.subtract), [r_c[j], r_lnm], [r_lnt[i2]])
            S.op("pool", lambda e, i2=i2: e.tensor_tensor(out=lnt[i2], in0=lnt[i2], in1=lnr, op=ALU.mult), [r_lnt[i2], r_lnr], [r_lnt[i2]])
            S.op("act", lambda e, j=j, i2=i2: e.activation(out=lnt[i2], in_=lnt[i2], func=AF.Silu, scale=pvec[:, j, 32:33], bias=pvec[:, j, 33:34]),
                 [r_lnt[i2], r_pvec], [r_lnt[i2]])
            S.op("dve", lambda e, j=j, i2=i2: e.tensor_tensor(out=mixT[:, 8 + j, :], in0=lnt[i2], in1=sgb[:, j, :], op=ALU.mult),
                 [r_lnt[i2], r_sgb[j]], [r_mix[8 + j]])
        for blk in range(NB):
            for half in range(2):
                bo = 5 + (blk * 2 + half) % 2
                for kc in range(16):
                    S.op("pe", lambda e, blk=blk, half=half, kc=kc, bo=bo: e.matmul(psum[bo][:, :], lhsT=mixT[:, kc, blk * 128:(blk + 1) * 128],
                                                                                  rhs=wout[:, kc, half * 512:(half + 1) * 512], start=(kc == 0), stop=(kc == 15)),
                         [r_mix[kc], r_wout], [rps[bo]])
                S.op("dve", lambda e, blk=blk, half=half, bo=bo: e.tensor_tensor(out=xt[:, blk, half * 512:(half + 1) * 512], in0=psum[bo][:, :],
                                                                               in1=xt[:, blk, half * 512:(half + 1) * 512], op=ALU.add), [rps[bo], r_xt], [r_xt])
        S.dma("sp", lambda e, tok0=tok0: e.dma_start(out=T["x1_scr"][tok0:tok0 + TT, :].rearrange("(k p) d -> p k d", p=128), in_=xt), [r_xt], [r_x1[t]])

    stt = cbuf.rearrange("p c t -> p (c t)")[:, 0:D]
    for c in range(4):
        S.op("pe", lambda e, c=c: e.transpose(out=psum[0][0:15, c * 128:(c + 1) * 128], in_=xa_h[:, c, :], identity=ident_f), [r_xah, r_identf], [rps[0]])
    for c in range(4):
        S.op("pe", lambda e, c=c: e.transpose(out=psum[1][0:30, c * 128:(c + 1) * 128], in_=glu_h[:, c, :], identity=ident_f), [r_gluh, r_identf], [rps[1]])
        S.op("pe", lambda e, c=c: e.transpose(out=psum[2][0:30, c * 128:(c + 1) * 128], in_=glu_h[:, 4 + c, :], identity=ident_f), [r_gluh, r_identf], [rps[2]])
    S.op("dve", lambda e: e.tensor_copy(out=stt[0:15, 0:512], in_=psum[0][0:15, 0:512]), [rps[0]], r_c[0:4])
    outs.append(S.dma("sp", lambda e: e.dma_start(out=T["pool_st"][:, 0:512], in_=stt[0:15, 0:512]), r_c[0:4], []))
    for c in range(4):
        S.op("pe", lambda e, c=c: e.transpose(out=psum[3][0:15, c * 128:(c + 1) * 128], in_=xa_h[:, 4 + c, :], identity=ident_f), [r_xah, r_identf], [rps[3]])
    S.op("dve", lambda e: e.tensor_copy(out=stt[0:15, 512:1024], in_=psum[3][0:15, 0:512]), [rps[3]], r_c[0:4])
    outs.append(S.dma("sp", lambda e: e.dma_start(out=T["pool_st"][:, 512:1024], in_=stt[0:15, 512:1024]), r_c[0:4], []))
    stc = stt
    S.op("dve", lambda e: e.tensor_copy(out=stc[0:30, 0:512], in_=psum[1][0:30, 0:512]), [rps[1]], r_c[0:4])
    S.op("dve", lambda e: e.tensor_copy(out=stc[0:30, 512:1024], in_=psum[2][0:30, 0:512]), [rps[2]], r_c[0:4])
    outs.append(S.dma("sp", lambda e: e.dma_start(out=T["conv_st"][:, :], in_=stc[0:30, :]), r_c[0:4], []))

    S.barrier()
    A.reset(mP)
    S.dma("sp", lambda e: e.dma_start(out=nw, in_=T["norm_w_c"][0:1, :].partition_broadcast(128)), [], [r_nw])
    wc = A.alloc([8, 4 * D], BF16); r_wc = S.res()
    woc = A.alloc([8, D], BF16); r_woc = S.res()
    cast_load(wc, r_wc, T["w_in_c"], 8, 4 * D)
    cast_load(woc, r_woc, T["w_out_c"], 8, D)
    xt2 = A.alloc([NB, D], F32); r_xt2 = S.res()
    h1 = A.alloc([NB, D], BF16); r_h1 = S.res()
    h1T = A.alloc([8, TT], BF16); r_h1T = S.res()
    stg = [A.alloc([D], F32) for _ in range(2)]; r_stg = [S.res() for _ in range(2)]
    sq = A.alloc([D], F32); r_sq = S.res()
    nb16 = A.alloc([D], BF16); r_nb16 = S.res()
    ktile = A.alloc([8, TT], BF16); r_ktile = S.res()
    vtile = A.alloc([8, NB, 129], BF16); r_vtile = S.res()
    QT = A.alloc([8, TT], BF16); r_QT = S.res()
    sg = A.alloc([NB, D], BF16); r_sg = S.res()
    KCH = 4 * TT
    kth = [A.alloc([KCH], BF16) for _ in range(3)]; r_kth = [S.res() for _ in range(3)]
    vh = [A.alloc([KCH // 128, 129], BF16) for _ in range(3)]; r_vh = [S.res() for _ in range(3)]
    PT = [A.alloc([2, TT], BF16) for _ in range(3)]; r_PT = [S.res() for _ in range(3)]
    og = A.alloc([NB, D], BF16); r_og = S.res()
    ogT = A.alloc([8, TT], BF16); r_ogT = S.res()
    o1 = A.alloc([128], F32); r_o1 = S.res()
    o2 = A.alloc([128], F32); r_o2 = S.res()
    sm2 = A.alloc([16], F32); r_sm2 = S.res()
    rs16 = A.alloc([16], F32); r_rs16 = S.res()
    print("phase2 arena bytes:", A.off)
    S.op("dve", lambda e: e.memset(vtile, 1.0), [], [r_vtile])

    r_kscr = [S.res() for _ in range(NT)]
    r_vscr = [S.res() for _ in range(NT)]
    nload = [0]
    npt = [0]
    pj = [0]
    ownr = A.alloc([16], I32); r_ownr = S.res()
    amask = A.alloc([2, 2, TT], BF16); r_amask = S.res()
    S.dma("sp", lambda e: e.dma_start(out=ownr, in_=T["own_rows"][:, :]), [], [r_ownr])
    S.dma("pool", lambda e: e.dma_start(out=amask, in_=T["amask"][:, :].rearrange("p (a b q) -> p a b q", a=2, b=2)), [], [r_amask])
    xq = A.alloc([NB, D], F32); r_xq = S.res()
    print("phase2 arena bytes (final):", A.off)

    def proj_bank():
        b = 6 + pj[0] % 2
        pj[0] += 1
        return b

    def head_norm(src_stage, r_src, wcol0, np_=128):
        src_ = src_stage[0:np_]
        S.op("pool", lambda e: e.tensor_tensor(out=sq[0:np_], in0=src_, in1=src_, op=ALU.mult), [r_src], [r_sq])
        S.op("dve", lambda e: e.tensor_reduce(out=rs16[0:np_], in_=sq[0:np_].rearrange("p (g d) -> p g d", d=64), axis=AX.X, op=ALU.add), [r_sq], [r_rs16])
        rms_rstd(rs16[0:np_], 64, [r_rs16])
        v3 = src_.rearrange("p (g d) -> p g d", d=64)
        S.op("dve", lambda e: e.tensor_tensor(out=v3, in0=v3, in1=rs16[0:np_].unsqueeze(2).broadcast_to([np_, 16, 64]), op=ALU.mult), [r_src, r_rs16], [r_src])
        S.op("pool", lambda e: e.tensor_tensor(out=v3, in0=v3, in1=qkw[0:np_, wcol0:wcol0 + 64].unsqueeze(1).broadcast_to([np_, 16, 64]), op=ALU.mult),
             [r_src, r_qkw], [r_src])

    def proj(hT, r_hT, blk, col0, half):
        b = proj_bank()
        for c in range(8):
            S.op("pe", lambda e, c=c, b=b: e.matmul(psum[b][:, :], lhsT=hT[:, c, blk * 128:(blk + 1) * 128],
                                                    rhs=wc[:, c, col0 + half * 512: col0 + (half + 1) * 512], start=(c == 0), stop=(c == 7)),
                 [r_hT, r_wc], [rps[b]])
        return b

    def to_featmajor(dst, r_dst, blk):
        pb = ps_bf(7)
        for h in range(8):
            S.op("pe", lambda e, h=h, pb=pb: e.transpose(out=pb[:, h * 128:(h + 1) * 128], in_=nb16[:, h * 128:(h + 1) * 128], identity=ident_b),
                 [r_nb16, r_identb], [rps[7]])
        S.op("dve", lambda e, pb=pb, blk=blk: e.tensor_copy(out=dst[:, :, blk * 128:(blk + 1) * 128], in_=pb.rearrange("p (h t) -> p h t", h=8)),
             [rps[7]], [r_dst])

    for t in range(DBG["nt2"]):
        tok0 = t * TT
        S.dma("sp", lambda e, tok0=tok0: e.dma_start(out=xt2, in_=T["x1_scr"][tok0:tok0 + TT, :].rearrange("(k p) d -> p k d", p=128)), [r_x1[t]], [r_xt2])
        rmsnorm_tile(xt2, r_xt2, h1, r_h1, h1T, r_h1T, NB, TT)
        for blk in range(NB):
            row = tok0 + blk * 128
            ks, rks = stg[0], r_stg[0]
            for half in range(2):
                b = proj(h1T, r_h1T, blk, D, half)
                S.op("act", lambda e, b=b, half=half: e.copy(out=ks[:, half * 512:(half + 1) * 512], in_=psum[b][:, :]), [rps[b]], [rks])
            head_norm(ks, rks, 64)
            outs.append(S.dma("sp", lambda e, row=row: e.dma_start(out=T["k_all"][row:row + 128, :], in_=ks), [rks], []))
            S.op("act", lambda e: e.copy(out=nb16, in_=ks), [rks], [r_nb16])
            to_featmajor(ktile, r_ktile, blk)
            vs, rvs = stg[1], r_stg[1]
            for half in range(2):
                b = proj(h1T, r_h1T, blk, 2 * D, half)
                S.op("act", lambda e, b=b, half=half: e.copy(out=vs[:, half * 512:(half + 1) * 512], in_=psum[b][:, :]), [rps[b]], [rvs])
                S.op("dve", lambda e, b=b, half=half, blk=blk: e.tensor_copy(out=vtile[:, half * 4:(half + 1) * 4, blk, 0:128],
                                                                          in_=psum[b][:, :].rearrange("p (h d) -> p h d", h=4)), [rps[b]], [r_vtile])
            outs.append(S.dma("sp", lambda e, row=row: e.dma_start(out=T["v_all"][row:row + 128, :], in_=vs), [rvs], []))
        S.dma("sp", lambda e, tok0=tok0: e.dma_start(out=T["kt_scr"][:, :, tok0:tok0 + TT].rearrange("h p t -> p h t"), in_=ktile), [r_ktile], [r_kscr[t]])
        S.dma("sp", lambda e, t=t: e.dma_start(out=T["v_scr"][:, :, t * NB:(t + 1) * NB, :].rearrange("h p k d -> p h (k d)"), in_=vtile.rearrange("p h k d -> p h (k d)")), [r_vtile], [r_vscr[t]])
        if t % 2 == 0 or not DBG["attn"]:
            continue
        s_ = t // 2
        for blk in range(NB):
            S.dma("pool", lambda e, blk=blk, s_=s_: e.indirect_dma_start(out=xq[:, blk, :], out_offset=None, in_=T["x1_scr"][:, :],
                                                                     in_offset=bass.IndirectOffsetOnAxis(ap=ownr[:, s_ * 2 + blk:s_ * 2 + blk + 1], axis=0)),
                  [r_ownr, r_x1[t - 1], r_x1[t]], [r_xq])
        rmsnorm_tile(xq, r_xq, h1, r_h1, h1T, r_h1T, NB, TT)
        for blk in range(NB):
            qs, rqs = stg[0], r_stg[0]
            for half in range(2):
                b = proj(h1T, r_h1T, blk, 0, half)
                S.op("act", lambda e, b=b, half=half: e.copy(out=qs[:, half * 512:(half + 1) * 512], in_=psum[b][:, :]), [rps[b]], [rqs])
            head_norm(qs, rqs, 0)
            S.op("act", lambda e: e.copy(out=nb16, in_=qs), [rqs], [r_nb16])
            to_featmajor(QT, r_QT, blk)
            for half in range(2):
                b = proj(h1T, r_h1T, blk, 3 * D, half)
                S.op("act", lambda e, b=b, half=half, blk=blk: e.activation(out=sg[:, blk, half * 512:(half + 1) * 512], in_=psum[b][:, :], func=AF.Silu),
                     [rps[b]], [r_sg])
        nkb = (t + 1) * NB
        for h in range(8):
            ob = [0, 1]
            for qb in range(NB):
                S.op("dve", lambda e, b=ob[qb]: e.memset(psum[b][:, 0:258], 0.0), [], [rps[ob[qb]]])
            for ch0 in range(0, nkb, KCH // 128):
                nb_ch = min(KCH // 128, nkb - ch0)
                li = nload[0] % 3
                nload[0] += 1
                tiles_in = sorted(set((ch0 + i) // NB for i in range(nb_ch)))
                S.dma("sp", lambda e, h=h, li=li, ch0=ch0, nb_ch=nb_ch: e.dma_start(out=kth[li][:, 0:nb_ch * 128],
                                                                                   in_=T["kt_scr"][h, :, ch0 * 128:(ch0 + nb_ch) * 128]),
                      [r_kscr[x] for x in tiles_in], [r_kth[li]])
                S.dma("sp", lambda e, h=h, li=li, ch0=ch0, nb_ch=nb_ch: e.dma_start(out=vh[li][:, 0:nb_ch, :],
                                                                                   in_=T["v_scr"][h, :, ch0:ch0 + nb_ch, :]),
                      [r_vscr[x] for x in tiles_in], [r_vh[li]])
                for i in range(nb_ch):
                    kb = ch0 + i
                    jd = kb - (nkb - 4)
                    sb_ = 2 + 2 * (kb % 2)
                    pi_ = npt[0] % 3
                    npt[0] += 1
                    for c in range(2):
                        S.op("pe", lambda e, c=c, li=li, i=i, h=h, sb_=sb_: e.matmul(
                            psum[sb_ + c][:, 0:TT], lhsT=kth[li][c * 64:(c + 1) * 64, i * 128:(i + 1) * 128],
                            rhs=QT[c * 64:(c + 1) * 64, h, :], start=True, stop=True), [r_kth[li], r_QT], [rps[sb_ + c]])
                    pt = PT[pi_]; rpt = r_PT[pi_]
                    S.op("act", lambda e, pt=pt, sb_=sb_: e.activation(out=pt, in_=psum_all[:, sb_ * 512:(sb_ + 2) * 512].rearrange("p (c x) -> p c x", c=2)[:, :, 0:TT],
                                                                     func=AF.Exp, scale=0.125, bias=lam[:, 1:2]), [rps[sb_], rps[sb_ + 1], r_lam], [rpt])
                    if jd >= 0:
                        S.op("dve", lambda e, pt=pt, jd=jd: e.tensor_tensor(out=pt, in0=pt, in1=amask[:, jd // 2, jd % 2, :].unsqueeze(1).broadcast_to([128, 2, TT]),
                                                                           op=ALU.mult), [rpt, r_amask], [rpt])
                    for qb in range(NB):
                        for c in range(2):
                            S.op("pe", lambda e, c=c, qb=qb, pt=pt, li=li, i=i, b=ob[qb]: e.matmul(
                                psum[b][:, c * 129:(c + 1) * 129], lhsT=pt[:, c, qb * 128:(qb + 1) * 128], rhs=vh[li][:, i, :],
                                start=False, stop=False, skip_group_check=True), [rpt, r_vh[li]], [rps[b]])
            for qb in range(NB):
                b = ob[qb]
                O = psum[b][:, 0:258].rearrange("p (c d) -> p c d", c=2)
                S.op("dve", lambda e, O=O: e.reciprocal(out=sm2[:, 0:2], in_=O[:, :, 128]), [rps[b]], [r_sm2])
                S.op("dve", lambda e: e.tensor_tensor(out=sm2[:, 1:2], in0=sm2[:, 1:2], in1=lam[:, 0:1], op=ALU.mult), [r_sm2, r_lam], [r_sm2])
                S.op("act", lambda e, O=O: e.activation(out=o1, in_=O[:, 0, 0:128], func=AF.Copy, scale=sm2[:, 0:1]), [rps[b], r_sm2], [r_o1])
                S.op("dve", lambda e, O=O: e.scalar_tensor_tensor(out=o2, in0=O[:, 1, 0:128], scalar=sm2[:, 1:2], in1=o1, op0=ALU.mult, op1=ALU.add),
                     [rps[b], r_sm2, r_o1], [r_o2])
                S.op("act", lambda e: e.activation(out=junk[:, 0:128], in_=o2, func=AF.Square, accum_out=sm2[:, 2:3]), [r_o2], [r_junk, r_sm2])
                rms_rstd(sm2[:, 2:3], 128, [r_sm2])
                S.op("dve", lambda e: e.scalar_tensor_tensor(out=o2, in0=o2, scalar=sm2[:, 2:3], in1=sublnw, op0=ALU.mult, op1=ALU.mult),
                     [r_o2, r_sm2, r_subw], [r_o2])
                S.op("dve", lambda e, qb=qb, h=h: e.tensor_tensor(out=og[:, qb, h * 128:(h + 1) * 128], in0=o2, in1=sg[:, qb, h * 128:(h + 1) * 128], op=ALU.mult),
                     [r_o2, r_sg], [r_og])
        for blk in range(NB):
            pb = ps_bf(7)
            for h in range(8):
                S.op("pe", lambda e, h=h, pb=pb, blk=blk: e.transpose(out=pb[:, h * 128:(h + 1) * 128], in_=og[:, blk, h * 128:(h + 1) * 128], identity=ident_b),
                     [r_og, r_identb], [rps[7]])
            S.op("act", lambda e, pb=pb, blk=blk: e.copy(out=ogT[:, :, blk * 128:(blk + 1) * 128], in_=pb.rearrange("p (h t) -> p h t", h=8)),
                 [rps[7]], [r_ogT])
        for blk in range(NB):
            for half in range(2):
                b = proj_bank()
                for kc in range(8):
                    S.op("pe", lambda e, blk=blk, half=half, kc=kc, b=b: e.matmul(psum[b][:, :], lhsT=ogT[:, kc, blk * 128:(blk + 1) * 128],
                                                                                rhs=woc[:, kc, half * 512:(half + 1) * 512], start=(kc == 0), stop=(kc == 7)),
                         [r_ogT, r_woc], [rps[b]])
                S.op("dve", lambda e, blk=blk, half=half, b=b: e.tensor_tensor(out=xq[:, blk, half * 512:(half + 1) * 512], in0=psum[b][:, :],
                                                                             in1=xq[:, blk, half * 512:(half + 1) * 512], op=ALU.add), [rps[b], r_xq], [r_xq])
        outs.append(S.dma("sp", lambda e, s_=s_: e.dma_start(out=T["y_own"][s_ * TT:(s_ + 1) * TT, :].rearrange("(k p) d -> p k d", p=128), in_=xq), [r_xq], []))

    def sample_l1():
        NS = 16
        ktn = A.alloc([8, NS], BF16); r_ktn = S.res()
        QTs = A.alloc([8, NS], BF16); r_QTs = S.res()
        vnew = A.alloc([8, 129], BF16); r_vnew = S.res()
        kpg = [A.alloc([D], BF16) for _ in range(2)]; r_kpg = [S.res() for _ in range(2)]
        vpg = [A.alloc([8, 129], BF16) for _ in range(2)]; r_vpg = [S.res() for _ in range(2)]
        ktp = [A.alloc([8, 128], BF16) for _ in range(2)]; r_ktp = [S.res() for _ in range(2)]
        vfl = [A.alloc([D], BF16) for _ in range(2)]; r_vfl = [S.res() for _ in range(2)]
        pte = [[A.alloc([8, 2, NS], BF16) for _ in range(2)] for _ in range(4)]
        r_pte = [[S.res() for _ in range(2)] for _ in range(4)]
        ptn = A.alloc([8, 2, NS], BF16); r_ptn = S.res()
        pti = A.alloc([256], I32); r_pti = S.res()
        ptf = A.alloc([256], F32); r_ptf = S.res()
        idx = A.alloc([256], I32); r_idx = S.res()
        smf = A.alloc([NS], F32); r_smf = S.res()
        smb = A.alloc([NS], BF16); r_smb = S.res()
        rr = A.alloc([16], F32); r_rr = S.res()
        print("sample-L1 arena bytes:", A.off)
        O1 = xq.rearrange("p a d -> p (a d)")[:, 0:1032]; r_O1 = r_xq
        O2 = xt2.rearrange("p a d -> p (a d)")[:, 0:1032]; r_O2 = r_xt2

        S.op("pool", lambda e: e.memset(vnew, 1.0), [], [r_vnew])
        for i in range(2):
            S.op("pool", lambda e, i=i: e.memset(vpg[i], 1.0), [], [r_vpg[i]])
        for e_ in range(4):
            for i in range(2):
                S.op("pool", lambda e, e_=e_, i=i: e.memset(pte[e_][i], 0.0), [], [r_pte[e_][i]])
        S.dma("sp", lambda e: e.dma_start(out=pti, in_=T["pt"][0:1, :].partition_broadcast(128)), [], [r_pti])
        S.op("dve", lambda e: e.tensor_copy(out=ptf, in_=pti), [r_pti], [r_ptf])
        S.op("dve", lambda e: e.scalar_tensor_tensor(out=ptf, in0=ptf, scalar=128.0, in1=iot[:, 0:1].broadcast_to([128, 256]), op0=ALU.mult, op1=ALU.subtract),
             [r_ptf, r_iot], [r_ptf])
        S.op("dve", lambda e: e.tensor_copy(out=idx, in_=ptf), [r_ptf], [r_idx])
        S.dma("sp", lambda e: e.dma_start(out=smf[0:NS, :], in_=T["smask"][:, :]), [], [r_smf])
        S.op("dve", lambda e: e.tensor_copy(out=smb[0:NS, :], in_=smf[0:NS, :]), [r_smf], [r_smb])

        ss = small[0:NS, 9:10]
        S.op("act", lambda e: e.activation(out=junk[0:NS, :], in_=x1s[0:NS, :], func=AF.Square, accum_out=ss), [r_x1s], [r_junk, r_small])
        rms_rstd(ss, D, [r_small])
        S.op("dve", lambda e: e.scalar_tensor_tensor(out=h1[0:NS, 0, :], in0=x1s[0:NS, :], scalar=ss, in1=nw[0:NS, :], op0=ALU.mult, op1=ALU.mult),
             [r_x1s, r_small, r_nw], [r_h1])
        pb = ps_bf(7)
        for c in range(8):
            S.op("pe", lambda e, c=c: e.transpose(out=pb[:, c * NS:(c + 1) * NS], in_=h1[0:NS, 0, c * 128:(c + 1) * 128], identity=ident_b[0:NS, 0:NS]),
                 [r_h1, r_identb], [rps[7]])
        S.op("act", lambda e: e.copy(out=h1T[:, :, 0:NS], in_=pb[:, 0:8 * NS].rearrange("p (c t) -> p c t", c=8)), [rps[7]], [r_h1T])

        def sproj(col0, half):
            b = proj_bank()
            for c in range(8):
                S.op("pe", lambda e, c=c, b=b: e.matmul(psum[b][0:NS, :], lhsT=h1T[:, c, 0:NS], rhs=wc[:, c, col0 + half * 512: col0 + (half + 1) * 512],
                                                        start=(c == 0), stop=(c == 7)), [r_h1T, r_wc], [rps[b]])
            return b

        def heads_T(dst, r_dst):
            pb_ = ps_bf(7)
            for h in range(8):
                S.op("pe", lambda e, h=h: e.transpose(out=pb_[:, h * NS:(h + 1) * NS], in_=nb16[0:NS, h * 128:(h + 1) * 128], identity=ident_b[0:NS, 0:NS]),
                     [r_nb16, r_identb], [rps[7]])
            S.op("dve", lambda e: e.tensor_copy(out=dst, in_=pb_[:, 0:8 * NS].rearrange("p (h t) -> p h t", h=8)), [rps[7]], [r_dst])

        ks, rks = stg[0], r_stg[0]
        vs, rvs = stg[1], r_stg[1]
        for half in range(2):
            b = sproj(D, half)
            S.op("act", lambda e, b=b, half=half: e.copy(out=ks[0:NS, half * 512:(half + 1) * 512], in_=psum[b][0:NS, :]), [rps[b]], [rks])
        head_norm(ks, rks, 64, NS)
        outs.append(S.dma("sp", lambda e: e.dma_start(out=T["ks_out"][:, :], in_=ks[0:NS, :]), [rks], []))
        S.op("act", lambda e: e.copy(out=nb16[0:NS, :], in_=ks[0:NS, :]), [rks], [r_nb16])
        heads_T(ktn, r_ktn)
        for half in range(2):
            b = sproj(2 * D, half)
            S.op("act", lambda e, b=b, half=half: e.copy(out=vs[0:NS, half * 512:(half + 1) * 512], in_=psum[b][0:NS, :]), [rps[b]], [rvs])
        S.op("dve", lambda e: e.tensor_copy(out=vnew[0:NS, :, 0:128], in_=vs[0:NS, :].rearrange("p (h d) -> p h d", h=8)), [rvs], [r_vnew])
        outs.append(S.dma("sp", lambda e: e.dma_start(out=T["vs_out"][:, :], in_=vs[0:NS, :]), [rvs], []))
        for half in range(2):
            b = sproj(0, half)
            S.op("act", lambda e, b=b, half=half: e.copy(out=ks[0:NS, half * 512:(half + 1) * 512], in_=psum[b][0:NS, :]), [rps[b]], [rks])
        head_norm(ks, rks, 0, NS)
        S.op("act", lambda e: e.copy(out=nb16[0:NS, :], in_=ks[0:NS, :]), [rks], [r_nb16])
        heads_T(QTs, r_QTs)
        for half in range(2):
            b = sproj(3 * D, half)
            S.op("act", lambda e, b=b, half=half: e.activation(out=sg[0:NS, 0, half * 512:(half + 1) * 512], in_=psum[b][0:NS, :], func=AF.Silu), [rps[b]], [r_sg])

        for b in range(3):
            S.op("dve", lambda e, b=b: e.memset(psum[b][0:32, :], 0.0), [], [rps[b]])

        def pv(P2d_of_h, rP, vt, r_vt, np_):
            for h in range(8):
                S.op("pe", lambda e, h=h: e.matmul(psum[h // 3][0:32, (h % 3) * 129:(h % 3 + 1) * 129], lhsT=P2d_of_h(h), rhs=vt[0:np_, h, :],
                                                   start=False, stop=False, skip_group_check=True), [rP, r_vt], [rps[h // 3]])

        npages = DBG.get("npg", NPAGE)
        ck = T["cache_k"]
        cv3 = T["cache_v"].rearrange("r (h d) -> r h d", h=8)
        for e_ in range(4):
            for j in range(npages):
                gi = e_ * NPAGE + j
                bi = gi % 2
                S.dma("pool", lambda e, gi=gi, bi=bi: e.indirect_dma_start(out=kpg[bi], out_offset=None, in_=ck[:, :],
                                                                        in_offset=bass.IndirectOffsetOnAxis(ap=idx[:, gi:gi + 1], axis=0)), [r_idx], [r_kpg[bi]])
                S.dma("pool", lambda e, gi=gi, bi=bi: e.indirect_dma_start(out=vfl[bi], out_offset=None, in_=T["cache_v"][:, :],
                                                                        in_offset=bass.IndirectOffsetOnAxis(ap=idx[:, gi:gi + 1], axis=0)), [r_idx], [r_vfl[bi]])
                S.op("pool", lambda e, bi=bi: e.tensor_copy(out=vpg[bi][:, :, 0:128], in_=vfl[bi].rearrange("p (h d) -> p h d", h=8)), [r_vfl[bi]], [r_vpg[bi]])
                pb7 = ps_bf(7)
                for h in range(8):
                    S.op("pe", lambda e, h=h, bi=bi: e.transpose(out=pb7[:, h * 128:(h + 1) * 128], in_=kpg[bi][:, h * 128:(h + 1) * 128], identity=ident_b),
                         [r_kpg[bi], r_identb], [rps[7]])
                S.op("dve" if gi % 2 else "act",
                     (lambda e, bi=bi: e.tensor_copy(out=ktp[bi], in_=pb7.rearrange("p (h t) -> p h t", h=8))) if gi % 2 else
                     (lambda e, bi=bi: e.copy(out=ktp[bi], in_=pb7.rearrange("p (h t) -> p h t", h=8))), [rps[7]], [r_ktp[bi]])
                sbk = 3 + 2 * (gi % 2)
                for h in range(8):
                    for c in range(2):
                        S.op("pe", lambda e, h=h, c=c, bi=bi, sbk=sbk, e_=e_: e.matmul(psum[sbk + c][:, h * 4:(h + 1) * 4], lhsT=ktp[bi][c * 64:(c + 1) * 64, h, :],
                                                                                    rhs=QTs[c * 64:(c + 1) * 64, h, e_ * 4:(e_ + 1) * 4], start=True, stop=True),
                             [r_ktp[bi], r_QTs], [rps[sbk + c]])
                P = pte[e_][bi]; rP = r_pte[e_][bi]
                for c in range(2):
                    S.op("act", lambda e, c=c, P=P, sbk=sbk, e_=e_: e.activation(out=P[:, :, c, e_ * 4:(e_ + 1) * 4], in_=psum[sbk + c][:, 0:32].rearrange("p (h q) -> p h q", h=8),
                                                                             func=AF.Exp, scale=0.125, bias=lam[:, 1:2]), [rps[sbk + c], r_lam], [rP])
                pv(lambda h, P=P: P[:, h].rearrange("p c q -> p (c q)"), rP, vpg[bi], r_vpg[bi], 128)
        for h in range(8):
            for c in range(2):
                S.op("pe", lambda e, h=h, c=c: e.matmul(psum[3 + c][0:NS, h * NS:(h + 1) * NS], lhsT=ktn[c * 64:(c + 1) * 64, h, :], rhs=QTs[c * 64:(c + 1) * 64, h, :],
                                                        start=True, stop=True), [r_ktn, r_QTs], [rps[3 + c]])
        for c in range(2):
            S.op("act", lambda e, c=c: e.activation(out=ptn[0:NS, :, c, :], in_=psum[3 + c][0:NS, 0:8 * NS].rearrange("p (h q) -> p h q", h=8),
                                                    func=AF.Exp, scale=0.125, bias=lam[0:NS, 1:2]), [rps[3 + c], r_lam], [r_ptn])
            S.op("dve", lambda e, c=c: e.tensor_tensor(out=ptn[0:NS, :, c, :], in0=ptn[0:NS, :, c, :], in1=smb[0:NS, :].unsqueeze(1).broadcast_to([NS, 8, NS]), op=ALU.mult),
                 [r_ptn, r_smb], [r_ptn])
        pv(lambda h: ptn[0:NS, h].rearrange("p c q -> p (c q)"), r_ptn, vnew, r_vnew, NS)

        for b in range(3):
            n = 387 if b < 2 else 258
            S.op("act" if b % 2 else "dve",
                 (lambda e, b=b, n=n: e.copy(out=O1[0:32, b * 387:b * 387 + n], in_=psum[b][0:32, 0:n])) if b % 2 else
                 (lambda e, b=b, n=n: e.tensor_copy(out=O1[0:32, b * 387:b * 387 + n], in_=psum[b][0:32, 0:n])), [rps[b]], [r_O1])
        for i, (c0, n) in enumerate(((0, 512), (512, 512), (1024, 8))):
            S.op("pe", lambda e, i=i, c0=c0, n=n: e.matmul(psum[3 + i][0:NS, 0:n], lhsT=ident_f[0:32, 16:32], rhs=O1[0:32, c0:c0 + n], start=True, stop=True),
                 [r_O1, r_identf], [rps[3 + i]])
            S.op("dve", lambda e, i=i, c0=c0, n=n: e.tensor_copy(out=O2[0:NS, c0:c0 + n], in_=psum[3 + i][0:NS, 0:n]), [rps[3 + i]], [r_O2])
        O1v = O1[0:NS, :].rearrange("p (h d) -> p h d", h=8)
        O2v = O2[0:NS, :].rearrange("p (h d) -> p h d", h=8)
        S.op("dve", lambda e: e.reciprocal(out=rr[0:NS, 0:8], in_=O1v[:, :, 128]), [r_O1], [r_rr])
        S.op("dve", lambda e: e.reciprocal(out=rr[0:NS, 8:16], in_=O2v[:, :, 128]), [r_O2], [r_rr])
        S.op("dve", lambda e: e.tensor_scalar(out=rr[0:NS, 8:16], in0=rr[0:NS, 8:16], scalar1=lam[0:NS, 0:1], scalar2=None, op0=ALU.mult), [r_rr, r_lam], [r_rr])
        o_a = ks[0:NS, :].rearrange("p (h d) -> p h d", h=8)
        o_b = vs[0:NS, :].rearrange("p (h d) -> p h d", h=8)
        S.op("dve", lambda e: e.tensor_tensor(out=o_a, in0=O1v[:, :, 0:128], in1=rr[0:NS, 0:8].unsqueeze(2).broadcast_to([NS, 8, 128]), op=ALU.mult), [r_O1, r_rr], [rks])
        S.op("dve", lambda e: e.tensor_tensor(out=o_b, in0=O2v[:, :, 0:128], in1=rr[0:NS, 8:16].unsqueeze(2).broadcast_to([NS, 8, 128]), op=ALU.mult), [r_O2, r_rr], [rvs])
        S.op("dve", lambda e: e.tensor_tensor(out=o_a, in0=o_a, in1=o_b, op=ALU.add), [rks, rvs], [rks])
        S.op("pool", lambda e: e.tensor_tensor(out=sq[0:NS], in0=ks[0:NS], in1=ks[0:NS], op=ALU.mult), [rks], [r_sq])
        S.op("dve", lambda e: e.tensor_reduce(out=rs16[0:NS, 0:8], in_=sq[0:NS].rearrange("p (h d) -> p h d", h=8), axis=AX.X, op=ALU.add), [r_sq], [r_rs16])
        rms_rstd(rs16[0:NS, 0:8], 128, [r_rs16])
        S.op("dve", lambda e: e.tensor_tensor(out=o_a, in0=o_a, in1=rs16[0:NS, 0:8].unsqueeze(2).broadcast_to([NS, 8, 128]), op=ALU.mult), [rks, r_rs16], [rks])
        S.op("dve", lambda e: e.tensor_tensor(out=o_a, in0=o_a, in1=sublnw[0:NS, :].unsqueeze(1).broadcast_to([NS, 8, 128]), op=ALU.mult), [rks, r_subw], [rks])
        S.op("dve", lambda e: e.tensor_tensor(out=og[0:NS, 0, :], in0=ks[0:NS, :], in1=sg[0:NS, 0, :], op=ALU.mult), [rks, r_sg], [r_og])
        pb_ = ps_bf(7)
        for h in range(8):
            S.op("pe", lambda e, h=h: e.transpose(out=pb_[:, h * NS:(h + 1) * NS], in_=og[0:NS, 0, h * 128:(h + 1) * 128], identity=ident_b[0:NS, 0:NS]),
                 [r_og, r_identb], [rps[7]])
        S.op("act", lambda e: e.copy(out=ogT[:, :, 0:NS], in_=pb_[:, 0:8 * NS].rearrange("p (h t) -> p h t", h=8)), [rps[7]], [r_ogT])
        for half in range(2):
            b = 5 + half
            for kc in range(8):
                S.op("pe", lambda e, half=half, kc=kc, b=b: e.matmul(psum[b][0:NS, :], lhsT=ogT[:, kc, 0:NS], rhs=woc[:, kc, half * 512:(half + 1) * 512],
                                                                   start=(kc == 0), stop=(kc == 7)), [r_ogT, r_woc], [rps[b]])
            S.op("dve", lambda e, half=half, b=b: e.tensor_tensor(out=vs[0:NS, half * 512:(half + 1) * 512], in0=psum[b][0:NS, :],
                                                                in1=x1s[0:NS, half * 512:(half + 1) * 512], op=ALU.add), [rps[b], r_x1s], [rvs])
        outs.append(S.dma("sp", lambda e: e.dma_start(out=T["ys_out"][:, :], in_=vs[0:NS, :]), [rvs], []))

    if DBG["st"]:
        sample_l1()

    S.emit(nc, final_waits=outs)
    st.close()
    return nc


_CACHE = {}
_HOOK = [None]


def _get_nc():
    if "nc" not in _CACHE:
        nc, S, T = build_nc()
        emit_program(nc, S, T)
        _CACHE["nc"] = nc
    return _CACHE["nc"]


def kernel(x_prompt, x_sample, state_pool, state_conv, cache_k, cache_v, page_table,
           norm_w_ab, w_in_ab, pool_w, pool_scale, conv_w, conv_b, conv_ln_w, conv_ln_b, w_out_ab,
           norm_w_c, w_in_c, q_norm_w, k_norm_w, lambda_q1, lambda_k1, lambda_q2, lambda_k2,
           subln_w, w_out_c):
    f = lambda a: np.ascontiguousarray(np.asarray(a, dtype=np.float32))
    x_prompt = f(x_prompt)
    pvec_src = np.ascontiguousarray(np.concatenate([f(conv_w)[0], f(conv_b), f(conv_ln_w), f(conv_ln_b), f(pool_scale)], axis=0))
    qk_w = np.ascontiguousarray(np.concatenate([f(q_norm_w), f(k_norm_w)], axis=1))
    lam_v = np.ascontiguousarray(np.concatenate([f(lambda_q1), f(lambda_k1), f(lambda_q2), f(lambda_k2)], axis=1))
    shared = {
        "norm_w_ab": f(norm_w_ab), "w_in_ab": f(w_in_ab)[0], "pool_w": f(pool_w)[0], "pvec_src": pvec_src,
        "w_out_ab": f(w_out_ab)[0], "norm_w_c": f(norm_w_c), "w_in_c": f(w_in_c)[0], "qk_w": qk_w, "lam_v": lam_v,
        "subln_w": f(subln_w), "w_out_c": f(w_out_c)[0],
    }
    p = np.arange(128)
    x_sample = f(x_sample); state_pool = f(state_pool); state_conv = f(state_conv)
    ck2 = f(cache_k).reshape(-1, D)
    cv2 = f(cache_v).reshape(-1, D)
    kk = np.arange(16)
    smask = ((kk[:, None] // 4 == kk[None, :] // 4) & (kk[:, None] % 4 <= kk[None, :] % 4)).astype(np.float32)
    in_maps = []
    for c in range(8):
        b, hf = c // 2, c % 2
        own_rows = np.zeros((128, 16), np.int32)
        for s_ in range(8):
            for blk in range(2):
                own_rows[:, s_ * 2 + blk] = (2 * s_ + hf) * TT + blk * 128 + p
        am = np.zeros((128, 2, 2, TT), np.float32)
        q = np.arange(TT)
        for kt2 in range(2):
            for kblk in range(2):
                am[:, kt2, kblk, :] = ((kt2 - hf) * TT + kblk * 128 + p[:, None] <= q[None, :]).astype(np.float32)
        m = dict(shared)
        m["xs"] = np.ascontiguousarray(x_sample[4 * c:4 * c + 4].reshape(16, D))
        m["spool"] = np.ascontiguousarray(state_pool[0, 4 * c:4 * c + 4])
        m["sconv"] = np.ascontiguousarray(state_conv[0, 4 * c:4 * c + 4])
        m["pt"] = np.ascontiguousarray(np.asarray(page_table, dtype=np.int32)[4 * c:4 * c + 4].reshape(1, 256))
        m["smask"] = smask
        m["cache_k"] = ck2
        m["cache_v"] = cv2
        m["xp"] = x_prompt[b]
        m["own_rows"] = own_rows
        m["amask"] = am.reshape(128, 4 * TT)
        in_maps.append(m)
    nc = _get_nc()
    res = run_bass_kernel_spmd(nc, in_maps, core_ids=list(range(8)))
    R = res.results
    if _HOOK[0] is not None:
        _HOOK[0](R)
    y_prompt = np.zeros((NBATCH, SEQ, D), np.float32)
    k_p = np.zeros((1, NBATCH, SEQ, 8, 128), np.float32)
    v_p = np.zeros((1, NBATCH, SEQ, 8, 128), np.float32)
    pool_p = np.zeros((1, NBATCH, 15, D), np.float32)
    conv_p = np.zeros((1, NBATCH, 30, D), np.float32)
    for c in range(8):
        b, hf = c // 2, c % 2
        yo = R[c]["y_own"].reshape(8, TT, D)
        for s_ in range(8):
            t = 2 * s_ + hf
            y_prompt[b, t * TT:(t + 1) * TT] = yo[s_]
        if hf == 0:
            k_p[0, b] = R[c]["k_all"].reshape(SEQ, 8, 128)
            v_p[0, b] = R[c]["v_all"].reshape(SEQ, 8, 128)
            pool_p[0, b] = R[c]["pool_st"]
            conv_p[0, b] = R[c]["conv_st"]
    ys = np.zeros((32, 4, D), np.float32)
    pool_s = np.zeros((1, 32, 15, D), np.float32)
    conv_s = np.zeros((1, 32, 30, D), np.float32)
    k_s = np.zeros((1, 32, 4, 8, 128), np.float32)
    v_s = np.zeros((1, 32, 4, 8, 128), np.float32)
    for c in range(8):
        ys[4 * c:4 * c + 4] = R[c]["ys_out"].reshape(4, 4, D)
        pool_s[0, 4 * c:4 * c + 4] = R[c]["pool_s_out"]
        conv_s[0, 4 * c:4 * c + 4] = R[c]["conv_s_out"]
        k_s[0, 4 * c:4 * c + 4] = R[c]["ks_out"].reshape(4, 4, 8, 128)
        v_s[0, 4 * c:4 * c + 4] = R[c]["vs_out"].reshape(4, 4, 8, 128)
    return (y_prompt, ys, pool_p, conv_p, k_p, v_p, pool_s, conv_s, k_s, v_s)
```

```python
import contextlib
import math
import os
import numpy as np
import concourse.bass as bass
import concourse.mybir as mybir
from concourse.bass_utils import run_bass_kernel_spmd

F32 = mybir.dt.float32
BF16 = mybir.dt.bfloat16
I32 = mybir.dt.int32
U8 = mybir.dt.uint8
ALU = mybir.AluOpType
AF = mybir.ActivationFunctionType
AX = mybir.AxisListType

NDSEM = 12

D = 1024
SEQ = 4096
NBATCH = 4
TT = 256
NT = SEQ // TT
NB = TT // 128
EPS = 1e-6
LAMBDA_INIT = 0.8 - 0.6 * math.exp(-0.3 * 1)
OWN = {0: [0, 15, 2, 13, 4, 11, 6, 9], 1: [1, 14, 3, 12, 5, 10, 7, 8]}
POOL_W = (2, 4, 8, 16)
NPOOL = int(os.environ.get("K_NPOOL", 2560))
NPAGE = 64
import os
DBG = {"nt1": int(os.environ.get("K_NT1", NT)), "nt2": int(os.environ.get("K_NT2", NT)), "attn": int(os.environ.get("K_ATTN", 1)), "st": int(os.environ.get("K_ST", 1)), "npg": int(os.environ.get("K_NPG", NPAGE))}


class Res:
    __slots__ = ("name", "w", "r", "excl")

    def __init__(self, name="", excl=False):
        self.name = name
        self.w = None
        self.r = []
        self.excl = excl


class Sched:
    ENGS = ("pe", "act", "dve", "pool", "sp")

    def __init__(self):
        self.ops = {e: [] for e in self.ENGS}
        self.ndma = {e: 0 for e in self.ENGS}
        self.allres = []
        self.pending = {e: set() for e in self.ENGS}

    def res(self, name="", excl=False):
        r = Res(name, excl)
        self.allres.append(r)
        return r

    def barrier(self):
        b = set()
        for r in self.allres:
            if r.w is not None:
                b.add(r.w)
            b.update(r.r)
        for e in self.ENGS:
            self.pending[e] |= b

    def _deps(self, eng, reads, writes, is_dma):
        deps = set()
        for r in reads:
            if r.w is not None:
                w = r.w
                if w[0] == "c" and w[1] == eng and eng == "pe" and not is_dma:
                    continue
                deps.add(w)
        for wr in writes:
            cands = list(wr.r)
            if wr.w is not None:
                cands.append(wr.w)
            for w in cands:
                if w[0] == "c" and w[1] == eng and not is_dma:
                    continue
                deps.add(w)
        if self.pending[eng]:
            for w in self.pending[eng]:
                if w[0] == "c" and w[1] == eng and not is_dma:
                    continue
                deps.add(w)
            self.pending[eng] = set()
        return deps

    def _commit(self, me, reads, writes):
        for r in reads:
            r.r.append(me)
        for w in writes:
            w.w = me
            w.r = []

    def op(self, eng, fn, reads=(), writes=()):
        writes = list(writes) + [r for r in reads if r.excl]
        reads = [r for r in reads if not r.excl]
        deps = self._deps(eng, reads, writes, False)
        idx = len(self.ops[eng])
        self.ops[eng].append({"fn": fn, "deps": deps, "dma": None})
        me = ("c", eng, idx)
        self._commit(me, reads, writes)
        return me

    def dma(self, eng, fn, reads=(), writes=()):
        deps = self._deps(eng, reads, writes, True)
        k = self.ndma[eng]
        self.ndma[eng] += 1
        if k >= NDSEM:
            deps.add(("d", eng, k - NDSEM))
        self.ops[eng].append({"fn": fn, "deps": deps, "dma": k})
        me = ("d", eng, k)
        self._commit(me, reads, writes)
        return me

    def emit(self, nc, final_waits=()):
        need = {e: set() for e in self.ENGS}
        for e in self.ENGS:
            for o in self.ops[e]:
                for d in o["deps"]:
                    if d[0] == "c":
                        need[d[1]].add(d[2])
        for d in final_waits:
            if d[0] == "c":
                need[d[1]].add(d[2])
        cnt = {}
        for e in self.ENGS:
            c = 0
            m = {}
            for i in sorted(need[e]):
                c += 1
                m[i] = c
            cnt[e] = m
        with contextlib.ExitStack() as st:
            csem = {e: st.enter_context(nc.semaphore("c_" + e)) for e in self.ENGS}
            dsem = {e: [st.enter_context(nc.semaphore("d_%s_%d" % (e, i))) for i in range(NDSEM)]
                    for e in self.ENGS if self.ndma[e] > 0}
            block = st.enter_context(nc.Block())

            def body(e, engine):
                waited = {}

                def do_wait(d):
                    if d[0] == "c":
                        key = ("c", d[1]); sem = csem[d[1]]; val = cnt[d[1]][d[2]]
                    else:
                        key = ("d", d[1], d[2] % NDSEM); sem = dsem[d[1]][d[2] % NDSEM]
                        val = 16 * (d[2] // NDSEM + 1)
                    if waited.get(key, 0) >= val:
                        return
                    waited[key] = val
                    engine.wait_ge(sem, val)

                for i, o in enumerate(self.ops[e]):
                    for d in sorted(o["deps"]):
                        do_wait(d)
                    ins = o["fn"](engine)
                    if o["dma"] is not None:
                        ins.then_inc(dsem[e][o["dma"] % NDSEM], 16)
                    elif i in cnt[e]:
                        ins.then_inc(csem[e], 1)
                if e == "sp":
                    for d in final_waits:
                        do_wait(d)

            block.tensor(lambda eng: body("pe", eng))
            block.scalar(lambda eng: body("act", eng))
            block.vector(lambda eng: body("dve", eng))
            block.gpsimd(lambda eng: body("pool", eng))
            block.sync(lambda eng: body("sp", eng))


class Arena:
    def __init__(self, nc, st, nbytes):
        self.t = st.enter_context(nc.sbuf_tensor("arena", [128, nbytes], U8))
        self.n = nbytes
        self.off = 0
        self.peak = 0

    def alloc(self, shape, dt):
        esz = 4 if dt in (F32, I32) else 2
        n = esz
        for s in shape:
            n *= s
        n = (n + 31) // 32 * 32
        assert self.off + n <= self.n, "arena overflow: %d + %d > %d" % (self.off, n, self.n)
        v = self.t[:, self.off:self.off + n].bitcast(dt)
        tot = 1
        for s in shape:
            tot *= s
        v = v[:, 0:tot]
        if len(shape) == 2:
            v = v.rearrange("p (a b) -> p a b", a=shape[0])
        elif len(shape) == 3:
            v = v.rearrange("p (a b c) -> p a b c", a=shape[0], b=shape[1])
        self.off += n
        self.peak = max(self.peak, self.off)
        return v

    def mark(self):
        return self.off

    def reset(self, m):
        self.off = m


def build_nc():
    nc = bass.Bass("TRN2", target_bir_lowering=False)
    S = Sched()
    dt_in = lambda name, shape, dt=F32: nc.dram_tensor(name, shape, dt, kind="ExternalInput").ap()
    dt_out = lambda name, shape, dt=F32: nc.dram_tensor(name, shape, dt, kind="ExternalOutput").ap()
    dt_int = lambda name, shape, dt: nc.dram_tensor(name, shape, dt, kind="Internal").ap()

    xp = dt_in("xp", [SEQ, D])
    own_idx_unused = None
    norm_w_ab = dt_in("norm_w_ab", [1, D])
    w_in_ab = dt_in("w_in_ab", [D, 5 * D])
    pool_w = dt_in("pool_w", [4, 256, 256])
    pvec_src = dt_in("pvec_src", [35, D])
    w_out_ab = dt_in("w_out_ab", [2 * D, D])
    norm_w_c = dt_in("norm_w_c", [1, D])
    w_in_c = dt_in("w_in_c", [D, 4 * D])
    qk_w = dt_in("qk_w", [1, 128])
    lam_v = dt_in("lam_v", [1, 256])
    subln_w = dt_in("subln_w", [1, 128])
    w_out_c = dt_in("w_out_c", [D, D])
    own_tiles = None

    xs = dt_in("xs", [16, D])
    spool = dt_in("spool", [4, 15, D])
    sconv = dt_in("sconv", [4, 30, D])
    pt = dt_in("pt", [1, 256], I32)
    smask = dt_in("smask", [16, 16])
    cache_k = dt_in("cache_k", [NPOOL * 128, D])
    cache_v = dt_in("cache_v", [NPOOL * 128, D])
    ys_out = dt_out("ys_out", [16, D])
    pool_s_out = dt_out("pool_s_out", [4, 15, D])
    conv_s_out = dt_out("conv_s_out", [4, 30, D])
    ks_out = dt_out("ks_out", [16, D])
    vs_out = dt_out("vs_out", [16, D])
    own_rows = dt_in("own_rows", [128, 16], I32)
    amask = dt_in("amask", [128, 4 * TT])
    y_own = dt_out("y_own", [SEQ // 2, D])
    k_all = dt_out("k_all", [SEQ, D])
    v_all = dt_out("v_all", [SEQ, D])
    pool_st = dt_out("pool_st", [15, D])
    conv_st = dt_out("conv_st", [30, D])

    if os.environ.get("K_DUMP"):
        d_h0T = dt_out("d_h0T", [128, 8 * TT], BF16)
        d_win = dt_out("d_win", [128, 1024], BF16)
        d_h0 = dt_out("d_h0", [128, NB * D], BF16)
        d_xae = dt_out("d_xae", [128, 2 * (15 + TT)], F32)
    x1_scr = dt_int("x1_scr", [SEQ, D], F32)
    kt_scr = dt_int("kt_scr", [8, 128, SEQ], BF16)
    v_scr = dt_int("v_scr", [8, 128, SEQ // 128, 129], BF16)

    return nc, S, locals()


def emit_program(nc, S, T):
    xp = T["xp"]
    outs = []
    st = contextlib.ExitStack()
    A = Arena(nc, st, 211968)
    psum_all = st.enter_context(nc.psum_tensor("ps_all", [128, 4096], F32))
    psum = [psum_all[:, i * 512:(i + 1) * 512] for i in range(8)]
    rps = [S.res("ps%d" % i, excl=True) for i in range(8)]

    def ps_bf(i):
        return psum[i][:, :].bitcast(BF16)

    ident_f = A.alloc([128], F32); r_identf = S.res()
    ident_b = A.alloc([128], BF16); r_identb = S.res()
    mask_b = A.alloc([128], BF16); r_mask = S.res()
    ones_b = A.alloc([128], BF16); r_ones = S.res()
    iot = A.alloc([128], F32); r_iot = S.res()
    pvec = A.alloc([8, 35], F32); r_pvec = S.res()
    nw = A.alloc([D], F32); r_nw = S.res()
    qkw = A.alloc([128], F32); r_qkw = S.res()
    sublnw = A.alloc([128], F32); r_subw = S.res()
    lamv = A.alloc([256], F32); r_lamv = S.res()
    lam = A.alloc([4], F32); r_lam = S.res()
    invc = A.alloc([4, 16], F32); r_invc = S.res()
    small = A.alloc([64], F32); r_small = S.res()
    junk = A.alloc([D], BF16); r_junk = S.res()

    S.op("pool", lambda e: e.iota(out=iot, pattern=[[1, 128]], base=0, channel_multiplier=-1,
                                  allow_small_or_imprecise_dtypes=True), [], [r_iot])
    S.op("dve", lambda e: e.tensor_single_scalar(out=ident_f, in_=iot, scalar=0.0, op=ALU.is_equal), [r_iot], [r_identf])
    S.op("dve", lambda e: e.tensor_copy(out=ident_b, in_=ident_f), [r_identf], [r_identb])
    S.op("dve", lambda e: e.tensor_single_scalar(out=mask_b, in_=iot, scalar=0.0, op=ALU.is_ge), [r_iot], [r_mask])
    S.op("dve", lambda e: e.memset(ones_b, 1.0), [], [r_ones])
    iot2 = A.alloc([16], F32); r_iot2 = S.res()
    S.op("pool", lambda e: e.iota(out=iot2, pattern=[[1, 16]], base=1, channel_multiplier=0,
                                  allow_small_or_imprecise_dtypes=True), [], [r_iot2])
    for g, w in enumerate(POOL_W):
        S.op("dve", lambda e, g=g, w=w: e.tensor_scalar(out=invc[:, g, :], in0=iot2, scalar1=float(w), scalar2=None, op0=ALU.min),
             [r_iot2], [r_invc])
    S.op("dve", lambda e: e.reciprocal(out=invc, in_=invc), [r_invc], [r_invc])

    m0 = A.mark()
    t35 = A.alloc([D], F32); r_t35 = S.res()
    S.dma("sp", lambda e: e.dma_start(out=t35[0:35, :], in_=T["pvec_src"][:, :]), [], [r_t35])
    for c in range(8):
        S.op("pe", lambda e, c=c: e.transpose(out=psum[0][:, c * 35:(c + 1) * 35], in_=t35[0:35, c * 128:(c + 1) * 128],
                                              identity=ident_f[0:35, 0:35]), [r_t35, r_identf], [rps[0]])
    S.op("dve", lambda e: e.tensor_copy(out=pvec, in_=psum[0][:, 0:280].rearrange("p (c k) -> p c k", c=8)), [rps[0]], [r_pvec])

    S.dma("sp", lambda e: e.dma_start(out=qkw, in_=T["qk_w"][0:1, :].partition_broadcast(128)), [], [r_qkw])
    S.dma("sp", lambda e: e.dma_start(out=sublnw, in_=T["subln_w"][0:1, :].partition_broadcast(128)), [], [r_subw])
    S.dma("sp", lambda e: e.dma_start(out=lamv, in_=T["lam_v"][0:1, :].partition_broadcast(128)), [], [r_lamv])
    S.op("dve", lambda e: e.tensor_scalar(out=sublnw, in0=sublnw, scalar1=float(1.0 - LAMBDA_INIT), scalar2=None, op0=ALU.mult),
         [r_subw], [r_subw])
    S.op("dve", lambda e: e.tensor_tensor(out=small[:, 0:64], in0=lamv[:, 0:64], in1=lamv[:, 64:128], op=ALU.mult), [r_lamv], [r_small])
    S.op("dve", lambda e: e.tensor_reduce(out=lam[:, 2:3], in_=small[:, 0:64], axis=AX.X, op=ALU.add), [r_small], [r_lam])
    S.op("dve", lambda e: e.tensor_tensor(out=small[:, 0:64], in0=lamv[:, 128:192], in1=lamv[:, 192:256], op=ALU.mult), [r_lamv, r_lam], [r_small])
    S.op("dve", lambda e: e.tensor_reduce(out=lam[:, 3:4], in_=small[:, 0:64], axis=AX.X, op=ALU.add), [r_small], [r_lam])
    S.op("act", lambda e: e.activation(out=lam[:, 2:4], in_=lam[:, 2:4], func=AF.Exp), [r_lam], [r_lam])
    S.op("dve", lambda e: e.scalar_tensor_tensor(out=lam[:, 0:1], in0=lam[:, 3:4], scalar=float(-LAMBDA_INIT), in1=lam[:, 2:3],
                                                 op0=ALU.add, op1=ALU.subtract), [r_lam], [r_lam])
    S.op("dve", lambda e: e.tensor_reduce(out=lam[:, 2:3], in_=qkw[:, 0:64], axis=AX.X, op=ALU.max, apply_absolute_value=True), [r_qkw, r_lam], [r_lam])
    S.op("dve", lambda e: e.tensor_reduce(out=lam[:, 3:4], in_=qkw[:, 64:128], axis=AX.X, op=ALU.max, apply_absolute_value=True), [r_qkw, r_lam], [r_lam])
    S.op("dve", lambda e: e.scalar_tensor_tensor(out=lam[:, 1:2], in0=lam[:, 2:3], scalar=-8.0, in1=lam[:, 3:4],
                                                 op0=ALU.mult, op1=ALU.mult), [r_lam], [r_lam])

    def rms_rstd(ss_ap, n, res_list):
        S.op("dve", lambda e: e.tensor_scalar(out=ss_ap, in0=ss_ap, scalar1=1.0 / n, scalar2=EPS, op0=ALU.mult, op1=ALU.add), res_list, res_list)
        S.op("act", lambda e: e.activation(out=ss_ap, in_=ss_ap, func=AF.Sqrt), res_list, res_list)
        S.op("dve", lambda e: e.reciprocal(out=ss_ap, in_=ss_ap), res_list, res_list)

    def rmsnorm_tile(xt, r_xt, h, r_h, hT, r_hT, nblk, ncols):
        for blk in range(nblk):
            ss = small[:, blk:blk + 1]
            S.op("act", lambda e, blk=blk, ss=ss: e.activation(out=junk, in_=xt[:, blk, :], func=AF.Square, accum_out=ss),
                 [r_xt], [r_junk, r_small])
            rms_rstd(ss, D, [r_small])
            S.op("dve", lambda e, blk=blk, ss=ss: e.scalar_tensor_tensor(out=h[:, blk, :], in0=xt[:, blk, :], scalar=ss, in1=nw,
                                                                         op0=ALU.mult, op1=ALU.mult), [r_xt, r_small, r_nw], [r_h])
            pb = ps_bf(7)
            for c in range(8):
                S.op("pe", lambda e, blk=blk, c=c, pb=pb: e.transpose(out=pb[:, c * 128:(c + 1) * 128], in_=h[:, blk, c * 128:(c + 1) * 128],
                                                                      identity=ident_b), [r_h, r_identb], [rps[7]])
            S.op("act", lambda e, blk=blk, pb=pb: e.copy(out=hT[:, :, blk * 128:(blk + 1) * 128],
                                                         in_=pb.rearrange("p (c t) -> p c t", c=8)), [rps[7]], [r_hT])

    S.dma("sp", lambda e: e.dma_start(out=nw, in_=T["norm_w_ab"][0:1, :].partition_broadcast(128)), [], [r_nw])
    x1s = A.alloc([D], F32); r_x1s = S.res()
    mP = A.mark()
    win = A.alloc([8, 5 * D], BF16); r_win = S.res()
    wout = A.alloc([16, D], BF16); r_wout = S.res()
    wpool = A.alloc([4, 2, 256], BF16); r_wpool = S.res()
    def cast_load(dst, r_dst, src, nchunk, ncols):
        for c in range(nchunk):
            for k in range(ncols // 1024):
                S.dma("pool", lambda e, c=c, k=k: e.dma_start(out=dst[:, c, k * 1024:(k + 1) * 1024],
                                                            in_=src[c * 128:(c + 1) * 128, k * 1024:(k + 1) * 1024]), [], [r_dst])

    cast_load(win, r_win, T["w_in_ab"], 8, 5 * D)
    cast_load(wout, r_wout, T["w_out_ab"], 16, D)
    for g in range(4):
        for cc in range(2):
            S.dma("pool", lambda e, g=g, cc=cc: e.dma_start(out=wpool[:, g, cc, :], in_=T["pool_w"][g, cc * 128:(cc + 1) * 128, :]), [], [r_wpool])

    def sample_l0():
        NS = 16
        xs_t = A.alloc([D], F32); r_xs = S.res()
        hs = A.alloc([D], BF16); r_hs = S.res()
        hTs = A.alloc([8, NS], BF16); r_hTs = S.res()
        stp = [A.alloc([D], F32) for _ in range(2)]; r_stp = [S.res() for _ in range(2)]
        xae = A.alloc([8, 4, 19], F32); r_xaes = S.res()
        sab = [A.alloc([8, 4, 19], F32) for _ in range(2)]; r_sab = [S.res() for _ in range(2)]
        glue = A.alloc([8, 4, 34], F32); r_glues = S.res()
        xac = A.alloc([8, NS], F32); r_xac = S.res()
        gluc = A.alloc([8, NS], F32); r_gluc = S.res()
        tok = A.alloc([D], F32); r_tok = S.res()
        tok2 = A.alloc([D], F32); r_tok2 = S.res()
        sgas = A.alloc([8, NS], BF16); r_sgas = S.res()
        sgbs = A.alloc([8, NS], BF16); r_sgbs = S.res()
        sigs = A.alloc([8, NS], F32); r_sigs = S.res()
        dTs = A.alloc([8, NS], BF16); r_dTs = S.res()
        cs = A.alloc([8, NS], F32); r_cs = S.res(); r_csj = [S.res() for _ in range(8)]
        cs16 = A.alloc([8, NS], BF16); r_cs16 = S.res()
        csq = A.alloc([8, NS], BF16); r_csq = S.res()
        lnt8 = A.alloc([8, NS], F32); r_lnt8 = S.res()
        mixs = A.alloc([16, NS], BF16); r_mixs = S.res()
        lnms = A.alloc([NS], F32); r_lnms = S.res()
        lnrs = A.alloc([NS], F32); r_lnrs = S.res()

        outs.append(S.dma("sp", lambda e: e.dma_start(out=T["pool_s_out"][:, 0:11, :], in_=T["spool"][:, 4:15, :]), [], []))
        outs.append(S.dma("sp", lambda e: e.dma_start(out=T["conv_s_out"][:, 0:26, :], in_=T["sconv"][:, 4:30, :]), [], []))
        S.dma("sp", lambda e: e.dma_start(out=xs_t[0:NS, :], in_=T["xs"][:, :]), [], [r_xs])
        ss = small[0:NS, 8:9]
        S.op("act", lambda e: e.activation(out=junk[0:NS, :], in_=xs_t[0:NS, :], func=AF.Square, accum_out=ss), [r_xs], [r_junk, r_small])
        rms_rstd(ss, D, [r_small])
        S.op("dve", lambda e: e.scalar_tensor_tensor(out=hs[0:NS, :], in0=xs_t[0:NS, :], scalar=ss, in1=nw[0:NS, :], op0=ALU.mult, op1=ALU.mult),
             [r_xs, r_small, r_nw], [r_hs])
        pb = ps_bf(7)
        for c in range(8):
            S.op("pe", lambda e, c=c: e.transpose(out=pb[:, c * NS:(c + 1) * NS], in_=hs[0:NS, c * 128:(c + 1) * 128], identity=ident_b[0:NS, 0:NS]),
                 [r_hs, r_identb], [rps[7]])
        S.op("act", lambda e: e.copy(out=hTs, in_=pb[:, 0:8 * NS].rearrange("p (c t) -> p c t", c=8)), [rps[7]], [r_hTs])
        for e_ in range(4):
            S.dma("sp", lambda e, e_=e_: e.dma_start(out=stp[0][0:15, :], in_=T["spool"][e_, :, :]), [], [r_stp[0]])
            for c in range(8):
                S.op("pe", lambda e, c=c: e.transpose(out=psum[5][:, c * 15:(c + 1) * 15], in_=stp[0][0:15, c * 128:(c + 1) * 128], identity=ident_f[0:15, 0:15]),
                     [r_stp[0], r_identf], [rps[5]])
            S.op("dve", lambda e, e_=e_: e.tensor_copy(out=xae[:, :, e_, 0:15], in_=psum[5][:, 0:120].rearrange("p (c k) -> p c k", c=8)), [rps[5]], [r_xaes])
            S.dma("sp", lambda e, e_=e_: e.dma_start(out=stp[1][0:30, :], in_=T["sconv"][e_, :, :]), [], [r_stp[1]])
            for c in range(8):
                S.op("pe", lambda e, c=c: e.transpose(out=psum[6][:, c * 30:(c + 1) * 30], in_=stp[1][0:30, c * 128:(c + 1) * 128], identity=ident_f[0:30, 0:30]),
                     [r_stp[1], r_identf], [rps[6]])
            S.op("act", lambda e, e_=e_: e.copy(out=glue[:, :, e_, 0:30], in_=psum[6][:, 0:240].rearrange("p (c k) -> p c k", c=8)), [rps[6]], [r_glues])
        for f_ in range(40):
            bank, col = f_ // 8, (f_ % 8) * NS
            for c in range(8):
                S.op("pe", lambda e, c=c, f_=f_, bank=bank, col=col: e.matmul(psum[bank][:, col:col + NS], lhsT=win[:, c, f_ * 128:(f_ + 1) * 128], rhs=hTs[:, c, :],
                                                                            start=(c == 0), stop=(c == 7)), [r_win, r_hTs], [rps[bank]])
        v816 = lambda ap: ap[:, 0:8 * NS].rearrange("p (c t) -> p c t", c=8)
        S.op("act", lambda e: e.copy(out=xae[:, :, :, 15:19], in_=psum[0][:, 0:8 * NS].rearrange("p (c a t) -> p c a t", c=8, a=4)), [rps[0]], [r_xaes])
        S.op("dve", lambda e: e.tensor_copy(out=xac, in_=v816(psum[0])), [rps[0]], [r_xac])
        S.op("act", lambda e: e.activation(out=sgas, in_=v816(psum[1]), func=AF.Silu), [rps[1]], [r_sgas])
        S.op("act", lambda e: e.activation(out=sigs, in_=v816(psum[3]), func=AF.Sigmoid), [rps[3]], [r_sigs])
        S.op("dve", lambda e: e.tensor_tensor(out=gluc, in0=v816(psum[2]), in1=sigs, op=ALU.mult), [rps[2], r_sigs], [r_gluc])
        S.op("pool", lambda e: e.tensor_copy(out=glue[:, :, :, 30:34], in_=gluc.rearrange("p c (a t) -> p c a t", a=4)), [r_gluc], [r_glues])
        S.op("act", lambda e: e.activation(out=sgbs, in_=v816(psum[4]), func=AF.Silu), [rps[4]], [r_sgbs])
        for (srcc, r_srcc, tk, r_tk, dst, row0) in ((xac, r_xac, tok, r_tok, "pool_s_out", 11), (gluc, r_gluc, tok2, r_tok2, "conv_s_out", 26)):
            for c in range(8):
                bk = 5 + c // 4
                S.op("pe", lambda e, c=c, bk=bk, srcc=srcc: e.transpose(out=psum[bk][0:NS, (c % 4) * 128:(c % 4 + 1) * 128], in_=srcc[:, c, :], identity=ident_f),
                     [r_srcc, r_identf], [rps[bk]])
            S.op("dve", lambda e, tk=tk: e.tensor_copy(out=tk[0:NS, 0:512], in_=psum[5][0:NS, :]), [rps[5]], [r_tk])
            S.op("dve", lambda e, tk=tk: e.tensor_copy(out=tk[0:NS, 512:1024], in_=psum[6][0:NS, :]), [rps[6]], [r_tk])
            for e_ in range(4):
                outs.append(S.dma("sp", lambda e, e_=e_, tk=tk, dst=dst, row0=row0: e.dma_start(out=T[dst][e_, row0:row0 + 4, :], in_=tk[e_ * 4:(e_ + 1) * 4, :]),
                                  [r_tk], []))
        for g in range(4):
            w = POOL_W[g]
            Xg = xae[:, 2 * g:2 * g + 2]
            src, rsrc = Xg, r_xaes
            for k in range(g + 1):
                sh = 1 << k
                dst_, rdst = sab[k % 2][:, 2 * g:2 * g + 2], r_sab[k % 2]
                S.op("dve", lambda e, src=src, dst_=dst_, sh=sh: e.tensor_tensor(out=dst_[:, :, :, sh:19], in0=src[:, :, :, sh:19], in1=src[:, :, :, 0:19 - sh], op=ALU.add),
                     [rsrc], [rdst])
                src, rsrc = dst_, rdst
            S.op("dve", lambda e, src=src, Xg=Xg, g=g, w=w: e.scalar_tensor_tensor(out=dTs[:, 2 * g:2 * g + 2, :].rearrange("p c (a t) -> p c a t", a=4),
                                                                                 in0=src[:, :, :, 15:19], scalar=1.0 / w, in1=Xg[:, :, :, 15:19],
                                                                                 op0=ALU.mult, op1=ALU.subtract), [rsrc, r_xaes], [r_dTs])
            for ec in range(2):
                for cc in range(2):
                    S.op("pe", lambda e, ec=ec, cc=cc, g=g: e.matmul(psum[7][:, (2 * g + ec) * NS:(2 * g + ec + 1) * NS], lhsT=wpool[:, g, cc, ec * 128:(ec + 1) * 128],
                                                                    rhs=dTs[:, 2 * g + cc, :], start=(cc == 0), stop=(cc == 1)), [r_wpool, r_dTs], [rps[7]])
        for c in range(8):
            S.op("dve", lambda e, c=c: e.scalar_tensor_tensor(out=mixs[:, c, :], in0=psum[7][:, c * NS:(c + 1) * NS], scalar=pvec[:, c, 34:35], in1=sgas[:, c, :],
                                                             op0=ALU.mult, op1=ALU.mult), [rps[7], r_pvec, r_sgas], [r_mixs])
        for k in range(31):
            for j in range(8):
                csj = cs[:, j, :].rearrange("p (a t) -> p a t", a=4)
                if k == 0:
                    S.op("dve", lambda e, j=j, csj=csj: e.tensor_scalar(out=csj, in0=glue[:, j, :, 0:4], scalar1=pvec[:, j, 0:1], scalar2=pvec[:, j, 31:32],
                                                                        op0=ALU.mult, op1=ALU.add), [r_glues, r_pvec], [r_csj[j]])
                else:
                    S.op("dve", lambda e, j=j, k=k, csj=csj: e.scalar_tensor_tensor(out=csj, in0=glue[:, j, :, k:k + 4], scalar=pvec[:, j, k:k + 1], in1=csj,
                                                                                  op0=ALU.mult, op1=ALU.add), [r_glues, r_pvec, r_csj[j]], [r_csj[j]])
        S.op("dve", lambda e: e.memset(lnms, 0.0), r_csj, [r_cs, r_lnms])
        S.op("act", lambda e: e.activation(out=csq, in_=cs, func=AF.Square), [r_cs], [r_csq])
        S.op("pool", lambda e: e.tensor_copy(out=cs16, in_=cs), [r_cs], [r_cs16])
        for j in range(8):
            S.op("pe", lambda e, j=j: e.matmul(psum[5][:, 0:NS], lhsT=ones_b, rhs=cs16[:, j, :], start=(j == 0), stop=(j == 7)), [r_ones, r_cs16], [rps[5]])
        for j in range(8):
            S.op("pe", lambda e, j=j: e.matmul(psum[6][:, 0:NS], lhsT=ones_b, rhs=csq[:, j, :], start=(j == 0), stop=(j == 7)), [r_ones, r_csq], [rps[6]])
        S.op("dve", lambda e: e.tensor_scalar(out=lnms, in0=psum[5][:, 0:NS], scalar1=1.0 / D, scalar2=None, op0=ALU.mult), [rps[5]], [r_lnms])
        S.op("dve", lambda e: e.tensor_tensor(out=lnrs, in0=lnms, in1=lnms, op=ALU.mult), [r_lnms], [r_lnrs])
        S.op("dve", lambda e: e.scalar_tensor_tensor(out=lnrs, in0=psum[6][:, 0:NS], scalar=1.0 / D, in1=lnrs, op0=ALU.mult, op1=ALU.subtract), [rps[6], r_lnrs], [r_lnrs])
        S.op("dve", lambda e: e.tensor_scalar(out=lnrs, in0=lnrs, scalar1=EPS, scalar2=None, op0=ALU.add), [r_lnrs], [r_lnrs])
        S.op("act", lambda e: e.activation(out=lnrs, in_=lnrs, func=AF.Sqrt), [r_lnrs], [r_lnrs])
        S.op("dve", lambda e: e.reciprocal(out=lnrs, in_=lnrs), [r_lnrs], [r_lnrs])
        S.op("dve", lambda e: e.tensor_tensor(out=lnt8, in0=cs, in1=lnms.unsqueeze(1).broadcast_to([128, 8, NS]), op=ALU.subtract), [r_cs, r_lnms], [r_lnt8])
        S.op("dve", lambda e: e.tensor_tensor(out=lnt8, in0=lnt8, in1=lnrs.unsqueeze(1).broadcast_to([128, 8, NS]), op=ALU.mult), [r_lnt8, r_lnrs], [r_lnt8])
        for j in range(8):
            S.op("act", lambda e, j=j: e.activation(out=lnt8[:, j, :], in_=lnt8[:, j, :], func=AF.Silu, scale=pvec[:, j, 32:33], bias=pvec[:, j, 33:34]),
                 [r_lnt8, r_pvec], [r_lnt8])
        S.op("dve", lambda e: e.tensor_tensor(out=mixs[:, 8:16, :], in0=lnt8, in1=sgbs, op=ALU.mult), [r_lnt8, r_sgbs], [r_mixs])
        for half in range(2):
            bo = 5 + half
            for kc in range(16):
                S.op("pe", lambda e, half=half, kc=kc, bo=bo: e.matmul(psum[bo][0:NS, :], lhsT=mixs[:, kc, :], rhs=wout[:, kc, half * 512:(half + 1) * 512],
                                                                      start=(kc == 0), stop=(kc == 15)), [r_mixs, r_wout], [rps[bo]])
            S.op("dve", lambda e, half=half, bo=bo: e.tensor_tensor(out=x1s[0:NS, half * 512:(half + 1) * 512], in0=psum[bo][0:NS, :],
                                                                   in1=xs_t[0:NS, half * 512:(half + 1) * 512], op=ALU.add), [rps[bo], r_xs], [r_x1s])

    mS = A.mark()
    if DBG["st"]:
        sample_l0()
        print("sample-L0 arena bytes:", A.off)
        S.barrier()
    A.reset(mS)

    xt = A.alloc([NB, D], F32); r_xt = S.res()
    h0 = A.alloc([NB, D], BF16); r_h0 = S.res()
    h0T = A.alloc([8, TT], BF16); r_h0T = S.res()
    xa_h = A.alloc([8, 15], F32); r_xah = S.res()
    glu_h = A.alloc([8, 30], F32); r_gluh = S.res()
    xa_e = [A.alloc([2, 15 + TT], F32) for _ in range(2)]; r_xae = [S.res() for _ in range(2)]
    s_tmp = [A.alloc([2, 15 + TT], F32) for _ in range(2)]; r_stmp = [S.res() for _ in range(2)]
    sga = [A.alloc([2, TT], BF16) for _ in range(2)]; r_sga = [S.res() for _ in range(2)]
    dT = [A.alloc([2, TT], BF16) for _ in range(2)]; r_dT = [S.res() for _ in range(2)]
    glu_e = [A.alloc([30 + TT], F32) for _ in range(4)]; r_glue = [S.res() for _ in range(4)]
    sig = [A.alloc([TT], F32) for _ in range(2)]; r_sig = [S.res() for _ in range(2)]
    cbuf = A.alloc([8, TT], F32); r_c = [S.res() for _ in range(8)]
    cb16 = A.alloc([8, TT], BF16); r_cb16 = S.res()
    csq16 = A.alloc([8, TT], BF16); r_csq16 = S.res()
    sgb = A.alloc([8, TT], BF16); r_sgb = [S.res() for _ in range(8)]
    mixT = A.alloc([16, TT], BF16); r_mix = [S.res() for _ in range(16)]
    lnm = A.alloc([TT], F32); r_lnm = S.res()
    lnr = A.alloc([TT], F32); r_lnr = S.res()
    lnt = [A.alloc([TT], F32) for _ in range(2)]; r_lnt = [S.res() for _ in range(2)]
    print("phase1 arena bytes:", A.off)

    S.op("dve", lambda e: e.memset(xa_h, 0.0), [], [r_xah])
    S.op("dve", lambda e: e.memset(glu_h, 0.0), [], [r_gluh])

    r_x1 = [S.res() for _ in range(NT)]
    pi = [0]

    def nxt_bank():
        b = pi[0] % 4
        pi[0] += 1
        return b

    for t in range(DBG["nt1"]):
        tok0 = t * TT
        S.dma("sp", lambda e, tok0=tok0: e.dma_start(out=xt, in_=xp[tok0:tok0 + TT, :].rearrange("(k p) d -> p k d", p=128)), [], [r_xt])
        rmsnorm_tile(xt, r_xt, h0, r_h0, h0T, r_h0T, NB, TT)

        def inproj(bank, col, half, fchunk):
            for c in range(8):
                S.op("pe", lambda e, c=c: e.matmul(psum[bank][:, half * TT:(half + 1) * TT], lhsT=win[:, c, fchunk * 128:(fchunk + 1) * 128],
                                                   rhs=h0T[:, c, :], start=(c == 0), stop=(c == 7)), [r_win, r_h0T], [rps[bank]])

        if t == 0 and os.environ.get("K_DUMP"):
            outs.append(S.dma("sp", lambda e: e.dma_start(out=T["d_h0T"][:, :], in_=h0T.rearrange("p c t -> p (c t)")), [r_h0T], []))
            outs.append(S.dma("sp", lambda e: e.dma_start(out=T["d_h0"][:, :], in_=h0.rearrange("p c t -> p (c t)")), [r_h0], []))
            outs.append(S.dma("sp", lambda e: e.dma_start(out=T["d_win"][:, :], in_=win[:, 0, 0:1024]), [r_win], []))
        for g in range(4):
            w = POOL_W[g]
            i2 = g % 2
            ba = nxt_bank(); bb = nxt_bank()
            inproj(ba, 0, 0, 2 * g); inproj(ba, 0, 1, 2 * g + 1)
            inproj(bb, 0, 0, 8 + 2 * g); inproj(bb, 0, 1, 8 + 2 * g + 1)
            xe = xa_e[i2]; rxe = r_xae[i2]
            S.op("act", lambda e, bb=bb, i2=i2: e.activation(out=sga[i2], in_=psum[bb][:, :].rearrange("p (a t) -> p a t", a=2), func=AF.Silu),
                 [rps[bb]], [r_sga[i2]])
            S.op("act", lambda e, ba=ba, xe=xe: e.copy(out=xe[:, :, 15:15 + TT], in_=psum[ba][:, :].rearrange("p (a t) -> p a t", a=2)),
                 [rps[ba]], [rxe])
            S.op("act", lambda e, g=g, xe=xe: e.copy(out=xe[:, :, 0:15], in_=xa_h[:, 2 * g:2 * g + 2, :]), [r_xah], [rxe])
            if t == 0 and g == 0 and os.environ.get("K_DUMP"):
                outs.append(S.dma("sp", lambda e, xe=xe: e.dma_start(out=T["d_xae"][:, :], in_=xe.rearrange("p c t -> p (c t)")), [rxe], []))
            src, rsrc = xe, rxe
            L = 15 + TT
            for k in range(g + 1):
                sh = 1 << k
                dst, rdst = s_tmp[k % 2], r_stmp[k % 2]
                S.op("dve", lambda e, src=src, dst=dst, sh=sh: e.tensor_tensor(out=dst[:, :, sh:L], in0=src[:, :, sh:L], in1=src[:, :, 0:L - sh], op=ALU.add),
                     [rsrc], [rdst])
                src, rsrc = dst, rdst
            S.op("dve", lambda e, src=src, xe=xe, i2=i2, w=w: e.scalar_tensor_tensor(out=dT[i2], in0=src[:, :, 15:L], scalar=1.0 / w, in1=xe[:, :, 15:L],
                                                                                   op0=ALU.mult, op1=ALU.subtract), [rsrc, rxe], [r_dT[i2]])
            if t == 0:
                S.op("dve", lambda e, src=src, g=g: e.tensor_tensor(out=src[:, :, 15:30], in0=src[:, :, 15:30],
                                                                     in1=invc[:, g:g + 1, 0:15].broadcast_to([128, 2, 15]), op=ALU.mult), [rsrc, r_invc], [rsrc])
                S.op("dve", lambda e, src=src, xe=xe, i2=i2: e.tensor_tensor(out=dT[i2][:, :, 0:15], in0=src[:, :, 15:30], in1=xe[:, :, 15:30], op=ALU.subtract),
                     [rsrc, rxe], [r_dT[i2]])
            S.op("act", lambda e, g=g, xe=xe: e.copy(out=xa_h[:, 2 * g:2 * g + 2, :], in_=xe[:, :, TT:TT + 15]), [rxe], [r_xah])
            bc = nxt_bank()
            for ec in range(2):
                for cc in range(2):
                    S.op("pe", lambda e, ec=ec, cc=cc, g=g, i2=i2, bc=bc: e.matmul(psum[bc][:, ec * TT:(ec + 1) * TT], lhsT=wpool[:, g, cc, ec * 128:(ec + 1) * 128],
                                                                                  rhs=dT[i2][:, cc, :], start=(cc == 0), stop=(cc == 1)),
                         [r_wpool, r_dT[i2]], [rps[bc]])
            for ec in range(2):
                S.op("dve", lambda e, ec=ec, g=g, i2=i2, bc=bc: e.scalar_tensor_tensor(out=mixT[:, 2 * g + ec, :], in0=psum[bc][:, ec * TT:(ec + 1) * TT],
                                                                                      scalar=pvec[:, 2 * g + ec, 34:35], in1=sga[i2][:, ec, :],
                                                                                      op0=ALU.mult, op1=ALU.mult),
                     [rps[bc], r_pvec, r_sga[i2]], [r_mix[2 * g + ec]])

        for jg in (0, 4):
            for j in range(jg, jg + 4):
                i2 = j % 2
                i4 = j % 4
                bu = nxt_bank(); bg = nxt_bank()
                inproj(bu, 0, 0, 16 + j); inproj(bu, 0, 1, 24 + j)
                inproj(bg, 0, 0, 32 + j)
                ge = glu_e[i4]; rge = r_glue[i4]
                S.op("act", lambda e, bu=bu, i2=i2: e.activation(out=sig[i2], in_=psum[bu][:, TT:2 * TT], func=AF.Sigmoid), [rps[bu]], [r_sig[i2]])
                S.op("dve", lambda e, bu=bu, i2=i2, ge=ge: e.tensor_tensor(out=ge[:, 30:30 + TT], in0=psum[bu][:, 0:TT], in1=sig[i2], op=ALU.mult),
                     [rps[bu], r_sig[i2]], [rge])
                S.op("act", lambda e, j=j, ge=ge: e.copy(out=ge[:, 0:30], in_=glu_h[:, j, :]), [r_gluh], [rge])
                S.op("act", lambda e, bg=bg, j=j: e.activation(out=sgb[:, j, :], in_=psum[bg][:, 0:TT], func=AF.Silu), [rps[bg]], [r_sgb[j]])
            for k in range(31):
                for j in range(jg, jg + 4):
                    ge = glu_e[j % 4]; rge = r_glue[j % 4]
                    if k == 0:
                        S.op("dve", lambda e, j=j, ge=ge: e.tensor_scalar(out=cbuf[:, j, :], in0=ge[:, 0:TT], scalar1=pvec[:, j, 0:1], scalar2=pvec[:, j, 31:32],
                                                                          op0=ALU.mult, op1=ALU.add), [rge, r_pvec], [r_c[j]])
                    else:
                        S.op("dve", lambda e, j=j, ge=ge, k=k: e.scalar_tensor_tensor(out=cbuf[:, j, :], in0=ge[:, k:k + TT], scalar=pvec[:, j, k:k + 1], in1=cbuf[:, j, :],
                                                                                    op0=ALU.mult, op1=ALU.add), [rge, r_pvec, r_c[j]], [r_c[j]])
            for j in range(jg, jg + 4):
                ge = glu_e[j % 4]; rge = r_glue[j % 4]
                S.op("act", lambda e, j=j, ge=ge: e.copy(out=glu_h[:, j, :], in_=ge[:, TT:TT + 30]), [rge], [r_gluh])
                S.op("act", lambda e, j=j: e.activation(out=csq16[:, j, :], in_=cbuf[:, j, :], func=AF.Square), [r_c[j]], [r_csq16])
                S.op("act", lambda e, j=j: e.copy(out=cb16[:, j, :], in_=cbuf[:, j, :]), [r_c[j]], [r_cb16])
        bs = 4
        for j in range(8):
            S.op("pe", lambda e, j=j: e.matmul(psum[bs][:, 0:TT], lhsT=ones_b, rhs=cb16[:, j, :], start=(j == 0), stop=(j == 7)), [r_ones, r_cb16], [rps[bs]])
        for j in range(8):
            S.op("pe", lambda e, j=j: e.matmul(psum[5][:, 0:TT], lhsT=ones_b, rhs=csq16[:, j, :], start=(j == 0), stop=(j == 7)), [r_ones, r_csq16], [rps[5]])
        S.op("dve", lambda e: e.tensor_scalar(out=lnm, in0=psum[bs][:, 0:TT], scalar1=1.0 / D, scalar2=None, op0=ALU.mult), [rps[bs]], [r_lnm])
        S.op("dve", lambda e: e.tensor_tensor(out=lnr, in0=lnm, in1=lnm, op=ALU.mult), [r_lnm], [r_lnr])
        S.op("dve", lambda e: e.scalar_tensor_tensor(out=lnr, in0=psum[5][:, 0:TT], scalar=1.0 / D, in1=lnr, op0=ALU.mult, op1=ALU.subtract),
             [rps[5], r_lnr], [r_lnr])
        S.op("dve", lambda e: e.tensor_scalar(out=lnr, in0=lnr, scalar1=EPS, scalar2=None, op0=ALU.add), [r_lnr], [r_lnr])
        S.op("act", lambda e: e.activation(out=lnr, in_=lnr, func=AF.Sqrt), [r_lnr], [r_lnr])
        S.op("dve", lambda e: e.reciprocal(out=lnr, in_=lnr), [r_lnr], [r_lnr])
        for j in range(8):
            i2 = j % 2
            S.op("dve", lambda e, j=j, i2=i2: e.tensor_tensor(out=lnt[i2], in0=cbuf[:, j, :], in1=lnm, op=ALU.subtract), [r_c[j], r_lnm], [r_lnt[i2]])
            S.op("dve", lambda e, i2=i2: e.tensor_tensor(out=lnt[i2], in0=lnt[i2], in1=lnr, op=ALU.mult), [r_lnt[i2], r_lnr], [r_lnt[i2]])
            S.op("act", lambda e, j=j, i2=i2: e.activation(out=lnt[i2], in_=lnt[i2], func=AF.Silu, scale=pvec[:, j, 32:33], bias=pvec[:, j, 33:34]),
                 [r_lnt[i2], r_pvec], [r_lnt[i2]])
            S.op("dve", lambda e, j=j, i2=i2: e.tensor_tensor(out=mixT[:, 8 + j, :], in0=lnt[i2], in1=sgb[:, j, :], op=ALU.mult),
                 [r_lnt[i2], r_sgb[j]], [r_mix[8 + j]])
        for blk in range(NB):
            for half in range(2):
                bo = 5 + (blk * 2 + half) % 2
                for kc in range(16):
                    S.op("pe", lambda e, blk=blk, half=half, kc=kc, bo=bo: e.matmul(psum[bo][:, :], lhsT=mixT[:, kc, blk * 128:(blk + 1) * 128],
                                                                                  rhs=wout[:, kc, half * 512:(half + 1) * 512], start=(kc == 0), stop=(kc == 15)),
                         [r_mix[kc], r_wout], [rps[bo]])
                S.op("dve", lambda e, blk=blk, half=half, bo=bo: e.tensor_tensor(out=xt[:, blk, half * 512:(half + 1) * 512], in0=psum[bo][:, :],
                                                                               in1=xt[:, blk, half * 512:(half + 1) * 512], op=ALU.add), [rps[bo], r_xt], [r_xt])
        S.dma("sp", lambda e, tok0=tok0: e.dma_start(out=T["x1_scr"][tok0:tok0 + TT, :].rearrange("(k p) d -> p k d", p=128), in_=xt), [r_xt], [r_x1[t]])

    stt = cbuf.rearrange("p c t -> p (c t)")[:, 0:D]
    for c in range(4):
        S.op("pe", lambda e, c=c: e.transpose(out=psum[0][0:15, c * 128:(c + 1) * 128], in_=xa_h[:, c, :], identity=ident_f), [r_xah, r_identf], [rps[0]])
    for c in range(4):
        S.op("pe", lambda e, c=c: e.transpose(out=psum[1][0:30, c * 128:(c + 1) * 128], in_=glu_h[:, c, :], identity=ident_f), [r_gluh, r_identf], [rps[1]])
        S.op("pe", lambda e, c=c: e.transpose(out=psum[2][0:30, c * 128:(c + 1) * 128], in_=glu_h[:, 4 + c, :], identity=ident_f), [r_gluh, r_identf], [rps[2]])
    S.op("dve", lambda e: e.tensor_copy(out=stt[0:15, 0:512], in_=psum[0][0:15, 0:512]), [rps[0]], r_c[0:4])
    outs.append(S.dma("sp", lambda e: e.dma_start(out=T["pool_st"][:, 0:512], in_=stt[0:15, 0:512]), r_c[0:4], []))
    for c in range(4):
        S.op("pe", lambda e, c=c: e.transpose(out=psum[3][0:15, c * 128:(c + 1) * 128], in_=xa_h[:, 4 + c, :], identity=ident_f), [r_xah, r_identf], [rps[3]])
    S.op("dve", lambda e: e.tensor_copy(out=stt[0:15, 512:1024], in_=psum[3][0:15, 0:512]), [rps[3]], r_c[0:4])
    outs.append(S.dma("sp", lambda e: e.dma_start(out=T["pool_st"][:, 512:1024], in_=stt[0:15, 512:1024]), r_c[0:4], []))
    stc = stt
    S.op("dve", lambda e: e.tensor_copy(out=stc[0:30, 0:512], in_=psum[1][0:30, 0:512]), [rps[1]], r_c[0:4])
    S.op("dve", lambda e: e.tensor_copy(out=stc[0:30, 512:1024], in_=psum[2][0:30, 0:512]), [rps[2]], r_c[0:4])
    outs.append(S.dma("sp", lambda e: e.dma_start(out=T["conv_st"][:, :], in_=stc[0:30, :]), r_c[0:4], []))

    S.barrier()
    A.reset(mP)
    S.dma("sp", lambda e: e.dma_start(out=nw, in_=T["norm_w_c"][0:1, :].partition_broadcast(128)), [], [r_nw])
    wc = A.alloc([8, 4 * D], BF16); r_wc = S.res()
    woc = A.alloc([8, D], BF16); r_woc = S.res()
    cast_load(wc, r_wc, T["w_in_c"], 8, 4 * D)
    cast_load(woc, r_woc, T["w_out_c"], 8, D)
    xt2 = A.alloc([NB, D], F32); r_xt2 = S.res()
    h1 = A.alloc([NB, D], BF16); r_h1 = S.res()
    h1T = A.alloc([8, TT], BF16); r_h1T = S.res()
    stg = [A.alloc([D], F32) for _ in range(2)]; r_stg = [S.res() for _ in range(2)]
    sq = A.alloc([D], F32); r_sq = S.res()
    nb16 = A.alloc([D], BF16); r_nb16 = S.res()
    ktile = A.alloc([8, TT], BF16); r_ktile = S.res()
    vtile = A.alloc([8, NB, 129], BF16); r_vtile = S.res()
    QT = A.alloc([8, TT], BF16); r_QT = S.res()
    sg = A.alloc([NB, D], BF16); r_sg = S.res()
    KCH = 4 * TT
    kth = [A.alloc([KCH], BF16) for _ in range(3)]; r_kth = [S.res() for _ in range(3)]
    vh = [A.alloc([KCH // 128, 129], BF16) for _ in range(3)]; r_vh = [S.res() for _ in range(3)]
    PT = [A.alloc([2, TT], BF16) for _ in range(3)]; r_PT = [S.res() for _ in range(3)]
    og = A.alloc([NB, D], BF16); r_og = S.res()
    ogT = A.alloc([8, TT], BF16); r_ogT = S.res()
    o1 = A.alloc([128], F32); r_o1 = S.res()
    o2 = A.alloc([128], F32); r_o2 = S.res()
    sm2 = A.alloc([16], F32); r_sm2 = S.res()
    rs16 = A.alloc([16], F32); r_rs16 = S.res()
    print("phase2 arena bytes:", A.off)
    S.op("dve", lambda e: e.memset(vtile, 1.0), [], [r_vtile])

    r_kscr = [S.res() for _ in range(NT)]
    r_vscr = [S.res() for _ in range(NT)]
    nload = [0]
    npt = [0]
    pj = [0]
    ownr = A.alloc([16], I32); r_ownr = S.res()
    amask = A.alloc([2, 2, TT], BF16); r_amask = S.res()
    S.dma("sp", lambda e: e.dma_start(out=ownr, in_=T["own_rows"][:, :]), [], [r_ownr])
    S.dma("pool", lambda e: e.dma_start(out=amask, in_=T["amask"][:, :].rearrange("p (a b q) -> p a b q", a=2, b=2)), [], [r_amask])
    xq = A.alloc([NB, D], F32); r_xq = S.res()
    print("phase2 arena bytes (final):", A.off)

    def proj_bank():
        b = 6 + pj[0] % 2
        pj[0] += 1
        return b

    def head_norm(src_stage, r_src, wcol0, np_=128):
        src_ = src_stage[0:np_]
        S.op("pool", lambda e: e.tensor_tensor(out=sq[0:np_], in0=src_, in1=src_, op=ALU.mult), [r_src], [r_sq])
        S.op("dve", lambda e: e.tensor_reduce(out=rs16[0:np_], in_=sq[0:np_].rearrange("p (g d) -> p g d", d=64), axis=AX.X, op=ALU.add), [r_sq], [r_rs16])
        rms_rstd(rs16[0:np_], 64, [r_rs16])
        v3 = src_.rearrange("p (g d) -> p g d", d=64)
        S.op("dve", lambda e: e.tensor_tensor(out=v3, in0=v3, in1=rs16[0:np_].unsqueeze(2).broadcast_to([np_, 16, 64]), op=ALU.mult), [r_src, r_rs16], [r_src])
        S.op("pool", lambda e: e.tensor_tensor(out=v3, in0=v3, in1=qkw[0:np_, wcol0:wcol0 + 64].unsqueeze(1).broadcast_to([np_, 16, 64]), op=ALU.mult),
             [r_src, r_qkw], [r_src])

    def proj(hT, r_hT, blk, col0, half):
        b = proj_bank()
        for c in range(8):
            S.op("pe", lambda e, c=c, b=b: e.matmul(psum[b][:, :], lhsT=hT[:, c, blk * 128:(blk + 1) * 128],
                                                    rhs=wc[:, c, col0 + half * 512: col0 + (half + 1) * 512], start=(c == 0), stop=(c == 7)),
                 [r_hT, r_wc], [rps[b]])
        return b

    def to_featmajor(dst, r_dst, blk):
        pb = ps_bf(7)
        for h in range(8):
            S.op("pe", lambda e, h=h, pb=pb: e.transpose(out=pb[:, h * 128:(h + 1) * 128], in_=nb16[:, h * 128:(h + 1) * 128], identity=ident_b),
                 [r_nb16, r_identb], [rps[7]])
        S.op("dve", lambda e, pb=pb, blk=blk: e.tensor_copy(out=dst[:, :, blk * 128:(blk + 1) * 128], in_=pb.rearrange("p (h t) -> p h t", h=8)),
             [rps[7]], [r_dst])

    for t in range(DBG["nt2"]):
        tok0 = t * TT
        S.dma("sp", lambda e, tok0=tok0: e.dma_start(out=xt2, in_=T["x1_scr"][tok0:tok0 + TT, :].rearrange("(k p) d -> p k d", p=128)), [r_x1[t]], [r_xt2])
        rmsnorm_tile(xt2, r_xt2, h1, r_h1, h1T, r_h1T, NB, TT)
        for blk in range(NB):
            row = tok0 + blk * 128
            ks, rks = stg[0], r_stg[0]
            for half in range(2):
                b = proj(h1T, r_h1T, blk, D, half)
                S.op("act", lambda e, b=b, half=half: e.copy(out=ks[:, half * 512:(half + 1) * 512], in_=psum[b][:, :]), [rps[b]], [rks])
            head_norm(ks, rks, 64)
            outs.append(S.dma("sp", lambda e, row=row: e.dma_start(out=T["k_all"][row:row + 128, :], in_=ks), [rks], []))
            S.op("act", lambda e: e.copy(out=nb16, in_=ks), [rks], [r_nb16])
            to_featmajor(ktile, r_ktile, blk)
            vs, rvs = stg[1], r_stg[1]
            for half in range(2):
                b = proj(h1T, r_h1T, blk, 2 * D, half)
                S.op("act", lambda e, b=b, half=half: e.copy(out=vs[:, half * 512:(half + 1) * 512], in_=psum[b][:, :]), [rps[b]], [rvs])
                S.op("dve", lambda e, b=b, half=half, blk=blk: e.tensor_copy(out=vtile[:, half * 4:(half + 1) * 4, blk, 0:128],
                                                                          in_=psum[b][:, :].rearrange("p (h d) -> p h d", h=4)), [rps[b]], [r_vtile])
            outs.append(S.dma("sp", lambda e, row=row: e.dma_start(out=T["v_all"][row:row + 128, :], in_=vs), [rvs], []))
        S.dma("sp", lambda e, tok0=tok0: e.dma_start(out=T["kt_scr"][:, :, tok0:tok0 + TT].rearrange("h p t -> p h t"), in_=ktile), [r_ktile], [r_kscr[t]])
        S.dma("sp", lambda e, t=t: e.dma_start(out=T["v_scr"][:, :, t * NB:(t + 1) * NB, :].rearrange("h p k d -> p h (k d)"), in_=vtile.rearrange("p h k d -> p h (k d)")), [r_vtile], [r_vscr[t]])
        if t % 2 == 0 or not DBG["attn"]:
            continue
        s_ = t // 2
        for blk in range(NB):
            S.dma("pool", lambda e, blk=blk, s_=s_: e.indirect_dma_start(out=xq[:, blk, :], out_offset=None, in_=T["x1_scr"][:, :],
                                                                     in_offset=bass.IndirectOffsetOnAxis(ap=ownr[:, s_ * 2 + blk:s_ * 2 + blk + 1], axis=0)),
                  [r_ownr, r_x1[t - 1], r_x1[t]], [r_xq])
        rmsnorm_tile(xq, r_xq, h1, r_h1, h1T, r_h1T, NB, TT)
        for blk in range(NB):
            qs, rqs = stg[0], r_stg[0]
            for half in range(2):
                b = proj(h1T, r_h1T, blk, 0, half)
                S.op("act", lambda e, b=b, half=half: e.copy(out=qs[:, half * 512:(half + 1) * 512], in_=psum[b][:, :]), [rps[b]], [rqs])
            head_norm(qs, rqs, 0)
            S.op("act", lambda e: e.copy(out=nb16, in_=qs), [rqs], [r_nb16])
            to_featmajor(QT, r_QT, blk)
            for half in range(2):
                b = proj(h1T, r_h1T, blk, 3 * D, half)
                S.op("act", lambda e, b=b, half=half, blk=blk: e.activation(out=sg[:, blk, half * 512:(half + 1) * 512], in_=psum[b][:, :], func=AF.Silu),
                     [rps[b]], [r_sg])
        nkb = (t + 1) * NB
        for h in range(8):
            ob = [0, 1]
            for qb in range(NB):
                S.op("dve", lambda e, b=ob[qb]: e.memset(psum[b][:, 0:258], 0.0), [], [rps[ob[qb]]])
            for ch0 in range(0, nkb, KCH // 128):
                nb_ch = min(KCH // 128, nkb - ch0)
                li = nload[0] % 3
                nload[0] += 1
                tiles_in = sorted(set((ch0 + i) // NB for i in range(nb_ch)))
                S.dma("sp", lambda e, h=h, li=li, ch0=ch0, nb_ch=nb_ch: e.dma_start(out=kth[li][:, 0:nb_ch * 128],
                                                                                   in_=T["kt_scr"][h, :, ch0 * 128:(ch0 + nb_ch) * 128]),
                      [r_kscr[x] for x in tiles_in], [r_kth[li]])
                S.dma("sp", lambda e, h=h, li=li, ch0=ch0, nb_ch=nb_ch: e.dma_start(out=vh[li][:, 0:nb_ch, :],
                                                                                   in_=T["v_scr"][h, :, ch0:ch0 + nb_ch, :]),
                      [r_vscr[x] for x in tiles_in], [r_vh[li]])
                for i in range(nb_ch):
                    kb = ch0 + i
                    jd = kb - (nkb - 4)
                    sb_ = 2 + 2 * (kb % 2)
                    pi_ = npt[0] % 3
                    npt[0] += 1
                    for c in range(2):
                        S.op("pe", lambda e, c=c, li=li, i=i, h=h, sb_=sb_: e.matmul(
                            psum[sb_ + c][:, 0:TT], lhsT=kth[li][c * 64:(c + 1) * 64, i * 128:(i + 1) * 128],
                            rhs=QT[c * 64:(c + 1) * 64, h, :], start=True, stop=True), [r_kth[li], r_QT], [rps[sb_ + c]])
                    pt = PT[pi_]; rpt = r_PT[pi_]
                    S.op("act", lambda e, pt=pt, sb_=sb_: e.activation(out=pt, in_=psum_all[:, sb_ * 512:(sb_ + 2) * 512].rearrange("p (c x) -> p c x", c=2)[:, :, 0:TT],
                                                                     func=AF.Exp, scale=0.125, bias=lam[:, 1:2]), [rps[sb_], rps[sb_ + 1], r_lam], [rpt])
                    if jd >= 0:
                        S.op("dve", lambda e, pt=pt, jd=jd: e.tensor_tensor(out=pt, in0=pt, in1=amask[:, jd // 2, jd % 2, :].unsqueeze(1).broadcast_to([128, 2, TT]),
                                                                           op=ALU.mult), [rpt, r_amask], [rpt])
                    for qb in range(NB):
                        for c in range(2):
                            S.op("pe", lambda e, c=c, qb=qb, pt=pt, li=li, i=i, b=ob[qb]: e.matmul(
                                psum[b][:, c * 129:(c + 1) * 129], lhsT=pt[:, c, qb * 128:(qb + 1) * 128], rhs=vh[li][:, i, :],
                                start=False, stop=False, skip_group_check=True), [rpt, r_vh[li]], [rps[b]])
            for qb in range(NB):
                b = ob[qb]
                O = psum[b][:, 0:258].rearrange("p (c d) -> p c d", c=2)
                S.op("dve", lambda e, O=O: e.reciprocal(out=sm2[:, 0:2], in_=O[:, :, 128]), [rps[b]], [r_sm2])
                S.op("dve", lambda e: e.tensor_tensor(out=sm2[:, 1:2], in0=sm2[:, 1:2], in1=lam[:, 0:1], op=ALU.mult), [r_sm2, r_lam], [r_sm2])
                S.op("act", lambda e, O=O: e.activation(out=o1, in_=O[:, 0, 0:128], func=AF.Copy, scale=sm2[:, 0:1]), [rps[b], r_sm2], [r_o1])
                S.op("dve", lambda e, O=O: e.scalar_tensor_tensor(out=o2, in0=O[:, 1, 0:128], scalar=sm2[:, 1:2], in1=o1, op0=ALU.mult, op1=ALU.add),
                     [rps[b], r_sm2, r_o1], [r_o2])
                S.op("act", lambda e: e.activation(out=junk[:, 0:128], in_=o2, func=AF.Square, accum_out=sm2[:, 2:3]), [r_o2], [r_junk, r_sm2])
                rms_rstd(sm2[:, 2:3], 128, [r_sm2])
                S.op("dve", lambda e: e.scalar_tensor_tensor(out=o2, in0=o2, scalar=sm2[:, 2:3], in1=sublnw, op0=ALU.mult, op1=ALU.mult),
                     [r_o2, r_sm2, r_subw], [r_o2])
                S.op("dve", lambda e, qb=qb, h=h: e.tensor_tensor(out=og[:, qb, h * 128:(h + 1) * 128], in0=o2, in1=sg[:, qb, h * 128:(h + 1) * 128], op=ALU.mult),
                     [r_o2, r_sg], [r_og])
        for blk in range(NB):
            pb = ps_bf(7)
            for h in range(8):
                S.op("pe", lambda e, h=h, pb=pb, blk=blk: e.transpose(out=pb[:, h * 128:(h + 1) * 128], in_=og[:, blk, h * 128:(h + 1) * 128], identity=ident_b),
                     [r_og, r_identb], [rps[7]])
            S.op("act", lambda e, pb=pb, blk=blk: e.copy(out=ogT[:, :, blk * 128:(blk + 1) * 128], in_=pb.rearrange("p (h t) -> p h t", h=8)),
                 [rps[7]], [r_ogT])
        for blk in range(NB):
            for half in range(2):
                b = proj_bank()
                for kc in range(8):
                    S.op("pe", lambda e, blk=blk, half=half, kc=kc, b=b: e.matmul(psum[b][:, :], lhsT=ogT[:, kc, blk * 128:(blk + 1) * 128],
                                                                                rhs=woc[:, kc, half * 512:(half + 1) * 512], start=(kc == 0), stop=(kc == 7)),
                         [r_ogT, r_woc], [rps[b]])
                S.op("dve", lambda e, blk=blk, half=half, b=b: e.tensor_tensor(out=xq[:, blk, half * 512:(half + 1) * 512], in0=psum[b][:, :],
                                                                             in1=xq[:, blk, half * 512:(half + 1) * 512], op=ALU.add), [rps[b], r_xq], [r_xq])
        outs.append(S.dma("sp", lambda e, s_=s_: e.dma_start(out=T["y_own"][s_ * TT:(s_ + 1) * TT, :].rearrange("(k p) d -> p k d", p=128), in_=xq), [r_xq], []))

    def sample_l1():
        NS = 16
        ktn = A.alloc([8, NS], BF16); r_ktn = S.res()
        QTs = A.alloc([8, NS], BF16); r_QTs = S.res()
        vnew = A.alloc([8, 129], BF16); r_vnew = S.res()
        kpg = [A.alloc([D], BF16) for _ in range(2)]; r_kpg = [S.res() for _ in range(2)]
        vpg = [A.alloc([8, 129], BF16) for _ in range(2)]; r_vpg = [S.res() for _ in range(2)]
        ktp = [A.alloc([8, 128], BF16) for _ in range(2)]; r_ktp = [S.res() for _ in range(2)]
        vfl = [A.alloc([D], BF16) for _ in range(2)]; r_vfl = [S.res() for _ in range(2)]
        pte = [[A.alloc([8, 2, NS], BF16) for _ in range(2)] for _ in range(4)]
        r_pte = [[S.res() for _ in range(2)] for _ in range(4)]
        ptn = A.alloc([8, 2, NS], BF16); r_ptn = S.res()
        pti = A.alloc([256], I32); r_pti = S.res()
        ptf = A.alloc([256], F32); r_ptf = S.res()
        idx = A.alloc([256], I32); r_idx = S.res()
        smf = A.alloc([NS], F32); r_smf = S.res()
        smb = A.alloc([NS], BF16); r_smb = S.res()
        rr = A.alloc([16], F32); r_rr = S.res()
        print("sample-L1 arena bytes:", A.off)
        O1 = xq.rearrange("p a d -> p (a d)")[:, 0:1032]; r_O1 = r_xq
        O2 = xt2.rearrange("p a d -> p (a d)")[:, 0:1032]; r_O2 = r_xt2

        S.op("pool", lambda e: e.memset(vnew, 1.0), [], [r_vnew])
        for i in range(2):
            S.op("pool", lambda e, i=i: e.memset(vpg[i], 1.0), [], [r_vpg[i]])
        for e_ in range(4):
            for i in range(2):
                S.op("pool", lambda e, e_=e_, i=i: e.memset(pte[e_][i], 0.0), [], [r_pte[e_][i]])
        S.dma("sp", lambda e: e.dma_start(out=pti, in_=T["pt"][0:1, :].partition_broadcast(128)), [], [r_pti])
        S.op("dve", lambda e: e.tensor_copy(out=ptf, in_=pti), [r_pti], [r_ptf])
        S.op("dve", lambda e: e.scalar_tensor_tensor(out=ptf, in0=ptf, scalar=128.0, in1=iot[:, 0:1].broadcast_to([128, 256]), op0=ALU.mult, op1=ALU.subtract),
             [r_ptf, r_iot], [r_ptf])
        S.op("dve", lambda e: e.tensor_copy(out=idx, in_=ptf), [r_ptf], [r_idx])
        S.dma("sp", lambda e: e.dma_start(out=smf[0:NS, :], in_=T["smask"][:, :]), [], [r_smf])
        S.op("dve", lambda e: e.tensor_copy(out=smb[0:NS, :], in_=smf[0:NS, :]), [r_smf], [r_smb])

        ss = small[0:NS, 9:10]
        S.op("act", lambda e: e.activation(out=junk[0:NS, :], in_=x1s[0:NS, :], func=AF.Square, accum_out=ss), [r_x1s], [r_junk, r_small])
        rms_rstd(ss, D, [r_small])
        S.op("dve", lambda e: e.scalar_tensor_tensor(out=h1[0:NS, 0, :], in0=x1s[0:NS, :], scalar=ss, in1=nw[0:NS, :], op0=ALU.mult, op1=ALU.mult),
             [r_x1s, r_small, r_nw], [r_h1])
        pb = ps_bf(7)
        for c in range(8):
            S.op("pe", lambda e, c=c: e.transpose(out=pb[:, c * NS:(c + 1) * NS], in_=h1[0:NS, 0, c * 128:(c + 1) * 128], identity=ident_b[0:NS, 0:NS]),
                 [r_h1, r_identb], [rps[7]])
        S.op("act", lambda e: e.copy(out=h1T[:, :, 0:NS], in_=pb[:, 0:8 * NS].rearrange("p (c t) -> p c t", c=8)), [rps[7]], [r_h1T])

        def sproj(col0, half):
            b = proj_bank()
            for c in range(8):
                S.op("pe", lambda e, c=c, b=b: e.matmul(psum[b][0:NS, :], lhsT=h1T[:, c, 0:NS], rhs=wc[:, c, col0 + half * 512: col0 + (half + 1) * 512],
                                                        start=(c == 0), stop=(c == 7)), [r_h1T, r_wc], [rps[b]])
            return b

        def heads_T(dst, r_dst):
            pb_ = ps_bf(7)
            for h in range(8):
                S.op("pe", lambda e, h=h: e.transpose(out=pb_[:, h * NS:(h + 1) * NS], in_=nb16[0:NS, h * 128:(h + 1) * 128], identity=ident_b[0:NS, 0:NS]),
                     [r_nb16, r_identb], [rps[7]])
            S.op("dve", lambda e: e.tensor_copy(out=dst, in_=pb_[:, 0:8 * NS].rearrange("p (h t) -> p h t", h=8)), [rps[7]], [r_dst])

        ks, rks = stg[0], r_stg[0]
        vs, rvs = stg[1], r_stg[1]
        for half in range(2):
            b = sproj(D, half)
            S.op("act", lambda e, b=b, half=half: e.copy(out=ks[0:NS, half * 512:(half + 1) * 512], in_=psum[b][0:NS, :]), [rps[b]], [rks])
        head_norm(ks, rks, 64, NS)
        outs.append(S.dma("sp", lambda e: e.dma_start(out=T["ks_out"][:, :], in_=ks[0:NS, :]), [rks], []))
        S.op("act", lambda e: e.copy(out=nb16[0:NS, :], in_=ks[0:NS, :]), [rks], [r_nb16])
        heads_T(ktn, r_ktn)
        for half in range(2):
            b = sproj(2 * D, half)
            S.op("act", lambda e, b=b, half=half: e.copy(out=vs[0:NS, half * 512:(half + 1) * 512], in_=psum[b][0:NS, :]), [rps[b]], [rvs])
        S.op("dve", lambda e: e.tensor_copy(out=vnew[0:NS, :, 0:128], in_=vs[0:NS, :].rearrange("p (h d) -> p h d", h=8)), [rvs], [r_vnew])
        outs.append(S.dma("sp", lambda e: e.dma_start(out=T["vs_out"][:, :], in_=vs[0:NS, :]), [rvs], []))
        for half in range(2):
            b = sproj(0, half)
            S.op("act", lambda e, b=b, half=half: e.copy(out=ks[0:NS, half * 512:(half + 1) * 512], in_=psum[b][0:NS, :]), [rps[b]], [rks])
        head_norm(ks, rks, 0, NS)
        S.op("act", lambda e: e.copy(out=nb16[0:NS, :], in_=ks[0:NS, :]), [rks], [r_nb16])
        heads_T(QTs, r_QTs)
        for half in range(2):
            b = sproj(3 * D, half)
            S.op("act", lambda e, b=b, half=half: e.activation(out=sg[0:NS, 0, half * 512:(half + 1) * 512], in_=psum[b][0:NS, :], func=AF.Silu), [rps[b]], [r_sg])

        for b in range(3):
            S.op("dve", lambda e, b=b: e.memset(psum[b][0:32, :], 0.0), [], [rps[b]])

        def pv(P2d_of_h, rP, vt, r_vt, np_):
            for h in range(8):
                S.op("pe", lambda e, h=h: e.matmul(psum[h // 3][0:32, (h % 3) * 129:(h % 3 + 1) * 129], lhsT=P2d_of_h(h), rhs=vt[0:np_, h, :],
                                                   start=False, stop=False, skip_group_check=True), [rP, r_vt], [rps[h // 3]])

        npages = DBG.get("npg", NPAGE)
        ck = T["cache_k"]
        cv3 = T["cache_v"].rearrange("r (h d) -> r h d", h=8)
        for e_ in range(4):
            for j in range(npages):
                gi = e_ * NPAGE + j
                bi = gi % 2
                S.dma("pool", lambda e, gi=gi, bi=bi: e.indirect_dma_start(out=kpg[bi], out_offset=None, in_=ck[:, :],
                                                                        in_offset=bass.IndirectOffsetOnAxis(ap=idx[:, gi:gi + 1], axis=0)), [r_idx], [r_kpg[bi]])
                S.dma("pool", lambda e, gi=gi, bi=bi: e.indirect_dma_start(out=vfl[bi], out_offset=None, in_=T["cache_v"][:, :],
                                                                        in_offset=bass.IndirectOffsetOnAxis(ap=idx[:, gi:gi + 1], axis=0)), [r_idx], [r_vfl[bi]])
                S.op("act" if gi % 2 else "dve",
                     (lambda e, bi=bi: e.copy(out=vpg[bi][:, :, 0:128], in_=vfl[bi].rearrange("p (h d) -> p h d", h=8))) if gi % 2 else
                     (lambda e, bi=bi: e.tensor_copy(out=vpg[bi][:, :, 0:128], in_=vfl[bi].rearrange("p (h d) -> p h d", h=8))), [r_vfl[bi]], [r_vpg[bi]])
                pb7 = ps_bf(7)
                for h in range(8):
                    S.op("pe", lambda e, h=h, bi=bi: e.transpose(out=pb7[:, h * 128:(h + 1) * 128], in_=kpg[bi][:, h * 128:(h + 1) * 128], identity=ident_b),
                         [r_kpg[bi], r_identb], [rps[7]])
                S.op("dve" if gi % 2 else "act",
                     (lambda e, bi=bi: e.tensor_copy(out=ktp[bi], in_=pb7.rearrange("p (h t) -> p h t", h=8))) if gi % 2 else
                     (lambda e, bi=bi: e.copy(out=ktp[bi], in_=pb7.rearrange("p (h t) -> p h t", h=8))), [rps[7]], [r_ktp[bi]])
                sbk = 3 + 2 * (gi % 2)
                for h in range(8):
                    for c in range(2):
                        S.op("pe", lambda e, h=h, c=c, bi=bi, sbk=sbk, e_=e_: e.matmul(psum[sbk + c][:, h * 4:(h + 1) * 4], lhsT=ktp[bi][c * 64:(c + 1) * 64, h, :],
                                                                                    rhs=QTs[c * 64:(c + 1) * 64, h, e_ * 4:(e_ + 1) * 4], start=True, stop=True),
                             [r_ktp[bi], r_QTs], [rps[sbk + c]])
                P = pte[e_][bi]; rP = r_pte[e_][bi]
                for c in range(2):
                    S.op("act", lambda e, c=c, P=P, sbk=sbk, e_=e_: e.activation(out=P[:, :, c, e_ * 4:(e_ + 1) * 4], in_=psum[sbk + c][:, 0:32].rearrange("p (h q) -> p h q", h=8),
                                                                             func=AF.Exp, scale=0.125, bias=lam[:, 1:2]), [rps[sbk + c], r_lam], [rP])
                pv(lambda h, P=P: P[:, h].rearrange("p c q -> p (c q)"), rP, vpg[bi], r_vpg[bi], 128)
        for h in range(8):
            for c in range(2):
                S.op("pe", lambda e, h=h, c=c: e.matmul(psum[3 + c][0:NS, h * NS:(h + 1) * NS], lhsT=ktn[c * 64:(c + 1) * 64, h, :], rhs=QTs[c * 64:(c + 1) * 64, h, :],
                                                        start=True, stop=True), [r_ktn, r_QTs], [rps[3 + c]])
        for c in range(2):
            S.op("act", lambda e, c=c: e.activation(out=ptn[0:NS, :, c, :], in_=psum[3 + c][0:NS, 0:8 * NS].rearrange("p (h q) -> p h q", h=8),
                                                    func=AF.Exp, scale=0.125, bias=lam[0:NS, 1:2]), [rps[3 + c], r_lam], [r_ptn])
            S.op("dve", lambda e, c=c: e.tensor_tensor(out=ptn[0:NS, :, c, :], in0=ptn[0:NS, :, c, :], in1=smb[0:NS, :].unsqueeze(1).broadcast_to([NS, 8, NS]), op=ALU.mult),
                 [r_ptn, r_smb], [r_ptn])
        pv(lambda h: ptn[0:NS, h].rearrange("p c q -> p (c q)"), r_ptn, vnew, r_vnew, NS)

        for b in range(3):
            n = 387 if b < 2 else 258
            S.op("act" if b % 2 else "dve",
                 (lambda e, b=b, n=n: e.copy(out=O1[0:32, b * 387:b * 387 + n], in_=psum[b][0:32, 0:n])) if b % 2 else
                 (lambda e, b=b, n=n: e.tensor_copy(out=O1[0:32, b * 387:b * 387 + n], in_=psum[b][0:32, 0:n])), [rps[b]], [r_O1])
        for i, (c0, n) in enumerate(((0, 512), (512, 512), (1024, 8))):
            S.op("pe", lambda e, i=i, c0=c0, n=n: e.matmul(psum[3 + i][0:NS, 0:n], lhsT=ident_f[0:32, 16:32], rhs=O1[0:32, c0:c0 + n], start=True, stop=True),
                 [r_O1, r_identf], [rps[3 + i]])
            S.op("dve", lambda e, i=i, c0=c0, n=n: e.tensor_copy(out=O2[0:NS, c0:c0 + n], in_=psum[3 + i][0:NS, 0:n]), [rps[3 + i]], [r_O2])
        O1v = O1[0:NS, :].rearrange("p (h d) -> p h d", h=8)
        O2v = O2[0:NS, :].rearrange("p (h d) -> p h d", h=8)
        S.op("dve", lambda e: e.reciprocal(out=rr[0:NS, 0:8], in_=O1v[:, :, 128]), [r_O1], [r_rr])
        S.op("dve", lambda e: e.reciprocal(out=rr[0:NS, 8:16], in_=O2v[:, :, 128]), [r_O2], [r_rr])
        S.op("dve", lambda e: e.tensor_scalar(out=rr[0:NS, 8:16], in0=rr[0:NS, 8:16], scalar1=lam[0:NS, 0:1], scalar2=None, op0=ALU.mult), [r_rr, r_lam], [r_rr])
        o_a = ks[0:NS, :].rearrange("p (h d) -> p h d", h=8)
        o_b = vs[0:NS, :].rearrange("p (h d) -> p h d", h=8)
        S.op("dve", lambda e: e.tensor_tensor(out=o_a, in0=O1v[:, :, 0:128], in1=rr[0:NS, 0:8].unsqueeze(2).broadcast_to([NS, 8, 128]), op=ALU.mult), [r_O1, r_rr], [rks])
        S.op("dve", lambda e: e.tensor_tensor(out=o_b, in0=O2v[:, :, 0:128], in1=rr[0:NS, 8:16].unsqueeze(2).broadcast_to([NS, 8, 128]), op=ALU.mult), [r_O2, r_rr], [rvs])
        S.op("dve", lambda e: e.tensor_tensor(out=o_a, in0=o_a, in1=o_b, op=ALU.add), [rks, rvs], [rks])
        S.op("pool", lambda e: e.tensor_tensor(out=sq[0:NS], in0=ks[0:NS], in1=ks[0:NS], op=ALU.mult), [rks], [r_sq])
        S.op("dve", lambda e: e.tensor_reduce(out=rs16[0:NS, 0:8], in_=sq[0:NS].rearrange("p (h d) -> p h d", h=8), axis=AX.X, op=ALU.add), [r_sq], [r_rs16])
        rms_rstd(rs16[0:NS, 0:8], 128, [r_rs16])
        S.op("dve", lambda e: e.tensor_tensor(out=o_a, in0=o_a, in1=rs16[0:NS, 0:8].unsqueeze(2).broadcast_to([NS, 8, 128]), op=ALU.mult), [rks, r_rs16], [rks])
        S.op("dve", lambda e: e.tensor_tensor(out=o_a, in0=o_a, in1=sublnw[0:NS, :].unsqueeze(1).broadcast_to([NS, 8, 128]), op=ALU.mult), [rks, r_subw], [rks])
        S.op("dve", lambda e: e.tensor_tensor(out=og[0:NS, 0, :], in0=ks[0:NS, :], in1=sg[0:NS, 0, :], op=ALU.mult), [rks, r_sg], [r_og])
        pb_ = ps_bf(7)
        for h in range(8):
            S.op("pe", lambda e, h=h: e.transpose(out=pb_[:, h * NS:(h + 1) * NS], in_=og[0:NS, 0, h * 128:(h + 1) * 128], identity=ident_b[0:NS, 0:NS]),
                 [r_og, r_identb], [rps[7]])
        S.op("act", lambda e: e.copy(out=ogT[:, :, 0:NS], in_=pb_[:, 0:8 * NS].rearrange("p (h t) -> p h t", h=8)), [rps[7]], [r_ogT])
        for half in range(2):
            b = 5 + half
            for kc in range(8):
                S.op("pe", lambda e, half=half, kc=kc, b=b: e.matmul(psum[b][0:NS, :], lhsT=ogT[:, kc, 0:NS], rhs=woc[:, kc, half * 512:(half + 1) * 512],
                                                                   start=(kc == 0), stop=(kc == 7)), [r_ogT, r_woc], [rps[b]])
            S.op("dve", lambda e, half=half, b=b: e.tensor_tensor(out=vs[0:NS, half * 512:(half + 1) * 512], in0=psum[b][0:NS, :],
                                                                in1=x1s[0:NS, half * 512:(half + 1) * 512], op=ALU.add), [rps[b], r_x1s], [rvs])
        outs.append(S.dma("sp", lambda e: e.dma_start(out=T["ys_out"][:, :], in_=vs[0:NS, :]), [rvs], []))

    if DBG["st"]:
        sample_l1()

    S.emit(nc, final_waits=outs)
    st.close()
    return nc


_CACHE = {}
_HOOK = [None]


def _get_nc():
    if "nc" not in _CACHE:
        nc, S, T = build_nc()
        emit_program(nc, S, T)
        _CACHE["nc"] = nc
    return _CACHE["nc"]


def kernel(x_prompt, x_sample, state_pool, state_conv, cache_k, cache_v, page_table,
           norm_w_ab, w_in_ab, pool_w, pool_scale, conv_w, conv_b, conv_ln_w, conv_ln_b, w_out_ab,
           norm_w_c, w_in_c, q_norm_w, k_norm_w, lambda_q1, lambda_k1, lambda_q2, lambda_k2,
           subln_w, w_out_c):
    f = lambda a: np.ascontiguousarray(np.asarray(a, dtype=np.float32))
    x_prompt = f(x_prompt)
    pvec_src = np.ascontiguousarray(np.concatenate([f(conv_w)[0], f(conv_b), f(conv_ln_w), f(conv_ln_b), f(pool_scale)], axis=0))
    qk_w = np.ascontiguousarray(np.concatenate([f(q_norm_w), f(k_norm_w)], axis=1))
    lam_v = np.ascontiguousarray(np.concatenate([f(lambda_q1), f(lambda_k1), f(lambda_q2), f(lambda_k2)], axis=1))
    shared = {
        "norm_w_ab": f(norm_w_ab), "w_in_ab": f(w_in_ab)[0], "pool_w": f(pool_w)[0], "pvec_src": pvec_src,
        "w_out_ab": f(w_out_ab)[0], "norm_w_c": f(norm_w_c), "w_in_c": f(w_in_c)[0], "qk_w": qk_w, "lam_v": lam_v,
        "subln_w": f(subln_w), "w_out_c": f(w_out_c)[0],
    }
    p = np.arange(128)
    x_sample = f(x_sample); state_pool = f(state_pool); state_conv = f(state_conv)
    ck2 = f(cache_k).reshape(-1, D)
    cv2 = f(cache_v).reshape(-1, D)
    kk = np.arange(16)
    smask = ((kk[:, None] // 4 == kk[None, :] // 4) & (kk[:, None] % 4 <= kk[None, :] % 4)).astype(np.float32)
    in_maps = []
    for c in range(8):
        b, hf = c // 2, c % 2
        own_rows = np.zeros((128, 16), np.int32)
        for s_ in range(8):
            for blk in range(2):
                own_rows[:, s_ * 2 + blk] = (2 * s_ + hf) * TT + blk * 128 + p
        am = np.zeros((128, 2, 2, TT), np.float32)
        q = np.arange(TT)
        for kt2 in range(2):
            for kblk in range(2):
                am[:, kt2, kblk, :] = ((kt2 - hf) * TT + kblk * 128 + p[:, None] <= q[None, :]).astype(np.float32)
        m = dict(shared)
        m["xs"] = np.ascontiguousarray(x_sample[4 * c:4 * c + 4].reshape(16, D))
        m["spool"] = np.ascontiguousarray(state_pool[0, 4 * c:4 * c + 4])
        m["sconv"] = np.ascontiguousarray(state_conv[0, 4 * c:4 * c + 4])
        m["pt"] = np.ascontiguousarray(np.asarray(page_table, dtype=np.int32)[4 * c:4 * c + 4].reshape(1, 256))
        m["smask"] = smask
        m["cache_k"] = ck2
        m["cache_v"] = cv2
        m["xp"] = x_prompt[b]
        m["own_rows"] = own_rows
        m["amask"] = am.reshape(128, 4 * TT)
        in_maps.append(m)
    nc = _get_nc()
    res = run_bass_kernel_spmd(nc, in_maps, core_ids=list(range(8)))
    R = res.results
    if _HOOK[0] is not None:
        _HOOK[0](R)
    y_prompt = np.zeros((NBATCH, SEQ, D), np.float32)
    k_p = np.zeros((1, NBATCH, SEQ, 8, 128), np.float32)
    v_p = np.zeros((1, NBATCH, SEQ, 8, 128), np.float32)
    pool_p = np.zeros((1, NBATCH, 15, D), np.float32)
    conv_p = np.zeros((1, NBATCH, 30, D), np.float32)
    for c in range(8):
        b, hf = c // 2, c % 2
        yo = R[c]["y_own"].reshape(8, TT, D)
        for s_ in range(8):
            t = 2 * s_ + hf
            y_prompt[b, t * TT:(t + 1) * TT] = yo[s_]
        if hf == 0:
            k_p[0, b] = R[c]["k_all"].reshape(SEQ, 8, 128)
            v_p[0, b] = R[c]["v_all"].reshape(SEQ, 8, 128)
            pool_p[0, b] = R[c]["pool_st"]
            conv_p[0, b] = R[c]["conv_st"]
    ys = np.zeros((32, 4, D), np.float32)
    pool_s = np.zeros((1, 32, 15, D), np.float32)
    conv_s = np.zeros((1, 32, 30, D), np.float32)
    k_s = np.zeros((1, 32, 4, 8, 128), np.float32)
    v_s = np.zeros((1, 32, 4, 8, 128), np.float32)
    for c in range(8):
        ys[4 * c:4 * c + 4] = R[c]["ys_out"].reshape(4, 4, D)
        pool_s[0, 4 * c:4 * c + 4] = R[c]["pool_s_out"]
        conv_s[0, 4 * c:4 * c + 4] = R[c]["conv_s_out"]
        k_s[0, 4 * c:4 * c + 4] = R[c]["ks_out"].reshape(4, 4, 8, 128)
        v_s[0, 4 * c:4 * c + 4] = R[c]["vs_out"].reshape(4, 4, 8, 128)
    return (y_prompt, ys, pool_p, conv_p, k_p, v_p, pool_s, conv_s, k_s, v_s)
```

```python
import contextlib
import math
import os
import numpy as np
import concourse.bass as bass
import concourse.mybir as mybir
from concourse.bass_utils import run_bass_kernel_spmd

F32 = mybir.dt.float32
BF16 = mybir.dt.bfloat16
I32 = mybir.dt.int32
U8 = mybir.dt.uint8
ALU = mybir.AluOpType
AF = mybir.ActivationFunctionType
AX = mybir.AxisListType

NDSEM = 12

D = 1024
SEQ = 4096
NBATCH = 4
TT = 256
NT = SEQ // TT
NB = TT // 128
EPS = 1e-6
LAMBDA_INIT = 0.8 - 0.6 * math.exp(-0.3 * 1)
OWN = {0: [0, 15, 2, 13, 4, 11, 6, 9], 1: [1, 14, 3, 12, 5, 10, 7, 8]}
POOL_W = (2, 4, 8, 16)
NPOOL = int(os.environ.get("K_NPOOL", 2560))
NPAGE = 64
import os
DBG = {"nt1": int(os.environ.get("K_NT1", NT)), "nt2": int(os.environ.get("K_NT2", NT)), "attn": int(os.environ.get("K_ATTN", 1)), "st": int(os.environ.get("K_ST", 1)), "npg": int(os.environ.get("K_NPG", NPAGE))}


class Res:
    __slots__ = ("name", "w", "r", "excl")

    def __init__(self, name="", excl=False):
        self.name = name
        self.w = None
        self.r = []
        self.excl = excl


class Sched:
    ENGS = ("pe", "act", "dve", "pool", "sp")

    def __init__(self):
        self.ops = {e: [] for e in self.ENGS}
        self.ndma = {e: 0 for e in self.ENGS}
        self.allres = []
        self.pending = {e: set() for e in self.ENGS}

    def res(self, name="", excl=False):
        r = Res(name, excl)
        self.allres.append(r)
        return r

    def barrier(self):
        b = set()
        for r in self.allres:
            if r.w is not None:
                b.add(r.w)
            b.update(r.r)
        for e in self.ENGS:
            self.pending[e] |= b

    def _deps(self, eng, reads, writes, is_dma):
        deps = set()
        for r in reads:
            if r.w is not None:
                w = r.w
                if w[0] == "c" and w[1] == eng and eng == "pe" and not is_dma:
                    continue
                deps.add(w)
        for wr in writes:
            cands = list(wr.r)
            if wr.w is not None:
                cands.append(wr.w)
            for w in cands:
                if w[0] == "c" and w[1] == eng and not is_dma:
                    continue
                deps.add(w)
        if self.pending[eng]:
            for w in self.pending[eng]:
                if w[0] == "c" and w[1] == eng and not is_dma:
                    continue
                deps.add(w)
            self.pending[eng] = set()
        return deps

    def _commit(self, me, reads, writes):
        for r in reads:
            r.r.append(me)
        for w in writes:
            w.w = me
            w.r = []

    def op(self, eng, fn, reads=(), writes=()):
        writes = list(writes) + [r for r in reads if r.excl]
        reads = [r for r in reads if not r.excl]
        deps = self._deps(eng, reads, writes, False)
        idx = len(self.ops[eng])
        self.ops[eng].append({"fn": fn, "deps": deps, "dma": None})
        me = ("c", eng, idx)
        self._commit(me, reads, writes)
        return me

    def dma(self, eng, fn, reads=(), writes=()):
        deps = self._deps(eng, reads, writes, True)
        k = self.ndma[eng]
        self.ndma[eng] += 1
        if k >= NDSEM:
            deps.add(("d", eng, k - NDSEM))
        self.ops[eng].append({"fn": fn, "deps": deps, "dma": k})
        me = ("d", eng, k)
        self._commit(me, reads, writes)
        return me

    def emit(self, nc, final_waits=()):
        need = {e: set() for e in self.ENGS}
        for e in self.ENGS:
            for o in self.ops[e]:
                for d in o["deps"]:
                    if d[0] == "c":
                        need[d[1]].add(d[2])
        for d in final_waits:
            if d[0] == "c":
                need[d[1]].add(d[2])
        cnt = {}
        for e in self.ENGS:
            c = 0
            m = {}
            for i in sorted(need[e]):
                c += 1
                m[i] = c
            cnt[e] = m
        with contextlib.ExitStack() as st:
            csem = {e: st.enter_context(nc.semaphore("c_" + e)) for e in self.ENGS}
            dsem = {e: [st.enter_context(nc.semaphore("d_%s_%d" % (e, i))) for i in range(NDSEM)]
                    for e in self.ENGS if self.ndma[e] > 0}
            block = st.enter_context(nc.Block())

            def body(e, engine):
                waited = {}

                def do_wait(d):
                    if d[0] == "c":
                        key = ("c", d[1]); sem = csem[d[1]]; val = cnt[d[1]][d[2]]
                    else:
                        key = ("d", d[1], d[2] % NDSEM); sem = dsem[d[1]][d[2] % NDSEM]
                        val = 16 * (d[2] // NDSEM + 1)
                    if waited.get(key, 0) >= val:
                        return
                    waited[key] = val
                    engine.wait_ge(sem, val)

                for i, o in enumerate(self.ops[e]):
                    for d in sorted(o["deps"]):
                        do_wait(d)
                    ins = o["fn"](engine)
                    if o["dma"] is not None:
                        ins.then_inc(dsem[e][o["dma"] % NDSEM], 16)
                    elif i in cnt[e]:
                        ins.then_inc(csem[e], 1)
                if e == "sp":
                    for d in final_waits:
                        do_wait(d)

            block.tensor(lambda eng: body("pe", eng))
            block.scalar(lambda eng: body("act", eng))
            block.vector(lambda eng: body("dve", eng))
            block.gpsimd(lambda eng: body("pool", eng))
            block.sync(lambda eng: body("sp", eng))


class Arena:
    def __init__(self, nc, st, nbytes):
        self.t = st.enter_context(nc.sbuf_tensor("arena", [128, nbytes], U8))
        self.n = nbytes
        self.off = 0
        self.peak = 0

    def alloc(self, shape, dt):
        esz = 4 if dt in (F32, I32) else 2
        n = esz
        for s in shape:
            n *= s
        n = (n + 31) // 32 * 32
        assert self.off + n <= self.n, "arena overflow: %d + %d > %d" % (self.off, n, self.n)
        v = self.t[:, self.off:self.off + n].bitcast(dt)
        tot = 1
        for s in shape:
            tot *= s
        v = v[:, 0:tot]
        if len(shape) == 2:
            v = v.rearrange("p (a b) -> p a b", a=shape[0])
        elif len(shape) == 3:
            v = v.rearrange("p (a b c) -> p a b c", a=shape[0], b=shape[1])
        self.off += n
        self.peak = max(self.peak, self.off)
        return v

    def mark(self):
        return self.off

    def reset(self, m):
        self.off = m


def build_nc():
    nc = bass.Bass("TRN2", target_bir_lowering=False)
    S = Sched()
    dt_in = lambda name, shape, dt=F32: nc.dram_tensor(name, shape, dt, kind="ExternalInput").ap()
    dt_out = lambda name, shape, dt=F32: nc.dram_tensor(name, shape, dt, kind="ExternalOutput").ap()
    dt_int = lambda name, shape, dt: nc.dram_tensor(name, shape, dt, kind="Internal").ap()

    xp = dt_in("xp", [SEQ, D])
    own_idx_unused = None
    norm_w_ab = dt_in("norm_w_ab", [1, D])
    w_in_ab = dt_in("w_in_ab", [D, 5 * D])
    pool_w = dt_in("pool_w", [4, 256, 256])
    pvec_src = dt_in("pvec_src", [35, D])
    w_out_ab = dt_in("w_out_ab", [2 * D, D])
    norm_w_c = dt_in("norm_w_c", [1, D])
    w_in_c = dt_in("w_in_c", [D, 4 * D])
    qk_w = dt_in("qk_w", [1, 128])
    lam_v = dt_in("lam_v", [1, 256])
    subln_w = dt_in("subln_w", [1, 128])
    w_out_c = dt_in("w_out_c", [D, D])
    own_tiles = None

    xs = dt_in("xs", [16, D])
    spool = dt_in("spool", [4, 15, D])
    sconv = dt_in("sconv", [4, 30, D])
    pt = dt_in("pt", [1, 256], I32)
    smask = dt_in("smask", [16, 16])
    cache_k = dt_in("cache_k", [NPOOL * 128, D])
    cache_v = dt_in("cache_v", [NPOOL * 128, D])
    ys_out = dt_out("ys_out", [16, D])
    pool_s_out = dt_out("pool_s_out", [4, 15, D])
    conv_s_out = dt_out("conv_s_out", [4, 30, D])
    ks_out = dt_out("ks_out", [16, D])
    vs_out = dt_out("vs_out", [16, D])
    own_rows = dt_in("own_rows", [128, 16], I32)
    amask = dt_in("amask", [128, 4 * TT])
    y_own = dt_out("y_own", [SEQ // 2, D])
    k_all = dt_out("k_all", [SEQ, D])
    v_all = dt_out("v_all", [SEQ, D])
    pool_st = dt_out("pool_st", [15, D])
    conv_st = dt_out("conv_st", [30, D])

    if os.environ.get("K_DUMP"):
        d_h0T = dt_out("d_h0T", [128, 8 * TT], BF16)
        d_win = dt_out("d_win", [128, 1024], BF16)
        d_h0 = dt_out("d_h0", [128, NB * D], BF16)
        d_xae = dt_out("d_xae", [128, 2 * (15 + TT)], F32)
    x1_scr = dt_int("x1_scr", [SEQ, D], F32)
    kt_scr = dt_int("kt_scr", [8, 128, SEQ], BF16)
    v_scr = dt_int("v_scr", [8, 128, SEQ // 128, 129], BF16)

    return nc, S, locals()


def emit_program(nc, S, T):
    xp = T["xp"]
    outs = []
    st = contextlib.ExitStack()
    A = Arena(nc, st, 211968)
    psum_all = st.enter_context(nc.psum_tensor("ps_all", [128, 4096], F32))
    psum = [psum_all[:, i * 512:(i + 1) * 512] for i in range(8)]
    rps = [S.res("ps%d" % i, excl=True) for i in range(8)]

    def ps_bf(i):
        return psum[i][:, :].bitcast(BF16)

    ident_f = A.alloc([128], F32); r_identf = S.res()
    ident_b = A.alloc([128], BF16); r_identb = S.res()
    mask_b = A.alloc([128], BF16); r_mask = S.res()
    ones_b = A.alloc([128], BF16); r_ones = S.res()
    iot = A.alloc([128], F32); r_iot = S.res()
    pvec = A.alloc([8, 35], F32); r_pvec = S.res()
    nw = A.alloc([D], F32); r_nw = S.res()
    qkw = A.alloc([128], F32); r_qkw = S.res()
    sublnw = A.alloc([128], F32); r_subw = S.res()
    lamv = A.alloc([256], F32); r_lamv = S.res()
    lam = A.alloc([4], F32); r_lam = S.res()
    invc = A.alloc([4, 16], F32); r_invc = S.res()
    small = A.alloc([64], F32); r_small = S.res()
    junk = A.alloc([D], BF16); r_junk = S.res()

    S.op("pool", lambda e: e.iota(out=iot, pattern=[[1, 128]], base=0, channel_multiplier=-1,
                                  allow_small_or_imprecise_dtypes=True), [], [r_iot])
    S.op("dve", lambda e: e.tensor_single_scalar(out=ident_f, in_=iot, scalar=0.0, op=ALU.is_equal), [r_iot], [r_identf])
    S.op("dve", lambda e: e.tensor_copy(out=ident_b, in_=ident_f), [r_identf], [r_identb])
    S.op("dve", lambda e: e.tensor_single_scalar(out=mask_b, in_=iot, scalar=0.0, op=ALU.is_ge), [r_iot], [r_mask])
    S.op("dve", lambda e: e.memset(ones_b, 1.0), [], [r_ones])
    iot2 = A.alloc([16], F32); r_iot2 = S.res()
    S.op("pool", lambda e: e.iota(out=iot2, pattern=[[1, 16]], base=1, channel_multiplier=0,
                                  allow_small_or_imprecise_dtypes=True), [], [r_iot2])
    for g, w in enumerate(POOL_W):
        S.op("dve", lambda e, g=g, w=w: e.tensor_scalar(out=invc[:, g, :], in0=iot2, scalar1=float(w), scalar2=None, op0=ALU.min),
             [r_iot2], [r_invc])
    S.op("dve", lambda e: e.reciprocal(out=invc, in_=invc), [r_invc], [r_invc])

    m0 = A.mark()
    t35 = A.alloc([D], F32); r_t35 = S.res()
    S.dma("sp", lambda e: e.dma_start(out=t35[0:35, :], in_=T["pvec_src"][:, :]), [], [r_t35])
    for c in range(8):
        S.op("pe", lambda e, c=c: e.transpose(out=psum[0][:, c * 35:(c + 1) * 35], in_=t35[0:35, c * 128:(c + 1) * 128],
                                              identity=ident_f[0:35, 0:35]), [r_t35, r_identf], [rps[0]])
    S.op("dve", lambda e: e.tensor_copy(out=pvec, in_=psum[0][:, 0:280].rearrange("p (c k) -> p c k", c=8)), [rps[0]], [r_pvec])

    S.dma("sp", lambda e: e.dma_start(out=qkw, in_=T["qk_w"][0:1, :].partition_broadcast(128)), [], [r_qkw])
    S.dma("sp", lambda e: e.dma_start(out=sublnw, in_=T["subln_w"][0:1, :].partition_broadcast(128)), [], [r_subw])
    S.dma("sp", lambda e: e.dma_start(out=lamv, in_=T["lam_v"][0:1, :].partition_broadcast(128)), [], [r_lamv])
    S.op("dve", lambda e: e.tensor_scalar(out=sublnw, in0=sublnw, scalar1=float(1.0 - LAMBDA_INIT), scalar2=None, op0=ALU.mult),
         [r_subw], [r_subw])
    S.op("dve", lambda e: e.tensor_tensor(out=small[:, 0:64], in0=lamv[:, 0:64], in1=lamv[:, 64:128], op=ALU.mult), [r_lamv], [r_small])
    S.op("dve", lambda e: e.tensor_reduce(out=lam[:, 2:3], in_=small[:, 0:64], axis=AX.X, op=ALU.add), [r_small], [r_lam])
    S.op("dve", lambda e: e.tensor_tensor(out=small[:, 0:64], in0=lamv[:, 128:192], in1=lamv[:, 192:256], op=ALU.mult), [r_lamv, r_lam], [r_small])
    S.op("dve", lambda e: e.tensor_reduce(out=lam[:, 3:4], in_=small[:, 0:64], axis=AX.X, op=ALU.add), [r_small], [r_lam])
    S.op("act", lambda e: e.activation(out=lam[:, 2:4], in_=lam[:, 2:4], func=AF.Exp), [r_lam], [r_lam])
    S.op("dve", lambda e: e.scalar_tensor_tensor(out=lam[:, 0:1], in0=lam[:, 3:4], scalar=float(-LAMBDA_INIT), in1=lam[:, 2:3],
                                                 op0=ALU.add, op1=ALU.subtract), [r_lam], [r_lam])
    S.op("dve", lambda e: e.tensor_reduce(out=lam[:, 2:3], in_=qkw[:, 0:64], axis=AX.X, op=ALU.max, apply_absolute_value=True), [r_qkw, r_lam], [r_lam])
    S.op("dve", lambda e: e.tensor_reduce(out=lam[:, 3:4], in_=qkw[:, 64:128], axis=AX.X, op=ALU.max, apply_absolute_value=True), [r_qkw, r_lam], [r_lam])
    S.op("dve", lambda e: e.scalar_tensor_tensor(out=lam[:, 1:2], in0=lam[:, 2:3], scalar=-8.0, in1=lam[:, 3:4],
                                                 op0=ALU.mult, op1=ALU.mult), [r_lam], [r_lam])

    def rms_rstd(ss_ap, n, res_list):
        S.op("dve", lambda e: e.tensor_scalar(out=ss_ap, in0=ss_ap, scalar1=1.0 / n, scalar2=EPS, op0=ALU.mult, op1=ALU.add), res_list, res_list)
        S.op("act", lambda e: e.activation(out=ss_ap, in_=ss_ap, func=AF.Sqrt), res_list, res_list)
        S.op("dve", lambda e: e.reciprocal(out=ss_ap, in_=ss_ap), res_list, res_list)

    def rmsnorm_tile(xt, r_xt, h, r_h, hT, r_hT, nblk, ncols):
        for blk in range(nblk):
            ss = small[:, blk:blk + 1]
            S.op("act", lambda e, blk=blk, ss=ss: e.activation(out=junk, in_=xt[:, blk, :], func=AF.Square, accum_out=ss),
                 [r_xt], [r_junk, r_small])
            rms_rstd(ss, D, [r_small])
            S.op("dve", lambda e, blk=blk, ss=ss: e.scalar_tensor_tensor(out=h[:, blk, :], in0=xt[:, blk, :], scalar=ss, in1=nw,
                                                                         op0=ALU.mult, op1=ALU.mult), [r_xt, r_small, r_nw], [r_h])
            pb = ps_bf(7)
            for c in range(8):
                S.op("pe", lambda e, blk=blk, c=c, pb=pb: e.transpose(out=pb[:, c * 128:(c + 1) * 128], in_=h[:, blk, c * 128:(c + 1) * 128],
                                                                      identity=ident_b), [r_h, r_identb], [rps[7]])
            S.op("act", lambda e, blk=blk, pb=pb: e.copy(out=hT[:, :, blk * 128:(blk + 1) * 128],
                                                         in_=pb.rearrange("p (c t) -> p c t", c=8)), [rps[7]], [r_hT])

    S.dma("sp", lambda e: e.dma_start(out=nw, in_=T["norm_w_ab"][0:1, :].partition_broadcast(128)), [], [r_nw])
    x1s = A.alloc([D], F32); r_x1s = S.res()
    mP = A.mark()
    win = A.alloc([8, 5 * D], BF16); r_win = S.res()
    wout = A.alloc([16, D], BF16); r_wout = S.res()
    wpool = A.alloc([4, 2, 256], BF16); r_wpool = S.res()
    ncast = [0]

    def cast_load(dst, r_dst, src, nchunk, ncols, stage, r_stage):
        for c in range(nchunk):
            for k in range(ncols // 1024):
                i = ncast[0] % len(stage)
                ncast[0] += 1
                S.dma("sp", lambda e, c=c, k=k, i=i: e.dma_start(out=stage[i], in_=src[c * 128:(c + 1) * 128, k * 1024:(k + 1) * 1024]), [], [r_stage[i]])
                if ncast[0] % 2:
                    S.op("act", lambda e, c=c, k=k, i=i: e.copy(out=dst[:, c, k * 1024:(k + 1) * 1024], in_=stage[i]), [r_stage[i]], [r_dst])
                else:
                    S.op("dve", lambda e, c=c, k=k, i=i: e.tensor_copy(out=dst[:, c, k * 1024:(k + 1) * 1024], in_=stage[i]), [r_stage[i]], [r_dst])

    mW = A.mark()
    stage1 = [A.alloc([D], F32) for _ in range(4)]; r_stage1 = [S.res() for _ in range(4)]
    cast_load(win, r_win, T["w_in_ab"], 8, 5 * D, stage1, r_stage1)
    cast_load(wout, r_wout, T["w_out_ab"], 16, D, stage1, r_stage1)
    for g in range(4):
        for cc in range(2):
            S.dma("pool", lambda e, g=g, cc=cc: e.dma_start(out=wpool[:, g, cc, :], in_=T["pool_w"][g, cc * 128:(cc + 1) * 128, :]), [], [r_wpool])

    S.barrier()
    A.reset(mW)

    def sample_l0():
        NS = 16
        xs_t = A.alloc([D], F32); r_xs = S.res()
        hs = A.alloc([D], BF16); r_hs = S.res()
        hTs = A.alloc([8, NS], BF16); r_hTs = S.res()
        stp = [A.alloc([D], F32) for _ in range(2)]; r_stp = [S.res() for _ in range(2)]
        xae = A.alloc([8, 4, 19], F32); r_xaes = S.res()
        sab = [A.alloc([8, 4, 19], F32) for _ in range(2)]; r_sab = [S.res() for _ in range(2)]
        glue = A.alloc([8, 4, 34], F32); r_glues = S.res()
        xac = A.alloc([8, NS], F32); r_xac = S.res()
        gluc = A.alloc([8, NS], F32); r_gluc = S.res()
        tok = A.alloc([D], F32); r_tok = S.res()
        tok2 = A.alloc([D], F32); r_tok2 = S.res()
        sgas = A.alloc([8, NS], BF16); r_sgas = S.res()
        sgbs = A.alloc([8, NS], BF16); r_sgbs = S.res()
        sigs = A.alloc([8, NS], F32); r_sigs = S.res()
        dTs = A.alloc([8, NS], BF16); r_dTs = S.res()
        cs = A.alloc([8, NS], F32); r_cs = S.res(); r_csj = [S.res() for _ in range(8)]
        cs16 = A.alloc([8, NS], BF16); r_cs16 = S.res()
        csq = A.alloc([8, NS], BF16); r_csq = S.res()
        lnt8 = A.alloc([8, NS], F32); r_lnt8 = S.res()
        mixs = A.alloc([16, NS], BF16); r_mixs = S.res()
        lnms = A.alloc([NS], F32); r_lnms = S.res()
        lnrs = A.alloc([NS], F32); r_lnrs = S.res()

        outs.append(S.dma("sp", lambda e: e.dma_start(out=T["pool_s_out"][:, 0:11, :], in_=T["spool"][:, 4:15, :]), [], []))
        outs.append(S.dma("sp", lambda e: e.dma_start(out=T["conv_s_out"][:, 0:26, :], in_=T["sconv"][:, 4:30, :]), [], []))
        S.dma("sp", lambda e: e.dma_start(out=xs_t[0:NS, :], in_=T["xs"][:, :]), [], [r_xs])
        ss = small[0:NS, 8:9]
        S.op("act", lambda e: e.activation(out=junk[0:NS, :], in_=xs_t[0:NS, :], func=AF.Square, accum_out=ss), [r_xs], [r_junk, r_small])
        rms_rstd(ss, D, [r_small])
        S.op("dve", lambda e: e.scalar_tensor_tensor(out=hs[0:NS, :], in0=xs_t[0:NS, :], scalar=ss, in1=nw[0:NS, :], op0=ALU.mult, op1=ALU.mult),
             [r_xs, r_small, r_nw], [r_hs])
        pb = ps_bf(7)
        for c in range(8):
            S.op("pe", lambda e, c=c: e.transpose(out=pb[:, c * NS:(c + 1) * NS], in_=hs[0:NS, c * 128:(c + 1) * 128], identity=ident_b[0:NS, 0:NS]),
                 [r_hs, r_identb], [rps[7]])
        S.op("act", lambda e: e.copy(out=hTs, in_=pb[:, 0:8 * NS].rearrange("p (c t) -> p c t", c=8)), [rps[7]], [r_hTs])
        for e_ in range(4):
            S.dma("sp", lambda e, e_=e_: e.dma_start(out=stp[0][0:15, :], in_=T["spool"][e_, :, :]), [], [r_stp[0]])
            for c in range(8):
                S.op("pe", lambda e, c=c: e.transpose(out=psum[5][:, c * 15:(c + 1) * 15], in_=stp[0][0:15, c * 128:(c + 1) * 128], identity=ident_f[0:15, 0:15]),
                     [r_stp[0], r_identf], [rps[5]])
            S.op("dve", lambda e, e_=e_: e.tensor_copy(out=xae[:, :, e_, 0:15], in_=psum[5][:, 0:120].rearrange("p (c k) -> p c k", c=8)), [rps[5]], [r_xaes])
            S.dma("sp", lambda e, e_=e_: e.dma_start(out=stp[1][0:30, :], in_=T["sconv"][e_, :, :]), [], [r_stp[1]])
            for c in range(8):
                S.op("pe", lambda e, c=c: e.transpose(out=psum[6][:, c * 30:(c + 1) * 30], in_=stp[1][0:30, c * 128:(c + 1) * 128], identity=ident_f[0:30, 0:30]),
                     [r_stp[1], r_identf], [rps[6]])
            S.op("act", lambda e, e_=e_: e.copy(out=glue[:, :, e_, 0:30], in_=psum[6][:, 0:240].rearrange("p (c k) -> p c k", c=8)), [rps[6]], [r_glues])
        for f_ in range(40):
            bank, col = f_ // 8, (f_ % 8) * NS
            for c in range(8):
                S.op("pe", lambda e, c=c, f_=f_, bank=bank, col=col: e.matmul(psum[bank][:, col:col + NS], lhsT=win[:, c, f_ * 128:(f_ + 1) * 128], rhs=hTs[:, c, :],
                                                                            start=(c == 0), stop=(c == 7)), [r_win, r_hTs], [rps[bank]])
        v816 = lambda ap: ap[:, 0:8 * NS].rearrange("p (c t) -> p c t", c=8)
        S.op("act", lambda e: e.copy(out=xae[:, :, :, 15:19], in_=psum[0][:, 0:8 * NS].rearrange("p (c a t) -> p c a t", c=8, a=4)), [rps[0]], [r_xaes])
        S.op("dve", lambda e: e.tensor_copy(out=xac, in_=v816(psum[0])), [rps[0]], [r_xac])
        S.op("act", lambda e: e.activation(out=sgas, in_=v816(psum[1]), func=AF.Silu), [rps[1]], [r_sgas])
        S.op("act", lambda e: e.activation(out=sigs, in_=v816(psum[3]), func=AF.Sigmoid), [rps[3]], [r_sigs])
        S.op("dve", lambda e: e.tensor_tensor(out=gluc, in0=v816(psum[2]), in1=sigs, op=ALU.mult), [rps[2], r_sigs], [r_gluc])
        S.op("pool", lambda e: e.tensor_copy(out=glue[:, :, :, 30:34], in_=gluc.rearrange("p c (a t) -> p c a t", a=4)), [r_gluc], [r_glues])
        S.op("act", lambda e: e.activation(out=sgbs, in_=v816(psum[4]), func=AF.Silu), [rps[4]], [r_sgbs])
        for (srcc, r_srcc, tk, r_tk, dst, row0) in ((xac, r_xac, tok, r_tok, "pool_s_out", 11), (gluc, r_gluc, tok2, r_tok2, "conv_s_out", 26)):
            for c in range(8):
                bk = 5 + c // 4
                S.op("pe", lambda e, c=c, bk=bk, srcc=srcc: e.transpose(out=psum[bk][0:NS, (c % 4) * 128:(c % 4 + 1) * 128], in_=srcc[:, c, :], identity=ident_f),
                     [r_srcc, r_identf], [rps[bk]])
            S.op("dve", lambda e, tk=tk: e.tensor_copy(out=tk[0:NS, 0:512], in_=psum[5][0:NS, :]), [rps[5]], [r_tk])
            S.op("dve", lambda e, tk=tk: e.tensor_copy(out=tk[0:NS, 512:1024], in_=psum[6][0:NS, :]), [rps[6]], [r_tk])
            for e_ in range(4):
                outs.append(S.dma("sp", lambda e, e_=e_, tk=tk, dst=dst, row0=row0: e.dma_start(out=T[dst][e_, row0:row0 + 4, :], in_=tk[e_ * 4:(e_ + 1) * 4, :]),
                                  [r_tk], []))
        for g in range(4):
            w = POOL_W[g]
            Xg = xae[:, 2 * g:2 * g + 2]
            src, rsrc = Xg, r_xaes
            for k in range(g + 1):
                sh = 1 << k
                dst_, rdst = sab[k % 2][:, 2 * g:2 * g + 2], r_sab[k % 2]
                S.op("dve", lambda e, src=src, dst_=dst_, sh=sh: e.tensor_tensor(out=dst_[:, :, :, sh:19], in0=src[:, :, :, sh:19], in1=src[:, :, :, 0:19 - sh], op=ALU.add),
                     [rsrc], [rdst])
                src, rsrc = dst_, rdst
            S.op("dve", lambda e, src=src, Xg=Xg, g=g, w=w: e.scalar_tensor_tensor(out=dTs[:, 2 * g:2 * g + 2, :].rearrange("p c (a t) -> p c a t", a=4),
                                                                                 in0=src[:, :, :, 15:19], scalar=1.0 / w, in1=Xg[:, :, :, 15:19],
                                                                                 op0=ALU.mult, op1=ALU.subtract), [rsrc, r_xaes], [r_dTs])
            for ec in range(2):
                for cc in range(2):
                    S.op("pe", lambda e, ec=ec, cc=cc, g=g: e.matmul(psum[7][:, (2 * g + ec) * NS:(2 * g + ec + 1) * NS], lhsT=wpool[:, g, cc, ec * 128:(ec + 1) * 128],
                                                                    rhs=dTs[:, 2 * g + cc, :], start=(cc == 0), stop=(cc == 1)), [r_wpool, r_dTs], [rps[7]])
        for c in range(8):
            S.op("dve", lambda e, c=c: e.scalar_tensor_tensor(out=mixs[:, c, :], in0=psum[7][:, c * NS:(c + 1) * NS], scalar=pvec[:, c, 34:35], in1=sgas[:, c, :],
                                                             op0=ALU.mult, op1=ALU.mult), [rps[7], r_pvec, r_sgas], [r_mixs])
        for k in range(31):
            for j in range(8):
                csj = cs[:, j, :].rearrange("p (a t) -> p a t", a=4)
                if k == 0:
                    S.op("dve", lambda e, j=j, csj=csj: e.tensor_scalar(out=csj, in0=glue[:, j, :, 0:4], scalar1=pvec[:, j, 0:1], scalar2=pvec[:, j, 31:32],
                                                                        op0=ALU.mult, op1=ALU.add), [r_glues, r_pvec], [r_csj[j]])
                else:
                    S.op("dve", lambda e, j=j, k=k, csj=csj: e.scalar_tensor_tensor(out=csj, in0=glue[:, j, :, k:k + 4], scalar=pvec[:, j, k:k + 1], in1=csj,
                                                                                  op0=ALU.mult, op1=ALU.add), [r_glues, r_pvec, r_csj[j]], [r_csj[j]])
        S.op("dve", lambda e: e.memset(lnms, 0.0), r_csj, [r_cs, r_lnms])
        S.op("act", lambda e: e.activation(out=csq, in_=cs, func=AF.Square), [r_cs], [r_csq])
        S.op("pool", lambda e: e.tensor_copy(out=cs16, in_=cs), [r_cs], [r_cs16])
        for j in range(8):
            S.op("pe", lambda e, j=j: e.matmul(psum[5][:, 0:NS], lhsT=ones_b, rhs=cs16[:, j, :], start=(j == 0), stop=(j == 7)), [r_ones, r_cs16], [rps[5]])
        for j in range(8):
            S.op("pe", lambda e, j=j: e.matmul(psum[6][:, 0:NS], lhsT=ones_b, rhs=csq[:, j, :], start=(j == 0), stop=(j == 7)), [r_ones, r_csq], [rps[6]])
        S.op("dve", lambda e: e.tensor_scalar(out=lnms, in0=psum[5][:, 0:NS], scalar1=1.0 / D, scalar2=None, op0=ALU.mult), [rps[5]], [r_lnms])
        S.op("dve", lambda e: e.tensor_tensor(out=lnrs, in0=lnms, in1=lnms, op=ALU.mult), [r_lnms], [r_lnrs])
        S.op("dve", lambda e: e.scalar_tensor_tensor(out=lnrs, in0=psum[6][:, 0:NS], scalar=1.0 / D, in1=lnrs, op0=ALU.mult, op1=ALU.subtract), [rps[6], r_lnrs], [r_lnrs])
        S.op("dve", lambda e: e.tensor_scalar(out=lnrs, in0=lnrs, scalar1=EPS, scalar2=None, op0=ALU.add), [r_lnrs], [r_lnrs])
        S.op("act", lambda e: e.activation(out=lnrs, in_=lnrs, func=AF.Sqrt), [r_lnrs], [r_lnrs])
        S.op("dve", lambda e: e.reciprocal(out=lnrs, in_=lnrs), [r_lnrs], [r_lnrs])
        S.op("dve", lambda e: e.tensor_tensor(out=lnt8, in0=cs, in1=lnms.unsqueeze(1).broadcast_to([128, 8, NS]), op=ALU.subtract), [r_cs, r_lnms], [r_lnt8])
        S.op("dve", lambda e: e.tensor_tensor(out=lnt8, in0=lnt8, in1=lnrs.unsqueeze(1).broadcast_to([128, 8, NS]), op=ALU.mult), [r_lnt8, r_lnrs], [r_lnt8])
        for j in range(8):
            S.op("act", lambda e, j=j: e.activation(out=lnt8[:, j, :], in_=lnt8[:, j, :], func=AF.Silu, scale=pvec[:, j, 32:33], bias=pvec[:, j, 33:34]),
                 [r_lnt8, r_pvec], [r_lnt8])
        S.op("dve", lambda e: e.tensor_tensor(out=mixs[:, 8:16, :], in0=lnt8, in1=sgbs, op=ALU.mult), [r_lnt8, r_sgbs], [r_mixs])
        for half in range(2):
            bo = 5 + half
            for kc in range(16):
                S.op("pe", lambda e, half=half, kc=kc, bo=bo: e.matmul(psum[bo][0:NS, :], lhsT=mixs[:, kc, :], rhs=wout[:, kc, half * 512:(half + 1) * 512],
                                                                      start=(kc == 0), stop=(kc == 15)), [r_mixs, r_wout], [rps[bo]])
            S.op("dve", lambda e, half=half, bo=bo: e.tensor_tensor(out=x1s[0:NS, half * 512:(half + 1) * 512], in0=psum[bo][0:NS, :],
                                                                   in1=xs_t[0:NS, half * 512:(half + 1) * 512], op=ALU.add), [rps[bo], r_xs], [r_x1s])

    mS = A.mark()
    if DBG["st"]:
        sample_l0()
        print("sample-L0 arena bytes:", A.off)
        S.barrier()
    A.reset(mS)

    xt = A.alloc([NB, D], F32); r_xt = S.res()
    h0 = A.alloc([NB, D], BF16); r_h0 = S.res()
    h0T = A.alloc([8, TT], BF16); r_h0T = S.res()
    xa_h = A.alloc([8, 15], F32); r_xah = S.res()
    glu_h = A.alloc([8, 30], F32); r_gluh = S.res()
    xa_e = [A.alloc([2, 15 + TT], F32) for _ in range(2)]; r_xae = [S.res() for _ in range(2)]
    s_tmp = [A.alloc([2, 15 + TT], F32) for _ in range(2)]; r_stmp = [S.res() for _ in range(2)]
    sga = [A.alloc([2, TT], BF16) for _ in range(2)]; r_sga = [S.res() for _ in range(2)]
    dT = [A.alloc([2, TT], BF16) for _ in range(2)]; r_dT = [S.res() for _ in range(2)]
    glu_e = [A.alloc([30 + TT], F32) for _ in range(4)]; r_glue = [S.res() for _ in range(4)]
    sig = [A.alloc([TT], F32) for _ in range(2)]; r_sig = [S.res() for _ in range(2)]
    cbuf = A.alloc([8, TT], F32); r_c = [S.res() for _ in range(8)]
    cb16 = A.alloc([8, TT], BF16); r_cb16 = S.res()
    csq16 = A.alloc([8, TT], BF16); r_csq16 = S.res()
    sgb = A.alloc([8, TT], BF16); r_sgb = [S.res() for _ in range(8)]
    mixT = A.alloc([16, TT], BF16); r_mix = [S.res() for _ in range(16)]
    lnm = A.alloc([TT], F32); r_lnm = S.res()
    lnr = A.alloc([TT], F32); r_lnr = S.res()
    lnt = [A.alloc([TT], F32) for _ in range(2)]; r_lnt = [S.res() for _ in range(2)]
    print("phase1 arena bytes:", A.off)

    S.op("dve", lambda e: e.memset(xa_h, 0.0), [], [r_xah])
    S.op("dve", lambda e: e.memset(glu_h, 0.0), [], [r_gluh])

    r_x1 = [S.res() for _ in range(NT)]
    pi = [0]

    def nxt_bank():
        b = pi[0] % 4
        pi[0] += 1
        return b

    for t in range(DBG["nt1"]):
        tok0 = t * TT
        S.dma("sp", lambda e, tok0=tok0: e.dma_start(out=xt, in_=xp[tok0:tok0 + TT, :].rearrange("(k p) d -> p k d", p=128)), [], [r_xt])
        rmsnorm_tile(xt, r_xt, h0, r_h0, h0T, r_h0T, NB, TT)

        def inproj(bank, col, half, fchunk):
            for c in range(8):
                S.op("pe", lambda e, c=c: e.matmul(psum[bank][:, half * TT:(half + 1) * TT], lhsT=win[:, c, fchunk * 128:(fchunk + 1) * 128],
                                                   rhs=h0T[:, c, :], start=(c == 0), stop=(c == 7)), [r_win, r_h0T], [rps[bank]])

        if t == 0 and os.environ.get("K_DUMP"):
            outs.append(S.dma("sp", lambda e: e.dma_start(out=T["d_h0T"][:, :], in_=h0T.rearrange("p c t -> p (c t)")), [r_h0T], []))
            outs.append(S.dma("sp", lambda e: e.dma_start(out=T["d_h0"][:, :], in_=h0.rearrange("p c t -> p (c t)")), [r_h0], []))
            outs.append(S.dma("sp", lambda e: e.dma_start(out=T["d_win"][:, :], in_=win[:, 0, 0:1024]), [r_win], []))
        for g in range(4):
            w = POOL_W[g]
            i2 = g % 2
            ba = nxt_bank(); bb = nxt_bank()
            inproj(ba, 0, 0, 2 * g); inproj(ba, 0, 1, 2 * g + 1)
            inproj(bb, 0, 0, 8 + 2 * g); inproj(bb, 0, 1, 8 + 2 * g + 1)
            xe = xa_e[i2]; rxe = r_xae[i2]
            S.op("act", lambda e, bb=bb, i2=i2: e.activation(out=sga[i2], in_=psum[bb][:, :].rearrange("p (a t) -> p a t", a=2), func=AF.Silu),
                 [rps[bb]], [r_sga[i2]])
            S.op("act", lambda e, ba=ba, xe=xe: e.copy(out=xe[:, :, 15:15 + TT], in_=psum[ba][:, :].rearrange("p (a t) -> p a t", a=2)),
                 [rps[ba]], [rxe])
            S.op("act", lambda e, g=g, xe=xe: e.copy(out=xe[:, :, 0:15], in_=xa_h[:, 2 * g:2 * g + 2, :]), [r_xah], [rxe])
            if t == 0 and g == 0 and os.environ.get("K_DUMP"):
                outs.append(S.dma("sp", lambda e, xe=xe: e.dma_start(out=T["d_xae"][:, :], in_=xe.rearrange("p c t -> p (c t)")), [rxe], []))
            src, rsrc = xe, rxe
            L = 15 + TT
            for k in range(g + 1):
                sh = 1 << k
                dst, rdst = s_tmp[k % 2], r_stmp[k % 2]
                S.op("dve", lambda e, src=src, dst=dst, sh=sh: e.tensor_tensor(out=dst[:, :, sh:L], in0=src[:, :, sh:L], in1=src[:, :, 0:L - sh], op=ALU.add),
                     [rsrc], [rdst])
                src, rsrc = dst, rdst
            S.op("dve", lambda e, src=src, xe=xe, i2=i2, w=w: e.scalar_tensor_tensor(out=dT[i2], in0=src[:, :, 15:L], scalar=1.0 / w, in1=xe[:, :, 15:L],
                                                                                   op0=ALU.mult, op1=ALU.subtract), [rsrc, rxe], [r_dT[i2]])
            if t == 0:
                S.op("dve", lambda e, src=src, g=g: e.tensor_tensor(out=src[:, :, 15:30], in0=src[:, :, 15:30],
                                                                     in1=invc[:, g:g + 1, 0:15].broadcast_to([128, 2, 15]), op=ALU.mult), [rsrc, r_invc], [rsrc])
                S.op("dve", lambda e, src=src, xe=xe, i2=i2: e.tensor_tensor(out=dT[i2][:, :, 0:15], in0=src[:, :, 15:30], in1=xe[:, :, 15:30], op=ALU.subtract),
                     [rsrc, rxe], [r_dT[i2]])
            S.op("act", lambda e, g=g, xe=xe: e.copy(out=xa_h[:, 2 * g:2 * g + 2, :], in_=xe[:, :, TT:TT + 15]), [rxe], [r_xah])
            bc = nxt_bank()
            for ec in range(2):
                for cc in range(2):
                    S.op("pe", lambda e, ec=ec, cc=cc, g=g, i2=i2, bc=bc: e.matmul(psum[bc][:, ec * TT:(ec + 1) * TT], lhsT=wpool[:, g, cc, ec * 128:(ec + 1) * 128],
                                                                                  rhs=dT[i2][:, cc, :], start=(cc == 0), stop=(cc == 1)),
                         [r_wpool, r_dT[i2]], [rps[bc]])
            for ec in range(2):
                S.op("dve", lambda e, ec=ec, g=g, i2=i2, bc=bc: e.scalar_tensor_tensor(out=mixT[:, 2 * g + ec, :], in0=psum[bc][:, ec * TT:(ec + 1) * TT],
                                                                                      scalar=pvec[:, 2 * g + ec, 34:35], in1=sga[i2][:, ec, :],
                                                                                      op0=ALU.mult, op1=ALU.mult),
                     [rps[bc], r_pvec, r_sga[i2]], [r_mix[2 * g + ec]])

        for jg in (0, 4):
            for j in range(jg, jg + 4):
                i2 = j % 2
                i4 = j % 4
                bu = nxt_bank(); bg = nxt_bank()
                inproj(bu, 0, 0, 16 + j); inproj(bu, 0, 1, 24 + j)
                inproj(bg, 0, 0, 32 + j)
                ge = glu_e[i4]; rge = r_glue[i4]
                S.op("act", lambda e, bu=bu, i2=i2: e.activation(out=sig[i2], in_=psum[bu][:, TT:2 * TT], func=AF.Sigmoid), [rps[bu]], [r_sig[i2]])
                S.op("dve", lambda e, bu=bu, i2=i2, ge=ge: e.tensor_tensor(out=ge[:, 30:30 + TT], in0=psum[bu][:, 0:TT], in1=sig[i2], op=ALU.mult),
                     [rps[bu], r_sig[i2]], [rge])
                S.op("act", lambda e, j=j, ge=ge: e.copy(out=ge[:, 0:30], in_=glu_h[:, j, :]), [r_gluh], [rge])
                S.op("act", lambda e, bg=bg, j=j: e.activation(out=sgb[:, j, :], in_=psum[bg][:, 0:TT], func=AF.Silu), [rps[bg]], [r_sgb[j]])
            for k in range(31):
                for j in range(jg, jg + 4):
                    ge = glu_e[j % 4]; rge = r_glue[j % 4]
                    if k == 0:
                        S.op("dve", lambda e, j=j, ge=ge: e.tensor_scalar(out=cbuf[:, j, :], in0=ge[:, 0:TT], scalar1=pvec[:, j, 0:1], scalar2=pvec[:, j, 31:32],
                                                                          op0=ALU.mult, op1=ALU.add), [rge, r_pvec], [r_c[j]])
                    else:
                        S.op("dve", lambda e, j=j, ge=ge, k=k: e.scalar_tensor_tensor(out=cbuf[:, j, :], in0=ge[:, k:k + TT], scalar=pvec[:, j, k:k + 1], in1=cbuf[:, j, :],
                                                                                    op0=ALU.mult, op1=ALU.add), [rge, r_pvec, r_c[j]], [r_c[j]])
            for j in range(jg, jg + 4):
                ge = glu_e[j % 4]; rge = r_glue[j % 4]
                S.op("act", lambda e, j=j, ge=ge: e.copy(out=glu_h[:, j, :], in_=ge[:, TT:TT + 30]), [rge], [r_gluh])
                S.op("act", lambda e, j=j: e.activation(out=csq16[:, j, :], in_=cbuf[:, j, :], func=AF.Square), [r_c[j]], [r_csq16])
                S.op("act", lambda e, j=j: e.copy(out=cb16[:, j, :], in_=cbuf[:, j, :]), [r_c[j]], [r_cb16])
        bs = 4
        for j in range(8):
            S.op("pe", lambda e, j=j: e.matmul(psum[bs][:, 0:TT], lhsT=ones_b, rhs=cb16[:, j, :], start=(j == 0), stop=(j == 7)), [r_ones, r_cb16], [rps[bs]])
        for j in range(8):
            S.op("pe", lambda e, j=j: e.matmul(psum[5][:, 0:TT], lhsT=ones_b, rhs=csq16[:, j, :], start=(j == 0), stop=(j == 7)), [r_ones, r_csq16], [rps[5]])
        S.op("dve", lambda e: e.tensor_scalar(out=lnm, in0=psum[bs][:, 0:TT], scalar1=1.0 / D, scalar2=None, op0=ALU.mult), [rps[bs]], [r_lnm])
        S.op("dve", lambda e: e.tensor_tensor(out=lnr, in0=lnm, in1=lnm, op=ALU.mult), [r_lnm], [r_lnr])
        S.op("dve", lambda e: e.scalar_tensor_tensor(out=lnr, in0=psum[5][:, 0:TT], scalar=1.0 / D, in1=lnr, op0=ALU.mult, op1=ALU.subtract),
             [rps[5], r_lnr], [r_lnr])
        S.op("dve", lambda e: e.tensor_scalar(out=lnr, in0=lnr, scalar1=EPS, scalar2=None, op0=ALU.add), [r_lnr], [r_lnr])
        S.op("act", lambda e: e.activation(out=lnr, in_=lnr, func=AF.Sqrt), [r_lnr], [r_lnr])
        S.op("dve", lambda e: e.reciprocal(out=lnr, in_=lnr), [r_lnr], [r_lnr])
        for j in range(8):
            i2 = j % 2
            S.op("dve", lambda e, j=j, i2=i2: e.tensor_tensor(out=lnt[i2], in0=cbuf[:, j, :], in1=lnm, op=ALU.subtract), [r_c[j], r_lnm], [r_lnt[i2]])
            S.op("dve", lambda e, i2=i2: e.tensor_tensor(out=lnt[i2], in0=lnt[i2], in1=lnr, op=ALU.mult), [r_lnt[i2], r_lnr], [r_lnt[i2]])
            S.op("act", lambda e, j=j, i2=i2: e.activation(out=lnt[i2], in_=lnt[i2], func=AF.Silu, scale=pvec[:, j, 32:33], bias=pvec[:, j, 33:34]),
                 [r_lnt[i2], r_pvec], [r_lnt[i2]])
            S.op("dve", lambda e, j=j, i2=i2: e.tensor_tensor(out=mixT[:, 8 + j, :], in0=lnt[i2], in1=sgb[:, j, :], op=ALU.mult),
                 [r_lnt[i2], r_sgb[j]], [r_mix[8 + j]])
        for blk in range(NB):
            for half in range(2):
                bo = 5 + (blk * 2 + half) % 2
                for kc in range(16):
                    S.op("pe", lambda e, blk=blk, half=half, kc=kc, bo=bo: e.matmul(psum[bo][:, :], lhsT=mixT[:, kc, blk * 128:(blk + 1) * 128],
                                                                                  rhs=wout[:, kc, half * 512:(half + 1) * 512], start=(kc == 0), stop=(kc == 15)),
                         [r_mix[kc], r_wout], [rps[bo]])
                S.op("dve", lambda e, blk=blk, half=half, bo=bo: e.tensor_tensor(out=xt[:, blk, half * 512:(half + 1) * 512], in0=psum[bo][:, :],
                                                                               in1=xt[:, blk, half * 512:(half + 1) * 512], op=ALU.add), [rps[bo], r_xt], [r_xt])
        S.dma("sp", lambda e, tok0=tok0: e.dma_start(out=T["x1_scr"][tok0:tok0 + TT, :].rearrange("(k p) d -> p k d", p=128), in_=xt), [r_xt], [r_x1[t]])

    stt = cbuf.rearrange("p c t -> p (c t)")[:, 0:D]
    for c in range(4):
        S.op("pe", lambda e, c=c: e.transpose(out=psum[0][0:15, c * 128:(c + 1) * 128], in_=xa_h[:, c, :], identity=ident_f), [r_xah, r_identf], [rps[0]])
    for c in range(4):
        S.op("pe", lambda e, c=c: e.transpose(out=psum[1][0:30, c * 128:(c + 1) * 128], in_=glu_h[:, c, :], identity=ident_f), [r_gluh, r_identf], [rps[1]])
        S.op("pe", lambda e, c=c: e.transpose(out=psum[2][0:30, c * 128:(c + 1) * 128], in_=glu_h[:, 4 + c, :], identity=ident_f), [r_gluh, r_identf], [rps[2]])
    S.op("dve", lambda e: e.tensor_copy(out=stt[0:15, 0:512], in_=psum[0][0:15, 0:512]), [rps[0]], r_c[0:4])
    outs.append(S.dma("sp", lambda e: e.dma_start(out=T["pool_st"][:, 0:512], in_=stt[0:15, 0:512]), r_c[0:4], []))
    for c in range(4):
        S.op("pe", lambda e, c=c: e.transpose(out=psum[3][0:15, c * 128:(c + 1) * 128], in_=xa_h[:, 4 + c, :], identity=ident_f), [r_xah, r_identf], [rps[3]])
    S.op("dve", lambda e: e.tensor_copy(out=stt[0:15, 512:1024], in_=psum[3][0:15, 0:512]), [rps[3]], r_c[0:4])
    outs.append(S.dma("sp", lambda e: e.dma_start(out=T["pool_st"][:, 512:1024], in_=stt[0:15, 512:1024]), r_c[0:4], []))
    stc = stt
    S.op("dve", lambda e: e.tensor_copy(out=stc[0:30, 0:512], in_=psum[1][0:30, 0:512]), [rps[1]], r_c[0:4])
    S.op("dve", lambda e: e.tensor_copy(out=stc[0:30, 512:1024], in_=psum[2][0:30, 0:512]), [rps[2]], r_c[0:4])
    outs.append(S.dma("sp", lambda e: e.dma_start(out=T["conv_st"][:, :], in_=stc[0:30, :]), r_c[0:4], []))

    S.barrier()
    A.reset(mP)
    S.dma("sp", lambda e: e.dma_start(out=nw, in_=T["norm_w_c"][0:1, :].partition_broadcast(128)), [], [r_nw])
    wc = A.alloc([8, 4 * D], BF16); r_wc = S.res()
    woc = A.alloc([8, D], BF16); r_woc = S.res()
    mW2 = A.mark()
    stage2 = [A.alloc([D], F32) for _ in range(4)]; r_stage2 = [S.res() for _ in range(4)]
    cast_load(wc, r_wc, T["w_in_c"], 8, 4 * D, stage2, r_stage2)
    cast_load(woc, r_woc, T["w_out_c"], 8, D, stage2, r_stage2)
    S.barrier()
    A.reset(mW2)
    xt2 = A.alloc([NB, D], F32); r_xt2 = S.res()
    h1 = A.alloc([NB, D], BF16); r_h1 = S.res()
    h1T = A.alloc([8, TT], BF16); r_h1T = S.res()
    stg = [A.alloc([D], F32) for _ in range(2)]; r_stg = [S.res() for _ in range(2)]
    sq = A.alloc([D], F32); r_sq = S.res()
    nb16 = A.alloc([D], BF16); r_nb16 = S.res()
    ktile = A.alloc([8, TT], BF16); r_ktile = S.res()
    vtile = A.alloc([8, NB, 129], BF16); r_vtile = S.res()
    QT = A.alloc([8, TT], BF16); r_QT = S.res()
    sg = A.alloc([NB, D], BF16); r_sg = S.res()
    KCH = 4 * TT
    kth = [A.alloc([KCH], BF16) for _ in range(3)]; r_kth = [S.res() for _ in range(3)]
    vh = [A.alloc([KCH // 128, 129], BF16) for _ in range(3)]; r_vh = [S.res() for _ in range(3)]
    PT = [A.alloc([2, TT], BF16) for _ in range(3)]; r_PT = [S.res() for _ in range(3)]
    og = A.alloc([NB, D], BF16); r_og = S.res()
    ogT = A.alloc([8, TT], BF16); r_ogT = S.res()
    o1 = A.alloc([128], F32); r_o1 = S.res()
    o2 = A.alloc([128], F32); r_o2 = S.res()
    sm2 = A.alloc([16], F32); r_sm2 = S.res()
    rs16 = A.alloc([16], F32); r_rs16 = S.res()
    print("phase2 arena bytes:", A.off)
    S.op("dve", lambda e: e.memset(vtile, 1.0), [], [r_vtile])

    r_kscr = [S.res() for _ in range(NT)]
    r_vscr = [S.res() for _ in range(NT)]
    nload = [0]
    npt = [0]
    pj = [0]
    ownr = A.alloc([16], I32); r_ownr = S.res()
    amask = A.alloc([2, 2, TT], BF16); r_amask = S.res()
    S.dma("sp", lambda e: e.dma_start(out=ownr, in_=T["own_rows"][:, :]), [], [r_ownr])
    S.dma("pool", lambda e: e.dma_start(out=amask, in_=T["amask"][:, :].rearrange("p (a b q) -> p a b q", a=2, b=2)), [], [r_amask])
    xq = A.alloc([NB, D], F32); r_xq = S.res()
    print("phase2 arena bytes (final):", A.off)

    def proj_bank():
        b = 6 + pj[0] % 2
        pj[0] += 1
        return b

    def head_norm(src_stage, r_src, wcol0, np_=128):
        src_ = src_stage[0:np_]
        S.op("pool", lambda e: e.tensor_tensor(out=sq[0:np_], in0=src_, in1=src_, op=ALU.mult), [r_src], [r_sq])
        S.op("dve", lambda e: e.tensor_reduce(out=rs16[0:np_], in_=sq[0:np_].rearrange("p (g d) -> p g d", d=64), axis=AX.X, op=ALU.add), [r_sq], [r_rs16])
        rms_rstd(rs16[0:np_], 64, [r_rs16])
        v3 = src_.rearrange("p (g d) -> p g d", d=64)
        S.op("dve", lambda e: e.tensor_tensor(out=v3, in0=v3, in1=rs16[0:np_].unsqueeze(2).broadcast_to([np_, 16, 64]), op=ALU.mult), [r_src, r_rs16], [r_src])
        S.op("pool", lambda e: e.tensor_tensor(out=v3, in0=v3, in1=qkw[0:np_, wcol0:wcol0 + 64].unsqueeze(1).broadcast_to([np_, 16, 64]), op=ALU.mult),
             [r_src, r_qkw], [r_src])

    def proj(hT, r_hT, blk, col0, half):
        b = proj_bank()
        for c in range(8):
            S.op("pe", lambda e, c=c, b=b: e.matmul(psum[b][:, :], lhsT=hT[:, c, blk * 128:(blk + 1) * 128],
                                                    rhs=wc[:, c, col0 + half * 512: col0 + (half + 1) * 512], start=(c == 0), stop=(c == 7)),
                 [r_hT, r_wc], [rps[b]])
        return b

    def to_featmajor(dst, r_dst, blk):
        pb = ps_bf(7)
        for h in range(8):
            S.op("pe", lambda e, h=h, pb=pb: e.transpose(out=pb[:, h * 128:(h + 1) * 128], in_=nb16[:, h * 128:(h + 1) * 128], identity=ident_b),
                 [r_nb16, r_identb], [rps[7]])
        S.op("dve", lambda e, pb=pb, blk=blk: e.tensor_copy(out=dst[:, :, blk * 128:(blk + 1) * 128], in_=pb.rearrange("p (h t) -> p h t", h=8)),
             [rps[7]], [r_dst])

    for t in range(DBG["nt2"]):
        tok0 = t * TT
        S.dma("sp", lambda e, tok0=tok0: e.dma_start(out=xt2, in_=T["x1_scr"][tok0:tok0 + TT, :].rearrange("(k p) d -> p k d", p=128)), [r_x1[t]], [r_xt2])
        rmsnorm_tile(xt2, r_xt2, h1, r_h1, h1T, r_h1T, NB, TT)
        for blk in range(NB):
            row = tok0 + blk * 128
            ks, rks = stg[0], r_stg[0]
            for half in range(2):
                b = proj(h1T, r_h1T, blk, D, half)
                S.op("act", lambda e, b=b, half=half: e.copy(out=ks[:, half * 512:(half + 1) * 512], in_=psum[b][:, :]), [rps[b]], [rks])
            head_norm(ks, rks, 64)
            outs.append(S.dma("sp", lambda e, row=row: e.dma_start(out=T["k_all"][row:row + 128, :], in_=ks), [rks], []))
            S.op("act", lambda e: e.copy(out=nb16, in_=ks), [rks], [r_nb16])
            to_featmajor(ktile, r_ktile, blk)
            vs, rvs = stg[1], r_stg[1]
            for half in range(2):
                b = proj(h1T, r_h1T, blk, 2 * D, half)
                S.op("act", lambda e, b=b, half=half: e.copy(out=vs[:, half * 512:(half + 1) * 512], in_=psum[b][:, :]), [rps[b]], [rvs])
                S.op("dve", lambda e, b=b, half=half, blk=blk: e.tensor_copy(out=vtile[:, half * 4:(half + 1) * 4, blk, 0:128],
                                                                          in_=psum[b][:, :].rearrange("p (h d) -> p h d", h=4)), [rps[b]], [r_vtile])
            outs.append(S.dma("sp", lambda e, row=row: e.dma_start(out=T["v_all"][row:row + 128, :], in_=vs), [rvs], []))
        S.dma("sp", lambda e, tok0=tok0: e.dma_start(out=T["kt_scr"][:, :, tok0:tok0 + TT].rearrange("h p t -> p h t"), in_=ktile), [r_ktile], [r_kscr[t]])
        S.dma("sp", lambda e, t=t: e.dma_start(out=T["v_scr"][:, :, t * NB:(t + 1) * NB, :].rearrange("h p k d -> p h (k d)"), in_=vtile.rearrange("p h k d -> p h (k d)")), [r_vtile], [r_vscr[t]])
        if t % 2 == 0 or not DBG["attn"]:
            continue
        s_ = t // 2
        for blk in range(NB):
            S.dma("pool", lambda e, blk=blk, s_=s_: e.indirect_dma_start(out=xq[:, blk, :], out_offset=None, in_=T["x1_scr"][:, :],
                                                                     in_offset=bass.IndirectOffsetOnAxis(ap=ownr[:, s_ * 2 + blk:s_ * 2 + blk + 1], axis=0)),
                  [r_ownr, r_x1[t - 1], r_x1[t]], [r_xq])
        rmsnorm_tile(xq, r_xq, h1, r_h1, h1T, r_h1T, NB, TT)
        for blk in range(NB):
            qs, rqs = stg[0], r_stg[0]
            for half in range(2):
                b = proj(h1T, r_h1T, blk, 0, half)
                S.op("act", lambda e, b=b, half=half: e.copy(out=qs[:, half * 512:(half + 1) * 512], in_=psum[b][:, :]), [rps[b]], [rqs])
            head_norm(qs, rqs, 0)
            S.op("act", lambda e: e.copy(out=nb16, in_=qs), [rqs], [r_nb16])
            to_featmajor(QT, r_QT, blk)
            for half in range(2):
                b = proj(h1T, r_h1T, blk, 3 * D, half)
                S.op("act", lambda e, b=b, half=half, blk=blk: e.activation(out=sg[:, blk, half * 512:(half + 1) * 512], in_=psum[b][:, :], func=AF.Silu),
                     [rps[b]], [r_sg])
        nkb = (t + 1) * NB
        for h in range(8):
            ob = [0, 1]
            for qb in range(NB):
                S.op("dve", lambda e, b=ob[qb]: e.memset(psum[b][:, 0:258], 0.0), [], [rps[ob[qb]]])
            pend_pv = []
            for ch0 in range(0, nkb, KCH // 128):
                nb_ch = min(KCH // 128, nkb - ch0)
                li = nload[0] % 3
                nload[0] += 1
                tiles_in = sorted(set((ch0 + i) // NB for i in range(nb_ch)))
                S.dma("sp", lambda e, h=h, li=li, ch0=ch0, nb_ch=nb_ch: e.dma_start(out=kth[li][:, 0:nb_ch * 128],
                                                                                   in_=T["kt_scr"][h, :, ch0 * 128:(ch0 + nb_ch) * 128]),
                      [r_kscr[x] for x in tiles_in], [r_kth[li]])
                S.dma("sp", lambda e, h=h, li=li, ch0=ch0, nb_ch=nb_ch: e.dma_start(out=vh[li][:, 0:nb_ch, :],
                                                                                   in_=T["v_scr"][h, :, ch0:ch0 + nb_ch, :]),
                      [r_vscr[x] for x in tiles_in], [r_vh[li]])
                for i in range(nb_ch):
                    kb = ch0 + i
                    jd = kb - (nkb - 4)
                    sb_ = 2 + 2 * (kb % 2)
                    pi_ = npt[0] % 3
                    npt[0] += 1
                    for c in range(2):
                        S.op("pe", lambda e, c=c, li=li, i=i, h=h, sb_=sb_: e.matmul(
                            psum[sb_ + c][:, 0:TT], lhsT=kth[li][c * 64:(c + 1) * 64, i * 128:(i + 1) * 128],
                            rhs=QT[c * 64:(c + 1) * 64, h, :], start=True, stop=True), [r_kth[li], r_QT], [rps[sb_ + c]])
                    while len(pend_pv) > 0:
                        pend_pv.pop(0)()
                    pt = PT[pi_]; rpt = r_PT[pi_]
                    S.op("act", lambda e, pt=pt, sb_=sb_: e.activation(out=pt, in_=psum_all[:, sb_ * 512:(sb_ + 2) * 512].rearrange("p (c x) -> p c x", c=2)[:, :, 0:TT],
                                                                     func=AF.Exp, scale=0.125, bias=lam[:, 1:2]), [rps[sb_], rps[sb_ + 1], r_lam], [rpt])
                    if jd >= 0:
                        S.op("dve", lambda e, pt=pt, jd=jd: e.tensor_tensor(out=pt, in0=pt, in1=amask[:, jd // 2, jd % 2, :].unsqueeze(1).broadcast_to([128, 2, TT]),
                                                                           op=ALU.mult), [rpt, r_amask], [rpt])
                    def do_pv(pt=pt, rpt=rpt, li=li, i=i):
                        for qb in range(NB):
                            for c in range(2):
                                S.op("pe", lambda e, c=c, qb=qb, b=ob[qb]: e.matmul(
                                    psum[b][:, c * 129:(c + 1) * 129], lhsT=pt[:, c, qb * 128:(qb + 1) * 128], rhs=vh[li][:, i, :],
                                    start=False, stop=False, skip_group_check=True), [rpt, r_vh[li]], [rps[b]])
                    pend_pv.append(do_pv)
            while len(pend_pv) > 0:
                pend_pv.pop(0)()
            for qb in range(NB):
                b = ob[qb]
                O = psum[b][:, 0:258].rearrange("p (c d) -> p c d", c=2)
                S.op("dve", lambda e, O=O: e.reciprocal(out=sm2[:, 0:2], in_=O[:, :, 128]), [rps[b]], [r_sm2])
                S.op("dve", lambda e: e.tensor_tensor(out=sm2[:, 1:2], in0=sm2[:, 1:2], in1=lam[:, 0:1], op=ALU.mult), [r_sm2, r_lam], [r_sm2])
                S.op("act", lambda e, O=O: e.activation(out=o1, in_=O[:, 0, 0:128], func=AF.Copy, scale=sm2[:, 0:1]), [rps[b], r_sm2], [r_o1])
                S.op("dve", lambda e, O=O: e.scalar_tensor_tensor(out=o2, in0=O[:, 1, 0:128], scalar=sm2[:, 1:2], in1=o1, op0=ALU.mult, op1=ALU.add),
                     [rps[b], r_sm2, r_o1], [r_o2])
                S.op("act", lambda e: e.activation(out=junk[:, 0:128], in_=o2, func=AF.Square, accum_out=sm2[:, 2:3]), [r_o2], [r_junk, r_sm2])
                rms_rstd(sm2[:, 2:3], 128, [r_sm2])
                S.op("dve", lambda e: e.scalar_tensor_tensor(out=o2, in0=o2, scalar=sm2[:, 2:3], in1=sublnw, op0=ALU.mult, op1=ALU.mult),
                     [r_o2, r_sm2, r_subw], [r_o2])
                S.op("dve", lambda e, qb=qb, h=h: e.tensor_tensor(out=og[:, qb, h * 128:(h + 1) * 128], in0=o2, in1=sg[:, qb, h * 128:(h + 1) * 128], op=ALU.mult),
                     [r_o2, r_sg], [r_og])
        for blk in range(NB):
            pb = ps_bf(7)
            for h in range(8):
                S.op("pe", lambda e, h=h, pb=pb, blk=blk: e.transpose(out=pb[:, h * 128:(h + 1) * 128], in_=og[:, blk, h * 128:(h + 1) * 128], identity=ident_b),
                     [r_og, r_identb], [rps[7]])
            S.op("act", lambda e, pb=pb, blk=blk: e.copy(out=ogT[:, :, blk * 128:(blk + 1) * 128], in_=pb.rearrange("p (h t) -> p h t", h=8)),
                 [rps[7]], [r_ogT])
        for blk in range(NB):
            for half in range(2):
                b = proj_bank()
                for kc in range(8):
                    S.op("pe", lambda e, blk=blk, half=half, kc=kc, b=b: e.matmul(psum[b][:, :], lhsT=ogT[:, kc, blk * 128:(blk + 1) * 128],
                                                                                rhs=woc[:, kc, half * 512:(half + 1) * 512], start=(kc == 0), stop=(kc == 7)),
                         [r_ogT, r_woc], [rps[b]])
                S.op("dve", lambda e, blk=blk, half=half, b=b: e.tensor_tensor(out=xq[:, blk, half * 512:(half + 1) * 512], in0=psum[b][:, :],
                                                                             in1=xq[:, blk, half * 512:(half + 1) * 512], op=ALU.add), [rps[b], r_xq], [r_xq])
        outs.append(S.dma("sp", lambda e, s_=s_: e.dma_start(out=T["y_own"][s_ * TT:(s_ + 1) * TT, :].rearrange("(k p) d -> p k d", p=128), in_=xq), [r_xq], []))

    def sample_l1():
        NS = 16
        ktn = A.alloc([8, NS], BF16); r_ktn = S.res()
        QTs = A.alloc([8, NS], BF16); r_QTs = S.res()
        vnew = A.alloc([8, 129], BF16); r_vnew = S.res()
        kpg = [A.alloc([D], BF16) for _ in range(2)]; r_kpg = [S.res() for _ in range(2)]
        vpg = [A.alloc([8, 129], BF16) for _ in range(2)]; r_vpg = [S.res() for _ in range(2)]
        ktp = [A.alloc([8, 128], BF16) for _ in range(2)]; r_ktp = [S.res() for _ in range(2)]
        vfl = [A.alloc([D], BF16) for _ in range(2)]; r_vfl = [S.res() for _ in range(2)]
        pte = [[A.alloc([8, 2, NS], BF16) for _ in range(2)] for _ in range(4)]
        r_pte = [[S.res() for _ in range(2)] for _ in range(4)]
        ptn = A.alloc([8, 2, NS], BF16); r_ptn = S.res()
        pti = A.alloc([256], I32); r_pti = S.res()
        ptf = A.alloc([256], F32); r_ptf = S.res()
        idx = A.alloc([256], I32); r_idx = S.res()
        smf = A.alloc([NS], F32); r_smf = S.res()
        smb = A.alloc([NS], BF16); r_smb = S.res()
        rr = A.alloc([16], F32); r_rr = S.res()
        print("sample-L1 arena bytes:", A.off)
        O1 = xq.rearrange("p a d -> p (a d)")[:, 0:1032]; r_O1 = r_xq
        O2 = xt2.rearrange("p a d -> p (a d)")[:, 0:1032]; r_O2 = r_xt2

        S.op("pool", lambda e: e.memset(vnew, 1.0), [], [r_vnew])
        for i in range(2):
            S.op("pool", lambda e, i=i: e.memset(vpg[i], 1.0), [], [r_vpg[i]])
        for e_ in range(4):
            for i in range(2):
                S.op("pool", lambda e, e_=e_, i=i: e.memset(pte[e_][i], 0.0), [], [r_pte[e_][i]])
        S.dma("sp", lambda e: e.dma_start(out=pti, in_=T["pt"][0:1, :].partition_broadcast(128)), [], [r_pti])
        S.op("dve", lambda e: e.tensor_copy(out=ptf, in_=pti), [r_pti], [r_ptf])
        S.op("dve", lambda e: e.scalar_tensor_tensor(out=ptf, in0=ptf, scalar=128.0, in1=iot[:, 0:1].broadcast_to([128, 256]), op0=ALU.mult, op1=ALU.subtract),
             [r_ptf, r_iot], [r_ptf])
        S.op("dve", lambda e: e.tensor_copy(out=idx, in_=ptf), [r_ptf], [r_idx])
        S.dma("sp", lambda e: e.dma_start(out=smf[0:NS, :], in_=T["smask"][:, :]), [], [r_smf])
        S.op("dve", lambda e: e.tensor_copy(out=smb[0:NS, :], in_=smf[0:NS, :]), [r_smf], [r_smb])

        ss = small[0:NS, 9:10]
        S.op("act", lambda e: e.activation(out=junk[0:NS, :], in_=x1s[0:NS, :], func=AF.Square, accum_out=ss), [r_x1s], [r_junk, r_small])
        rms_rstd(ss, D, [r_small])
        S.op("dve", lambda e: e.scalar_tensor_tensor(out=h1[0:NS, 0, :], in0=x1s[0:NS, :], scalar=ss, in1=nw[0:NS, :], op0=ALU.mult, op1=ALU.mult),
             [r_x1s, r_small, r_nw], [r_h1])
        pb = ps_bf(7)
        for c in range(8):
            S.op("pe", lambda e, c=c: e.transpose(out=pb[:, c * NS:(c + 1) * NS], in_=h1[0:NS, 0, c * 128:(c + 1) * 128], identity=ident_b[0:NS, 0:NS]),
                 [r_h1, r_identb], [rps[7]])
        S.op("act", lambda e: e.copy(out=h1T[:, :, 0:NS], in_=pb[:, 0:8 * NS].rearrange("p (c t) -> p c t", c=8)), [rps[7]], [r_h1T])

        def sproj(col0, half):
            b = proj_bank()
            for c in range(8):
                S.op("pe", lambda e, c=c, b=b: e.matmul(psum[b][0:NS, :], lhsT=h1T[:, c, 0:NS], rhs=wc[:, c, col0 + half * 512: col0 + (half + 1) * 512],
                                                        start=(c == 0), stop=(c == 7)), [r_h1T, r_wc], [rps[b]])
            return b

        def heads_T(dst, r_dst):
            pb_ = ps_bf(7)
            for h in range(8):
                S.op("pe", lambda e, h=h: e.transpose(out=pb_[:, h * NS:(h + 1) * NS], in_=nb16[0:NS, h * 128:(h + 1) * 128], identity=ident_b[0:NS, 0:NS]),
                     [r_nb16, r_identb], [rps[7]])
            S.op("dve", lambda e: e.tensor_copy(out=dst, in_=pb_[:, 0:8 * NS].rearrange("p (h t) -> p h t", h=8)), [rps[7]], [r_dst])

        ks, rks = stg[0], r_stg[0]
        vs, rvs = stg[1], r_stg[1]
        for half in range(2):
            b = sproj(D, half)
            S.op("act", lambda e, b=b, half=half: e.copy(out=ks[0:NS, half * 512:(half + 1) * 512], in_=psum[b][0:NS, :]), [rps[b]], [rks])
        head_norm(ks, rks, 64, NS)
        outs.append(S.dma("sp", lambda e: e.dma_start(out=T["ks_out"][:, :], in_=ks[0:NS, :]), [rks], []))
        S.op("act", lambda e: e.copy(out=nb16[0:NS, :], in_=ks[0:NS, :]), [rks], [r_nb16])
        heads_T(ktn, r_ktn)
        for half in range(2):
            b = sproj(2 * D, half)
            S.op("act", lambda e, b=b, half=half: e.copy(out=vs[0:NS, half * 512:(half + 1) * 512], in_=psum[b][0:NS, :]), [rps[b]], [rvs])
        S.op("dve", lambda e: e.tensor_copy(out=vnew[0:NS, :, 0:128], in_=vs[0:NS, :].rearrange("p (h d) -> p h d", h=8)), [rvs], [r_vnew])
        outs.append(S.dma("sp", lambda e: e.dma_start(out=T["vs_out"][:, :], in_=vs[0:NS, :]), [rvs], []))
        for half in range(2):
            b = sproj(0, half)
            S.op("act", lambda e, b=b, half=half: e.copy(out=ks[0:NS, half * 512:(half + 1) * 512], in_=psum[b][0:NS, :]), [rps[b]], [rks])
        head_norm(ks, rks, 0, NS)
        S.op("act", lambda e: e.copy(out=nb16[0:NS, :], in_=ks[0:NS, :]), [rks], [r_nb16])
        heads_T(QTs, r_QTs)
        for half in range(2):
            b = sproj(3 * D, half)
            S.op("act", lambda e, b=b, half=half: e.activation(out=sg[0:NS, 0, half * 512:(half + 1) * 512], in_=psum[b][0:NS, :], func=AF.Silu), [rps[b]], [r_sg])

        for b in range(3):
            S.op("dve", lambda e, b=b: e.memset(psum[b][0:32, :], 0.0), [], [rps[b]])

        def pv(P2d_of_h, rP, vt, r_vt, np_):
            for h in range(8):
                S.op("pe", lambda e, h=h: e.matmul(psum[h // 3][0:32, (h % 3) * 129:(h % 3 + 1) * 129], lhsT=P2d_of_h(h), rhs=vt[0:np_, h, :],
                                                   start=False, stop=False, skip_group_check=True), [rP, r_vt], [rps[h // 3]])

        npages = DBG.get("npg", NPAGE)
        ck = T["cache_k"]
        cv3 = T["cache_v"].rearrange("r (h d) -> r h d", h=8)
        def stage_a(e_, j):
            if True:
                if True:
                    gi = e_ * NPAGE + j
                    bi = gi % 2
                    S.dma("pool", lambda e, gi=gi, bi=bi: e.indirect_dma_start(out=kpg[bi], out_offset=None, in_=ck[:, :],
                                                                            in_offset=bass.IndirectOffsetOnAxis(ap=idx[:, gi:gi + 1], axis=0)), [r_idx], [r_kpg[bi]])
                    S.dma("pool", lambda e, gi=gi, bi=bi: e.indirect_dma_start(out=vfl[bi], out_offset=None, in_=T["cache_v"][:, :],
                                                                            in_offset=bass.IndirectOffsetOnAxis(ap=idx[:, gi:gi + 1], axis=0)), [r_idx], [r_vfl[bi]])
                    S.op("act" if gi % 2 else "dve",
                         (lambda e, bi=bi: e.copy(out=vpg[bi][:, :, 0:128], in_=vfl[bi].rearrange("p (h d) -> p h d", h=8))) if gi % 2 else
                         (lambda e, bi=bi: e.tensor_copy(out=vpg[bi][:, :, 0:128], in_=vfl[bi].rearrange("p (h d) -> p h d", h=8))), [r_vfl[bi]], [r_vpg[bi]])
                    pb7 = ps_bf(7)
                    for h in range(8):
                        S.op("pe", lambda e, h=h, bi=bi: e.transpose(out=pb7[:, h * 128:(h + 1) * 128], in_=kpg[bi][:, h * 128:(h + 1) * 128], identity=ident_b),
                             [r_kpg[bi], r_identb], [rps[7]])
                    S.op("dve" if gi % 2 else "act",
                         (lambda e, bi=bi: e.tensor_copy(out=ktp[bi], in_=pb7.rearrange("p (h t) -> p h t", h=8))) if gi % 2 else
                         (lambda e, bi=bi: e.copy(out=ktp[bi], in_=pb7.rearrange("p (h t) -> p h t", h=8))), [rps[7]], [r_ktp[bi]])

        def stage_b(e_, j):
            if True:
                if True:
                    gi = e_ * NPAGE + j
                    bi = gi % 2
                    sbk = 3 + 2 * (gi % 2)
                    for h in range(8):
                        for c in range(2):
                            S.op("pe", lambda e, h=h, c=c, bi=bi, sbk=sbk, e_=e_: e.matmul(psum[sbk + c][:, h * 4:(h + 1) * 4], lhsT=ktp[bi][c * 64:(c + 1) * 64, h, :],
                                                                                        rhs=QTs[c * 64:(c + 1) * 64, h, e_ * 4:(e_ + 1) * 4], start=True, stop=True),
                                 [r_ktp[bi], r_QTs], [rps[sbk + c]])
                    P = pte[e_][bi]; rP = r_pte[e_][bi]
                    for c in range(2):
                        S.op("act", lambda e, c=c, P=P, sbk=sbk, e_=e_: e.activation(out=P[:, :, c, e_ * 4:(e_ + 1) * 4], in_=psum[sbk + c][:, 0:32].rearrange("p (h q) -> p h q", h=8),
                                                                                 func=AF.Exp, scale=0.125, bias=lam[:, 1:2]), [rps[sbk + c], r_lam], [rP])
                    pv(lambda h, P=P: P[:, h].rearrange("p c q -> p (c q)"), rP, vpg[bi], r_vpg[bi], 128)

        pages = [(e_, j) for e_ in range(4) for j in range(npages)]
        for n_, (e_, j) in enumerate(pages):
            stage_a(e_, j)
            if n_ >= 1:
                stage_b(*pages[n_ - 1])
        if pages:
            stage_b(*pages[-1])
        for h in range(8):
            for c in range(2):
                S.op("pe", lambda e, h=h, c=c: e.matmul(psum[3 + c][0:NS, h * NS:(h + 1) * NS], lhsT=ktn[c * 64:(c + 1) * 64, h, :], rhs=QTs[c * 64:(c + 1) * 64, h, :],
                                                        start=True, stop=True), [r_ktn, r_QTs], [rps[3 + c]])
        for c in range(2):
            S.op("act", lambda e, c=c: e.activation(out=ptn[0:NS, :, c, :], in_=psum[3 + c][0:NS, 0:8 * NS].rearrange("p (h q) -> p h q", h=8),
                                                    func=AF.Exp, scale=0.125, bias=lam[0:NS, 1:2]), [rps[3 + c], r_lam], [r_ptn])
            S.op("dve", lambda e, c=c: e.tensor_tensor(out=ptn[0:NS, :, c, :], in0=ptn[0:NS, :, c, :], in1=smb[0:NS, :].unsqueeze(1).broadcast_to([NS, 8, NS]), op=ALU.mult),
                 [r_ptn, r_smb], [r_ptn])
        pv(lambda h: ptn[0:NS, h].rearrange("p c q -> p (c q)"), r_ptn, vnew, r_vnew, NS)

        for b in range(3):
            n = 387 if b < 2 else 258
            S.op("act" if b % 2 else "dve",
                 (lambda e, b=b, n=n: e.copy(out=O1[0:32, b * 387:b * 387 + n], in_=psum[b][0:32, 0:n])) if b % 2 else
                 (lambda e, b=b, n=n: e.tensor_copy(out=O1[0:32, b * 387:b * 387 + n], in_=psum[b][0:32, 0:n])), [rps[b]], [r_O1])
        for i, (c0, n) in enumerate(((0, 512), (512, 512), (1024, 8))):
            S.op("pe", lambda e, i=i, c0=c0, n=n: e.matmul(psum[3 + i][0:NS, 0:n], lhsT=ident_f[0:32, 16:32], rhs=O1[0:32, c0:c0 + n], start=True, stop=True),
                 [r_O1, r_identf], [rps[3 + i]])
            S.op("dve", lambda e, i=i, c0=c0, n=n: e.tensor_copy(out=O2[0:NS, c0:c0 + n], in_=psum[3 + i][0:NS, 0:n]), [rps[3 + i]], [r_O2])
        O1v = O1[0:NS, :].rearrange("p (h d) -> p h d", h=8)
        O2v = O2[0:NS, :].rearrange("p (h d) -> p h d", h=8)
        S.op("dve", lambda e: e.reciprocal(out=rr[0:NS, 0:8], in_=O1v[:, :, 128]), [r_O1], [r_rr])
        S.op("dve", lambda e: e.reciprocal(out=rr[0:NS, 8:16], in_=O2v[:, :, 128]), [r_O2], [r_rr])
        S.op("dve", lambda e: e.tensor_scalar(out=rr[0:NS, 8:16], in0=rr[0:NS, 8:16], scalar1=lam[0:NS, 0:1], scalar2=None, op0=ALU.mult), [r_rr, r_lam], [r_rr])
        o_a = ks[0:NS, :].rearrange("p (h d) -> p h d", h=8)
        o_b = vs[0:NS, :].rearrange("p (h d) -> p h d", h=8)
        S.op("dve", lambda e: e.tensor_tensor(out=o_a, in0=O1v[:, :, 0:128], in1=rr[0:NS, 0:8].unsqueeze(2).broadcast_to([NS, 8, 128]), op=ALU.mult), [r_O1, r_rr], [rks])
        S.op("dve", lambda e: e.tensor_tensor(out=o_b, in0=O2v[:, :, 0:128], in1=rr[0:NS, 8:16].unsqueeze(2).broadcast_to([NS, 8, 128]), op=ALU.mult), [r_O2, r_rr], [rvs])
        S.op("dve", lambda e: e.tensor_tensor(out=o_a, in0=o_a, in1=o_b, op=ALU.add), [rks, rvs], [rks])
        S.op("pool", lambda e: e.tensor_tensor(out=sq[0:NS], in0=ks[0:NS], in1=ks[0:NS], op=ALU.mult), [rks], [r_sq])
        S.op("dve", lambda e: e.tensor_reduce(out=rs16[0:NS, 0:8], in_=sq[0:NS].rearrange("p (h d) -> p h d", h=8), axis=AX.X, op=ALU.add), [r_sq], [r_rs16])
        rms_rstd(rs16[0:NS, 0:8], 128, [r_rs16])
        S.op("dve", lambda e: e.tensor_tensor(out=o_a, in0=o_a, in1=rs16[0:NS, 0:8].unsqueeze(2).broadcast_to([NS, 8, 128]), op=ALU.mult), [rks, r_rs16], [rks])
        S.op("dve", lambda e: e.tensor_tensor(out=o_a, in0=o_a, in1=sublnw[0:NS, :].unsqueeze(1).broadcast_to([NS, 8, 128]), op=ALU.mult), [rks, r_subw], [rks])
        S.op("dve", lambda e: e.tensor_tensor(out=og[0:NS, 0, :], in0=ks[0:NS, :], in1=sg[0:NS, 0, :], op=ALU.mult), [rks, r_sg], [r_og])
        pb_ = ps_bf(7)
        for h in range(8):
            S.op("pe", lambda e, h=h: e.transpose(out=pb_[:, h * NS:(h + 1) * NS], in_=og[0:NS, 0, h * 128:(h + 1) * 128], identity=ident_b[0:NS, 0:NS]),
                 [r_og, r_identb], [rps[7]])
        S.op("act", lambda e: e.copy(out=ogT[:, :, 0:NS], in_=pb_[:, 0:8 * NS].rearrange("p (h t) -> p h t", h=8)), [rps[7]], [r_ogT])
        for half in range(2):
            b = 5 + half
            for kc in range(8):
                S.op("pe", lambda e, half=half, kc=kc, b=b: e.matmul(psum[b][0:NS, :], lhsT=ogT[:, kc, 0:NS], rhs=woc[:, kc, half * 512:(half + 1) * 512],
                                                                   start=(kc == 0), stop=(kc == 7)), [r_ogT, r_woc], [rps[b]])
            S.op("dve", lambda e, half=half, b=b: e.tensor_tensor(out=vs[0:NS, half * 512:(half + 1) * 512], in0=psum[b][0:NS, :],
                                                                in1=x1s[0:NS, half * 512:(half + 1) * 512], op=ALU.add), [rps[b], r_x1s], [rvs])
        outs.append(S.dma("sp", lambda e: e.dma_start(out=T["ys_out"][:, :], in_=vs[0:NS, :]), [rvs], []))

    if DBG["st"]:
        sample_l1()

    S.emit(nc, final_waits=outs)
    st.close()
    return nc


_CACHE = {}
_HOOK = [None]


def _get_nc():
    if "nc" not in _CACHE:
        nc, S, T = build_nc()
        emit_program(nc, S, T)
        _CACHE["nc"] = nc
    return _CACHE["nc"]


def kernel(x_prompt, x_sample, state_pool, state_conv, cache_k, cache_v, page_table,
           norm_w_ab, w_in_ab, pool_w, pool_scale, conv_w, conv_b, conv_ln_w, conv_ln_b, w_out_ab,
           norm_w_c, w_in_c, q_norm_w, k_norm_w, lambda_q1, lambda_k1, lambda_q2, lambda_k2,
           subln_w, w_out_c):
    f = lambda a: np.ascontiguousarray(np.asarray(a, dtype=np.float32))
    x_prompt = f(x_prompt)
    pvec_src = np.ascontiguousarray(np.concatenate([f(conv_w)[0], f(conv_b), f(conv_ln_w), f(conv_ln_b), f(pool_scale)], axis=0))
    qk_w = np.ascontiguousarray(np.concatenate([f(q_norm_w), f(k_norm_w)], axis=1))
    lam_v = np.ascontiguousarray(np.concatenate([f(lambda_q1), f(lambda_k1), f(lambda_q2), f(lambda_k2)], axis=1))
    shared = {
        "norm_w_ab": f(norm_w_ab), "w_in_ab": f(w_in_ab)[0], "pool_w": f(pool_w)[0], "pvec_src": pvec_src,
        "w_out_ab": f(w_out_ab)[0], "norm_w_c": f(norm_w_c), "w_in_c": f(w_in_c)[0], "qk_w": qk_w, "lam_v": lam_v,
        "subln_w": f(subln_w), "w_out_c": f(w_out_c)[0],
    }
    p = np.arange(128)
    x_sample = f(x_sample); state_pool = f(state_pool); state_conv = f(state_conv)
    ck2 = f(cache_k).reshape(-1, D)
    cv2 = f(cache_v).reshape(-1, D)
    kk = np.arange(16)
    smask = ((kk[:, None] // 4 == kk[None, :] // 4) & (kk[:, None] % 4 <= kk[None, :] % 4)).astype(np.float32)
    in_maps = []
    for c in range(8):
        b, hf = c // 2, c % 2
        own_rows = np.zeros((128, 16), np.int32)
        for s_ in range(8):
            for blk in range(2):
                own_rows[:, s_ * 2 + blk] = (2 * s_ + hf) * TT + blk * 128 + p
        am = np.zeros((128, 2, 2, TT), np.float32)
        q = np.arange(TT)
        for kt2 in range(2):
            for kblk in range(2):
                am[:, kt2, kblk, :] = ((kt2 - hf) * TT + kblk * 128 + p[:, None] <= q[None, :]).astype(np.float32)
        m = dict(shared)
        m["xs"] = np.ascontiguousarray(x_sample[4 * c:4 * c + 4].reshape(16, D))
        m["spool"] = np.ascontiguousarray(state_pool[0, 4 * c:4 * c + 4])
        m["sconv"] = np.ascontiguousarray(state_conv[0, 4 * c:4 * c + 4])
        m["pt"] = np.ascontiguousarray(np.asarray(page_table, dtype=np.int32)[4 * c:4 * c + 4].reshape(1, 256))
        m["smask"] = smask
        m["cache_k"] = ck2
        m["cache_v"] = cv2
        m["xp"] = x_prompt[b]
        m["own_rows"] = own_rows
        m["amask"] = am.reshape(128, 4 * TT)
        in_maps.append(m)
    nc = _get_nc()
    res = run_bass_kernel_spmd(nc, in_maps, core_ids=list(range(8)))
    R = res.results
    if _HOOK[0] is not None:
        _HOOK[0](R)
    y_prompt = np.zeros((NBATCH, SEQ, D), np.float32)
    k_p = np.zeros((1, NBATCH, SEQ, 8, 128), np.float32)
    v_p = np.zeros((1, NBATCH, SEQ, 8, 128), np.float32)
    pool_p = np.zeros((1, NBATCH, 15, D), np.float32)
    conv_p = np.zeros((1, NBATCH, 30, D), np.float32)
    for c in range(8):
        b, hf = c // 2, c % 2
        yo = R[c]["y_own"].reshape(8, TT, D)
        for s_ in range(8):
            t = 2 * s_ + hf
            y_prompt[b, t * TT:(t + 1) * TT] = yo[s_]
        if hf == 0:
            k_p[0, b] = R[c]["k_all"].reshape(SEQ, 8, 128)
            v_p[0, b] = R[c]["v_all"].reshape(SEQ, 8, 128)
            pool_p[0, b] = R[c]["pool_st"]
            conv_p[0, b] = R[c]["conv_st"]
    ys = np.zeros((32, 4, D), np.float32)
    pool_s = np.zeros((1, 32, 15, D), np.float32)
    conv_s = np.zeros((1, 32, 30, D), np.float32)
    k_s = np.zeros((1, 32, 4, 8, 128), np.float32)
    v_s = np.zeros((1, 32, 4, 8, 128), np.float32)
    for c in range(8):
        ys[4 * c:4 * c + 4] = R[c]["ys_out"].reshape(4, 4, D)
        pool_s[0, 4 * c:4 * c + 4] = R[c]["pool_s_out"]
        conv_s[0, 4 * c:4 * c + 4] = R[c]["conv_s_out"]
        k_s[0, 4 * c:4 * c + 4] = R[c]["ks_out"].reshape(4, 4, 8, 128)
        v_s[0, 4 * c:4 * c + 4] = R[c]["vs_out"].reshape(4, 4, 8, 128)
    return (y_prompt, ys, pool_p, conv_p, k_p, v_p, pool_s, conv_s, k_s, v_s)
```

```python
import contextlib
import math
import os
import numpy as np
import concourse.bass as bass
import concourse.mybir as mybir
from concourse.bass_utils import run_bass_kernel_spmd

F32 = mybir.dt.float32
BF16 = mybir.dt.bfloat16
I32 = mybir.dt.int32
U8 = mybir.dt.uint8
ALU = mybir.AluOpType
AF = mybir.ActivationFunctionType
AX = mybir.AxisListType

NDSEM = 12

D = 1024
SEQ = 4096
NBATCH = 4
TT = 256
NT = SEQ // TT
NB = TT // 128
EPS = 1e-6
LAMBDA_INIT = 0.8 - 0.6 * math.exp(-0.3 * 1)
OWN = {0: [0, 15, 2, 13, 4, 11, 6, 9], 1: [1, 14, 3, 12, 5, 10, 7, 8]}
POOL_W = (2, 4, 8, 16)
NPOOL = int(os.environ.get("K_NPOOL", 2560))
NPAGE = 64
import os
DBG = {"nt1": int(os.environ.get("K_NT1", NT)), "nt2": int(os.environ.get("K_NT2", NT)), "attn": int(os.environ.get("K_ATTN", 1)), "st": int(os.environ.get("K_ST", 1)), "npg": int(os.environ.get("K_NPG", NPAGE))}


class Res:
    __slots__ = ("name", "w", "r", "excl")

    def __init__(self, name="", excl=False):
        self.name = name
        self.w = None
        self.r = []
        self.excl = excl


class Sched:
    ENGS = ("pe", "act", "dve", "pool", "sp")

    def __init__(self):
        self.ops = {e: [] for e in self.ENGS}
        self.ndma = {e: 0 for e in self.ENGS}
        self.allres = []
        self.pending = {e: set() for e in self.ENGS}

    def res(self, name="", excl=False):
        r = Res(name, excl)
        self.allres.append(r)
        return r

    def barrier(self):
        b = set()
        for r in self.allres:
            if r.w is not None:
                b.add(r.w)
            b.update(r.r)
        for e in self.ENGS:
            self.pending[e] |= b

    def _deps(self, eng, reads, writes, is_dma):
        deps = set()
        for r in reads:
            if r.w is not None:
                w = r.w
                if w[0] == "c" and w[1] == eng and eng == "pe" and not is_dma:
                    continue
                deps.add(w)
        for wr in writes:
            cands = list(wr.r)
            if wr.w is not None:
                cands.append(wr.w)
            for w in cands:
                if w[0] == "c" and w[1] == eng and not is_dma:
                    continue
                deps.add(w)
        if self.pending[eng]:
            for w in self.pending[eng]:
                if w[0] == "c" and w[1] == eng and not is_dma:
                    continue
                deps.add(w)
            self.pending[eng] = set()
        return deps

    def _commit(self, me, reads, writes):
        for r in reads:
            r.r.append(me)
        for w in writes:
            w.w = me
            w.r = []

    def op(self, eng, fn, reads=(), writes=()):
        writes = list(writes) + [r for r in reads if r.excl]
        reads = [r for r in reads if not r.excl]
        deps = self._deps(eng, reads, writes, False)
        idx = len(self.ops[eng])
        self.ops[eng].append({"fn": fn, "deps": deps, "dma": None})
        me = ("c", eng, idx)
        self._commit(me, reads, writes)
        return me

    def dma(self, eng, fn, reads=(), writes=()):
        deps = self._deps(eng, reads, writes, True)
        k = self.ndma[eng]
        self.ndma[eng] += 1
        if k >= NDSEM:
            deps.add(("d", eng, k - NDSEM))
        self.ops[eng].append({"fn": fn, "deps": deps, "dma": k})
        me = ("d", eng, k)
        self._commit(me, reads, writes)
        return me

    def emit(self, nc, final_waits=()):
        need = {e: set() for e in self.ENGS}
        for e in self.ENGS:
            for o in self.ops[e]:
                for d in o["deps"]:
                    if d[0] == "c":
                        need[d[1]].add(d[2])
        for d in final_waits:
            if d[0] == "c":
                need[d[1]].add(d[2])
        cnt = {}
        for e in self.ENGS:
            c = 0
            m = {}
            for i in sorted(need[e]):
                c += 1
                m[i] = c
            cnt[e] = m
        with contextlib.ExitStack() as st:
            csem = {e: st.enter_context(nc.semaphore("c_" + e)) for e in self.ENGS}
            dsem = {e: [st.enter_context(nc.semaphore("d_%s_%d" % (e, i))) for i in range(NDSEM)]
                    for e in self.ENGS if self.ndma[e] > 0}
            block = st.enter_context(nc.Block())

            def body(e, engine):
                waited = {}

                def do_wait(d):
                    if d[0] == "c":
                        key = ("c", d[1]); sem = csem[d[1]]; val = cnt[d[1]][d[2]]
                    else:
                        key = ("d", d[1], d[2] % NDSEM); sem = dsem[d[1]][d[2] % NDSEM]
                        val = 16 * (d[2] // NDSEM + 1)
                    if waited.get(key, 0) >= val:
                        return
                    waited[key] = val
                    engine.wait_ge(sem, val)

                def resolve(d):
                    if d[0] == "c":
                        return ("c", d[1]), cnt[d[1]][d[2]]
                    return ("d", d[1], d[2] % NDSEM), 16 * (d[2] // NDSEM + 1)

                for i, o in enumerate(self.ops[e]):
                    best = {}
                    for d in o["deps"]:
                        key, val = resolve(d)
                        if val > best.get(key, (0, None))[0]:
                            best[key] = (val, d)
                    for key in sorted(best):
                        do_wait(best[key][1])
                    ins = o["fn"](engine)
                    if o["dma"] is not None:
                        ins.then_inc(dsem[e][o["dma"] % NDSEM], 16)
                    elif i in cnt[e]:
                        ins.then_inc(csem[e], 1)
                if e == "sp":
                    for d in final_waits:
                        do_wait(d)

            block.tensor(lambda eng: body("pe", eng))
            block.scalar(lambda eng: body("act", eng))
            block.vector(lambda eng: body("dve", eng))
            block.gpsimd(lambda eng: body("pool", eng))
            block.sync(lambda eng: body("sp", eng))


class Arena:
    def __init__(self, nc, st, nbytes):
        self.t = st.enter_context(nc.sbuf_tensor("arena", [128, nbytes], U8))
        self.n = nbytes
        self.off = 0
        self.peak = 0

    def alloc(self, shape, dt):
        esz = 4 if dt in (F32, I32) else 2
        n = esz
        for s in shape:
            n *= s
        n = (n + 31) // 32 * 32
        assert self.off + n <= self.n, "arena overflow: %d + %d > %d" % (self.off, n, self.n)
        v = self.t[:, self.off:self.off + n].bitcast(dt)
        tot = 1
        for s in shape:
            tot *= s
        v = v[:, 0:tot]
        if len(shape) == 2:
            v = v.rearrange("p (a b) -> p a b", a=shape[0])
        elif len(shape) == 3:
            v = v.rearrange("p (a b c) -> p a b c", a=shape[0], b=shape[1])
        self.off += n
        self.peak = max(self.peak, self.off)
        return v

    def mark(self):
        return self.off

    def reset(self, m):
        self.off = m


def build_nc():
    nc = bass.Bass("TRN2", target_bir_lowering=False)
    S = Sched()
    dt_in = lambda name, shape, dt=F32: nc.dram_tensor(name, shape, dt, kind="ExternalInput").ap()
    dt_out = lambda name, shape, dt=F32: nc.dram_tensor(name, shape, dt, kind="ExternalOutput").ap()
    dt_int = lambda name, shape, dt: nc.dram_tensor(name, shape, dt, kind="Internal").ap()

    xp = dt_in("xp", [SEQ, D])
    own_idx_unused = None
    norm_w_ab = dt_in("norm_w_ab", [1, D])
    w_in_ab = dt_in("w_in_ab", [D, 5 * D])
    pool_w = dt_in("pool_w", [4, 256, 256])
    pvec_src = dt_in("pvec_src", [35, D])
    w_out_ab = dt_in("w_out_ab", [2 * D, D])
    norm_w_c = dt_in("norm_w_c", [1, D])
    w_in_c = dt_in("w_in_c", [D, 4 * D])
    qk_w = dt_in("qk_w", [1, 128])
    lam_v = dt_in("lam_v", [1, 256])
    subln_w = dt_in("subln_w", [1, 128])
    w_out_c = dt_in("w_out_c", [D, D])
    own_tiles = None

    xs = dt_in("xs", [16, D])
    spool = dt_in("spool", [4, 15, D])
    sconv = dt_in("sconv", [4, 30, D])
    pt = dt_in("pt", [1, 256], I32)
    smask = dt_in("smask", [16, 16])
    cache_k = dt_in("cache_k", [NPOOL * 128, D])
    cache_v = dt_in("cache_v", [NPOOL * 128, D])
    ys_out = dt_out("ys_out", [16, D])
    pool_s_out = dt_out("pool_s_out", [4, 15, D])
    conv_s_out = dt_out("conv_s_out", [4, 30, D])
    ks_out = dt_out("ks_out", [16, D])
    vs_out = dt_out("vs_out", [16, D])
    own_rows = dt_in("own_rows", [128, 16], I32)
    amask = dt_in("amask", [128, 4 * TT])
    y_own = dt_out("y_own", [SEQ // 2, D])
    k_all = dt_out("k_all", [SEQ, D])
    v_all = dt_out("v_all", [SEQ, D])
    pool_st = dt_out("pool_st", [15, D])
    conv_st = dt_out("conv_st", [30, D])

    if os.environ.get("K_DUMP"):
        d_h0T = dt_out("d_h0T", [128, 8 * TT], BF16)
        d_win = dt_out("d_win", [128, 1024], BF16)
        d_h0 = dt_out("d_h0", [128, NB * D], BF16)
        d_xae = dt_out("d_xae", [128, 2 * (15 + TT)], F32)
    x1_scr = dt_int("x1_scr", [SEQ, D], F32)
    kt_scr = dt_int("kt_scr", [8, 128, SEQ], BF16)
    v_scr = dt_int("v_scr", [8, 128, SEQ // 128, 129], BF16)

    return nc, S, locals()


def emit_program(nc, S, T):
    xp = T["xp"]
    outs = []
    st = contextlib.ExitStack()
    A = Arena(nc, st, 211968)
    psum_all = st.enter_context(nc.psum_tensor("ps_all", [128, 4096], F32))
    psum = [psum_all[:, i * 512:(i + 1) * 512] for i in range(8)]
    rps = [S.res("ps%d" % i, excl=True) for i in range(8)]

    def ps_bf(i):
        return psum[i][:, :].bitcast(BF16)

    ident_f = A.alloc([128], F32); r_identf = S.res()
    ident_b = A.alloc([128], BF16); r_identb = S.res()
    mask_b = A.alloc([128], BF16); r_mask = S.res()
    ones_b = A.alloc([128], BF16); r_ones = S.res()
    iot = A.alloc([128], F32); r_iot = S.res()
    pvec = A.alloc([8, 35], F32); r_pvec = S.res()
    nw = A.alloc([D], F32); r_nw = S.res()
    qkw = A.alloc([128], F32); r_qkw = S.res()
    sublnw = A.alloc([128], F32); r_subw = S.res()
    lamv = A.alloc([256], F32); r_lamv = S.res()
    lam = A.alloc([4], F32); r_lam = S.res()
    invc = A.alloc([4, 16], F32); r_invc = S.res()
    small = A.alloc([64], F32); r_small = S.res()
    junk = A.alloc([D], BF16); r_junk = S.res()

    S.op("pool", lambda e: e.iota(out=iot, pattern=[[1, 128]], base=0, channel_multiplier=-1,
                                  allow_small_or_imprecise_dtypes=True), [], [r_iot])
    S.op("dve", lambda e: e.tensor_single_scalar(out=ident_f, in_=iot, scalar=0.0, op=ALU.is_equal), [r_iot], [r_identf])
    S.op("dve", lambda e: e.tensor_copy(out=ident_b, in_=ident_f), [r_identf], [r_identb])
    S.op("dve", lambda e: e.tensor_single_scalar(out=mask_b, in_=iot, scalar=0.0, op=ALU.is_ge), [r_iot], [r_mask])
    S.op("dve", lambda e: e.memset(ones_b, 1.0), [], [r_ones])
    iot2 = A.alloc([16], F32); r_iot2 = S.res()
    S.op("pool", lambda e: e.iota(out=iot2, pattern=[[1, 16]], base=1, channel_multiplier=0,
                                  allow_small_or_imprecise_dtypes=True), [], [r_iot2])
    for g, w in enumerate(POOL_W):
        S.op("dve", lambda e, g=g, w=w: e.tensor_scalar(out=invc[:, g, :], in0=iot2, scalar1=float(w), scalar2=None, op0=ALU.min),
             [r_iot2], [r_invc])
    S.op("dve", lambda e: e.reciprocal(out=invc, in_=invc), [r_invc], [r_invc])

    m0 = A.mark()
    t35 = A.alloc([D], F32); r_t35 = S.res()
    S.dma("sp", lambda e: e.dma_start(out=t35[0:35, :], in_=T["pvec_src"][:, :]), [], [r_t35])
    for c in range(8):
        S.op("pe", lambda e, c=c: e.transpose(out=psum[0][:, c * 35:(c + 1) * 35], in_=t35[0:35, c * 128:(c + 1) * 128],
                                              identity=ident_f[0:35, 0:35]), [r_t35, r_identf], [rps[0]])
    S.op("dve", lambda e: e.tensor_copy(out=pvec, in_=psum[0][:, 0:280].rearrange("p (c k) -> p c k", c=8)), [rps[0]], [r_pvec])

    S.dma("sp", lambda e: e.dma_start(out=qkw, in_=T["qk_w"][0:1, :].partition_broadcast(128)), [], [r_qkw])
    S.dma("sp", lambda e: e.dma_start(out=sublnw, in_=T["subln_w"][0:1, :].partition_broadcast(128)), [], [r_subw])
    S.dma("sp", lambda e: e.dma_start(out=lamv, in_=T["lam_v"][0:1, :].partition_broadcast(128)), [], [r_lamv])
    S.op("dve", lambda e: e.tensor_scalar(out=sublnw, in0=sublnw, scalar1=float(1.0 - LAMBDA_INIT), scalar2=None, op0=ALU.mult),
         [r_subw], [r_subw])
    S.op("dve", lambda e: e.tensor_tensor(out=small[:, 0:64], in0=lamv[:, 0:64], in1=lamv[:, 64:128], op=ALU.mult), [r_lamv], [r_small])
    S.op("dve", lambda e: e.tensor_reduce(out=lam[:, 2:3], in_=small[:, 0:64], axis=AX.X, op=ALU.add), [r_small], [r_lam])
    S.op("dve", lambda e: e.tensor_tensor(out=small[:, 0:64], in0=lamv[:, 128:192], in1=lamv[:, 192:256], op=ALU.mult), [r_lamv, r_lam], [r_small])
    S.op("dve", lambda e: e.tensor_reduce(out=lam[:, 3:4], in_=small[:, 0:64], axis=AX.X, op=ALU.add), [r_small], [r_lam])
    S.op("act", lambda e: e.activation(out=lam[:, 2:4], in_=lam[:, 2:4], func=AF.Exp), [r_lam], [r_lam])
    S.op("dve", lambda e: e.scalar_tensor_tensor(out=lam[:, 0:1], in0=lam[:, 3:4], scalar=float(-LAMBDA_INIT), in1=lam[:, 2:3],
                                                 op0=ALU.add, op1=ALU.subtract), [r_lam], [r_lam])
    S.op("dve", lambda e: e.tensor_reduce(out=lam[:, 2:3], in_=qkw[:, 0:64], axis=AX.X, op=ALU.max, apply_absolute_value=True), [r_qkw, r_lam], [r_lam])
    S.op("dve", lambda e: e.tensor_reduce(out=lam[:, 3:4], in_=qkw[:, 64:128], axis=AX.X, op=ALU.max, apply_absolute_value=True), [r_qkw, r_lam], [r_lam])
    S.op("dve", lambda e: e.scalar_tensor_tensor(out=lam[:, 1:2], in0=lam[:, 2:3], scalar=-8.0, in1=lam[:, 3:4],
                                                 op0=ALU.mult, op1=ALU.mult), [r_lam], [r_lam])

    def rms_rstd(ss_ap, n, res_list):
        S.op("dve", lambda e: e.tensor_scalar(out=ss_ap, in0=ss_ap, scalar1=1.0 / n, scalar2=EPS, op0=ALU.mult, op1=ALU.add), res_list, res_list)
        S.op("act", lambda e: e.activation(out=ss_ap, in_=ss_ap, func=AF.Sqrt), res_list, res_list)
        S.op("dve", lambda e: e.reciprocal(out=ss_ap, in_=ss_ap), res_list, res_list)

    def rmsnorm_tile(xt, r_xt, h, r_h, hT, r_hT, nblk, ncols):
        for blk in range(nblk):
            ss = small[:, blk:blk + 1]
            S.op("act", lambda e, blk=blk, ss=ss: e.activation(out=junk, in_=xt[:, blk, :], func=AF.Square, accum_out=ss),
                 [r_xt], [r_junk, r_small])
            rms_rstd(ss, D, [r_small])
            S.op("dve", lambda e, blk=blk, ss=ss: e.scalar_tensor_tensor(out=h[:, blk, :], in0=xt[:, blk, :], scalar=ss, in1=nw,
                                                                         op0=ALU.mult, op1=ALU.mult), [r_xt, r_small, r_nw], [r_h])
            pb = ps_bf(7)
            for c in range(8):
                S.op("pe", lambda e, blk=blk, c=c, pb=pb: e.transpose(out=pb[:, c * 128:(c + 1) * 128], in_=h[:, blk, c * 128:(c + 1) * 128],
                                                                      identity=ident_b), [r_h, r_identb], [rps[7]])
            S.op("act", lambda e, blk=blk, pb=pb: e.copy(out=hT[:, :, blk * 128:(blk + 1) * 128],
                                                         in_=pb.rearrange("p (c t) -> p c t", c=8)), [rps[7]], [r_hT])

    S.dma("sp", lambda e: e.dma_start(out=nw, in_=T["norm_w_ab"][0:1, :].partition_broadcast(128)), [], [r_nw])
    x1s = A.alloc([D], F32); r_x1s = S.res()
    mP = A.mark()
    win = A.alloc([8, 5 * D], BF16); r_win = S.res()
    wout = A.alloc([16, D], BF16); r_wout = S.res()
    wpool = A.alloc([4, 2, 256], BF16); r_wpool = S.res()
    ncast = [0]

    def cast_load(dst, r_dst, src, nchunk, ncols, stage, r_stage):
        for c in range(nchunk):
            for k in range(ncols // 1024):
                i = ncast[0] % len(stage)
                ncast[0] += 1
                S.dma("sp", lambda e, c=c, k=k, i=i: e.dma_start(out=stage[i], in_=src[c * 128:(c + 1) * 128, k * 1024:(k + 1) * 1024]), [], [r_stage[i]])
                if ncast[0] % 2:
                    S.op("act", lambda e, c=c, k=k, i=i: e.copy(out=dst[:, c, k * 1024:(k + 1) * 1024], in_=stage[i]), [r_stage[i]], [r_dst])
                else:
                    S.op("dve", lambda e, c=c, k=k, i=i: e.tensor_copy(out=dst[:, c, k * 1024:(k + 1) * 1024], in_=stage[i]), [r_stage[i]], [r_dst])

    mW = A.mark()
    stage1 = [A.alloc([D], F32) for _ in range(4)]; r_stage1 = [S.res() for _ in range(4)]
    cast_load(win, r_win, T["w_in_ab"], 8, 5 * D, stage1, r_stage1)
    cast_load(wout, r_wout, T["w_out_ab"], 16, D, stage1, r_stage1)
    for g in range(4):
        for cc in range(2):
            S.dma("pool", lambda e, g=g, cc=cc: e.dma_start(out=wpool[:, g, cc, :], in_=T["pool_w"][g, cc * 128:(cc + 1) * 128, :]), [], [r_wpool])

    S.barrier()
    A.reset(mW)

    def sample_l0():
        NS = 16
        xs_t = A.alloc([D], F32); r_xs = S.res()
        hs = A.alloc([D], BF16); r_hs = S.res()
        hTs = A.alloc([8, NS], BF16); r_hTs = S.res()
        stp = [A.alloc([D], F32) for _ in range(2)]; r_stp = [S.res() for _ in range(2)]
        xae = A.alloc([8, 4, 19], F32); r_xaes = S.res()
        sab = [A.alloc([8, 4, 19], F32) for _ in range(2)]; r_sab = [S.res() for _ in range(2)]
        glue = A.alloc([8, 4, 34], F32); r_glues = S.res()
        xac = A.alloc([8, NS], F32); r_xac = S.res()
        gluc = A.alloc([8, NS], F32); r_gluc = S.res()
        tok = A.alloc([D], F32); r_tok = S.res()
        tok2 = A.alloc([D], F32); r_tok2 = S.res()
        sgas = A.alloc([8, NS], BF16); r_sgas = S.res()
        sgbs = A.alloc([8, NS], BF16); r_sgbs = S.res()
        sigs = A.alloc([8, NS], F32); r_sigs = S.res()
        dTs = A.alloc([8, NS], BF16); r_dTs = S.res()
        cs = A.alloc([8, NS], F32); r_cs = S.res(); r_csj = [S.res() for _ in range(8)]
        cs16 = A.alloc([8, NS], BF16); r_cs16 = S.res()
        csq = A.alloc([8, NS], BF16); r_csq = S.res()
        lnt8 = A.alloc([8, NS], F32); r_lnt8 = S.res()
        mixs = A.alloc([16, NS], BF16); r_mixs = S.res()
        lnms = A.alloc([NS], F32); r_lnms = S.res()
        lnrs = A.alloc([NS], F32); r_lnrs = S.res()

        outs.append(S.dma("sp", lambda e: e.dma_start(out=T["pool_s_out"][:, 0:11, :], in_=T["spool"][:, 4:15, :]), [], []))
        outs.append(S.dma("sp", lambda e: e.dma_start(out=T["conv_s_out"][:, 0:26, :], in_=T["sconv"][:, 4:30, :]), [], []))
        S.dma("sp", lambda e: e.dma_start(out=xs_t[0:NS, :], in_=T["xs"][:, :]), [], [r_xs])
        ss = small[0:NS, 8:9]
        S.op("act", lambda e: e.activation(out=junk[0:NS, :], in_=xs_t[0:NS, :], func=AF.Square, accum_out=ss), [r_xs], [r_junk, r_small])
        rms_rstd(ss, D, [r_small])
        S.op("dve", lambda e: e.scalar_tensor_tensor(out=hs[0:NS, :], in0=xs_t[0:NS, :], scalar=ss, in1=nw[0:NS, :], op0=ALU.mult, op1=ALU.mult),
             [r_xs, r_small, r_nw], [r_hs])
        pb = ps_bf(7)
        for c in range(8):
            S.op("pe", lambda e, c=c: e.transpose(out=pb[:, c * NS:(c + 1) * NS], in_=hs[0:NS, c * 128:(c + 1) * 128], identity=ident_b[0:NS, 0:NS]),
                 [r_hs, r_identb], [rps[7]])
        S.op("act", lambda e: e.copy(out=hTs, in_=pb[:, 0:8 * NS].rearrange("p (c t) -> p c t", c=8)), [rps[7]], [r_hTs])
        for e_ in range(4):
            S.dma("sp", lambda e, e_=e_: e.dma_start(out=stp[0][0:15, :], in_=T["spool"][e_, :, :]), [], [r_stp[0]])
            for c in range(8):
                S.op("pe", lambda e, c=c: e.transpose(out=psum[5][:, c * 15:(c + 1) * 15], in_=stp[0][0:15, c * 128:(c + 1) * 128], identity=ident_f[0:15, 0:15]),
                     [r_stp[0], r_identf], [rps[5]])
            S.op("dve", lambda e, e_=e_: e.tensor_copy(out=xae[:, :, e_, 0:15], in_=psum[5][:, 0:120].rearrange("p (c k) -> p c k", c=8)), [rps[5]], [r_xaes])
            S.dma("sp", lambda e, e_=e_: e.dma_start(out=stp[1][0:30, :], in_=T["sconv"][e_, :, :]), [], [r_stp[1]])
            for c in range(8):
                S.op("pe", lambda e, c=c: e.transpose(out=psum[6][:, c * 30:(c + 1) * 30], in_=stp[1][0:30, c * 128:(c + 1) * 128], identity=ident_f[0:30, 0:30]),
                     [r_stp[1], r_identf], [rps[6]])
            S.op("act", lambda e, e_=e_: e.copy(out=glue[:, :, e_, 0:30], in_=psum[6][:, 0:240].rearrange("p (c k) -> p c k", c=8)), [rps[6]], [r_glues])
        for f_ in range(40):
            bank, col = f_ // 8, (f_ % 8) * NS
            for c in range(8):
                S.op("pe", lambda e, c=c, f_=f_, bank=bank, col=col: e.matmul(psum[bank][:, col:col + NS], lhsT=win[:, c, f_ * 128:(f_ + 1) * 128], rhs=hTs[:, c, :],
                                                                            start=(c == 0), stop=(c == 7)), [r_win, r_hTs], [rps[bank]])
        v816 = lambda ap: ap[:, 0:8 * NS].rearrange("p (c t) -> p c t", c=8)
        S.op("act", lambda e: e.copy(out=xae[:, :, :, 15:19], in_=psum[0][:, 0:8 * NS].rearrange("p (c a t) -> p c a t", c=8, a=4)), [rps[0]], [r_xaes])
        S.op("dve", lambda e: e.tensor_copy(out=xac, in_=v816(psum[0])), [rps[0]], [r_xac])
        S.op("act", lambda e: e.activation(out=sgas, in_=v816(psum[1]), func=AF.Silu), [rps[1]], [r_sgas])
        S.op("act", lambda e: e.activation(out=sigs, in_=v816(psum[3]), func=AF.Sigmoid), [rps[3]], [r_sigs])
        S.op("dve", lambda e: e.tensor_tensor(out=gluc, in0=v816(psum[2]), in1=sigs, op=ALU.mult), [rps[2], r_sigs], [r_gluc])
        S.op("pool", lambda e: e.tensor_copy(out=glue[:, :, :, 30:34], in_=gluc.rearrange("p c (a t) -> p c a t", a=4)), [r_gluc], [r_glues])
        S.op("act", lambda e: e.activation(out=sgbs, in_=v816(psum[4]), func=AF.Silu), [rps[4]], [r_sgbs])
        for (srcc, r_srcc, tk, r_tk, dst, row0) in ((xac, r_xac, tok, r_tok, "pool_s_out", 11), (gluc, r_gluc, tok2, r_tok2, "conv_s_out", 26)):
            for c in range(8):
                bk = 5 + c // 4
                S.op("pe", lambda e, c=c, bk=bk, srcc=srcc: e.transpose(out=psum[bk][0:NS, (c % 4) * 128:(c % 4 + 1) * 128], in_=srcc[:, c, :], identity=ident_f),
                     [r_srcc, r_identf], [rps[bk]])
            S.op("dve", lambda e, tk=tk: e.tensor_copy(out=tk[0:NS, 0:512], in_=psum[5][0:NS, :]), [rps[5]], [r_tk])
            S.op("dve", lambda e, tk=tk: e.tensor_copy(out=tk[0:NS, 512:1024], in_=psum[6][0:NS, :]), [rps[6]], [r_tk])
            for e_ in range(4):
                outs.append(S.dma("sp", lambda e, e_=e_, tk=tk, dst=dst, row0=row0: e.dma_start(out=T[dst][e_, row0:row0 + 4, :], in_=tk[e_ * 4:(e_ + 1) * 4, :]),
                                  [r_tk], []))
        for g in range(4):
            w = POOL_W[g]
            Xg = xae[:, 2 * g:2 * g + 2]
            src, rsrc = Xg, r_xaes
            for k in range(g + 1):
                sh = 1 << k
                dst_, rdst = sab[k % 2][:, 2 * g:2 * g + 2], r_sab[k % 2]
                S.op("dve", lambda e, src=src, dst_=dst_, sh=sh: e.tensor_tensor(out=dst_[:, :, :, sh:19], in0=src[:, :, :, sh:19], in1=src[:, :, :, 0:19 - sh], op=ALU.add),
                     [rsrc], [rdst])
                src, rsrc = dst_, rdst
            S.op("dve", lambda e, src=src, Xg=Xg, g=g, w=w: e.scalar_tensor_tensor(out=dTs[:, 2 * g:2 * g + 2, :].rearrange("p c (a t) -> p c a t", a=4),
                                                                                 in0=src[:, :, :, 15:19], scalar=1.0 / w, in1=Xg[:, :, :, 15:19],
                                                                                 op0=ALU.mult, op1=ALU.subtract), [rsrc, r_xaes], [r_dTs])
            for ec in range(2):
                for cc in range(2):
                    S.op("pe", lambda e, ec=ec, cc=cc, g=g: e.matmul(psum[7][:, (2 * g + ec) * NS:(2 * g + ec + 1) * NS], lhsT=wpool[:, g, cc, ec * 128:(ec + 1) * 128],
                                                                    rhs=dTs[:, 2 * g + cc, :], start=(cc == 0), stop=(cc == 1)), [r_wpool, r_dTs], [rps[7]])
        for c in range(8):
            S.op("dve", lambda e, c=c: e.scalar_tensor_tensor(out=mixs[:, c, :], in0=psum[7][:, c * NS:(c + 1) * NS], scalar=pvec[:, c, 34:35], in1=sgas[:, c, :],
                                                             op0=ALU.mult, op1=ALU.mult), [rps[7], r_pvec, r_sgas], [r_mixs])
        for k in range(31):
            for j in range(8):
                csj = cs[:, j, :].rearrange("p (a t) -> p a t", a=4)
                if k == 0:
                    S.op("dve", lambda e, j=j, csj=csj: e.tensor_scalar(out=csj, in0=glue[:, j, :, 0:4], scalar1=pvec[:, j, 0:1], scalar2=pvec[:, j, 31:32],
                                                                        op0=ALU.mult, op1=ALU.add), [r_glues, r_pvec], [r_csj[j]])
                else:
                    S.op("dve", lambda e, j=j, k=k, csj=csj: e.scalar_tensor_tensor(out=csj, in0=glue[:, j, :, k:k + 4], scalar=pvec[:, j, k:k + 1], in1=csj,
                                                                                  op0=ALU.mult, op1=ALU.add), [r_glues, r_pvec, r_csj[j]], [r_csj[j]])
        S.op("dve", lambda e: e.memset(lnms, 0.0), r_csj, [r_cs, r_lnms])
        S.op("act", lambda e: e.activation(out=csq, in_=cs, func=AF.Square), [r_cs], [r_csq])
        S.op("pool", lambda e: e.tensor_copy(out=cs16, in_=cs), [r_cs], [r_cs16])
        for j in range(8):
            S.op("pe", lambda e, j=j: e.matmul(psum[5][:, 0:NS], lhsT=ones_b, rhs=cs16[:, j, :], start=(j == 0), stop=(j == 7)), [r_ones, r_cs16], [rps[5]])
        for j in range(8):
            S.op("pe", lambda e, j=j: e.matmul(psum[6][:, 0:NS], lhsT=ones_b, rhs=csq[:, j, :], start=(j == 0), stop=(j == 7)), [r_ones, r_csq], [rps[6]])
        S.op("dve", lambda e: e.tensor_scalar(out=lnms, in0=psum[5][:, 0:NS], scalar1=1.0 / D, scalar2=None, op0=ALU.mult), [rps[5]], [r_lnms])
        S.op("dve", lambda e: e.tensor_tensor(out=lnrs, in0=lnms, in1=lnms, op=ALU.mult), [r_lnms], [r_lnrs])
        S.op("dve", lambda e: e.scalar_tensor_tensor(out=lnrs, in0=psum[6][:, 0:NS], scalar=1.0 / D, in1=lnrs, op0=ALU.mult, op1=ALU.subtract), [rps[6], r_lnrs], [r_lnrs])
        S.op("dve", lambda e: e.tensor_scalar(out=lnrs, in0=lnrs, scalar1=EPS, scalar2=None, op0=ALU.add), [r_lnrs], [r_lnrs])
        S.op("act", lambda e: e.activation(out=lnrs, in_=lnrs, func=AF.Sqrt), [r_lnrs], [r_lnrs])
        S.op("dve", lambda e: e.reciprocal(out=lnrs, in_=lnrs), [r_lnrs], [r_lnrs])
        S.op("dve", lambda e: e.tensor_tensor(out=lnt8, in0=cs, in1=lnms.unsqueeze(1).broadcast_to([128, 8, NS]), op=ALU.subtract), [r_cs, r_lnms], [r_lnt8])
        S.op("dve", lambda e: e.tensor_tensor(out=lnt8, in0=lnt8, in1=lnrs.unsqueeze(1).broadcast_to([128, 8, NS]), op=ALU.mult), [r_lnt8, r_lnrs], [r_lnt8])
        for j in range(8):
            S.op("act", lambda e, j=j: e.activation(out=lnt8[:, j, :], in_=lnt8[:, j, :], func=AF.Silu, scale=pvec[:, j, 32:33], bias=pvec[:, j, 33:34]),
                 [r_lnt8, r_pvec], [r_lnt8])
        S.op("dve", lambda e: e.tensor_tensor(out=mixs[:, 8:16, :], in0=lnt8, in1=sgbs, op=ALU.mult), [r_lnt8, r_sgbs], [r_mixs])
        for half in range(2):
            bo = 5 + half
            for kc in range(16):
                S.op("pe", lambda e, half=half, kc=kc, bo=bo: e.matmul(psum[bo][0:NS, :], lhsT=mixs[:, kc, :], rhs=wout[:, kc, half * 512:(half + 1) * 512],
                                                                      start=(kc == 0), stop=(kc == 15)), [r_mixs, r_wout], [rps[bo]])
            S.op("dve", lambda e, half=half, bo=bo: e.tensor_tensor(out=x1s[0:NS, half * 512:(half + 1) * 512], in0=psum[bo][0:NS, :],
                                                                   in1=xs_t[0:NS, half * 512:(half + 1) * 512], op=ALU.add), [rps[bo], r_xs], [r_x1s])

    mS = A.mark()
    if DBG["st"]:
        sample_l0()
        print("sample-L0 arena bytes:", A.off)
        S.barrier()
    A.reset(mS)

    xt = A.alloc([NB, D], F32); r_xt = S.res()
    h0 = A.alloc([NB, D], BF16); r_h0 = S.res()
    h0T = A.alloc([8, TT], BF16); r_h0T = S.res()
    xa_h = A.alloc([8, 15], F32); r_xah = S.res()
    glu_h = A.alloc([8, 30], F32); r_gluh = S.res()
    xa_e = [A.alloc([2, 15 + TT], F32) for _ in range(2)]; r_xae = [S.res() for _ in range(2)]
    s_tmp = [A.alloc([2, 15 + TT], F32) for _ in range(2)]; r_stmp = [S.res() for _ in range(2)]
    sga = [A.alloc([2, TT], BF16) for _ in range(2)]; r_sga = [S.res() for _ in range(2)]
    dT = [A.alloc([2, TT], BF16) for _ in range(2)]; r_dT = [S.res() for _ in range(2)]
    glu_e = [A.alloc([30 + TT], F32) for _ in range(4)]; r_glue = [S.res() for _ in range(4)]
    sig = [A.alloc([TT], F32) for _ in range(2)]; r_sig = [S.res() for _ in range(2)]
    cbuf = A.alloc([8, TT], F32); r_c = [S.res() for _ in range(8)]
    cb16 = A.alloc([8, TT], BF16); r_cb16 = S.res()
    csq16 = A.alloc([8, TT], BF16); r_csq16 = S.res()
    sgb = A.alloc([8, TT], BF16); r_sgb = [S.res() for _ in range(8)]
    mixT = A.alloc([16, TT], BF16); r_mix = [S.res() for _ in range(16)]
    lnm = A.alloc([TT], F32); r_lnm = S.res()
    lnr = A.alloc([TT], F32); r_lnr = S.res()
    lnt = [A.alloc([TT], F32) for _ in range(2)]; r_lnt = [S.res() for _ in range(2)]
    print("phase1 arena bytes:", A.off)

    S.op("dve", lambda e: e.memset(xa_h, 0.0), [], [r_xah])
    S.op("dve", lambda e: e.memset(glu_h, 0.0), [], [r_gluh])

    r_x1 = [S.res() for _ in range(NT)]
    pi = [0]

    def nxt_bank():
        b = pi[0] % 4
        pi[0] += 1
        return b

    for t in range(DBG["nt1"]):
        tok0 = t * TT
        S.dma("sp", lambda e, tok0=tok0: e.dma_start(out=xt, in_=xp[tok0:tok0 + TT, :].rearrange("(k p) d -> p k d", p=128)), [], [r_xt])
        rmsnorm_tile(xt, r_xt, h0, r_h0, h0T, r_h0T, NB, TT)

        def inproj(bank, col, half, fchunk):
            for c in range(8):
                S.op("pe", lambda e, c=c: e.matmul(psum[bank][:, half * TT:(half + 1) * TT], lhsT=win[:, c, fchunk * 128:(fchunk + 1) * 128],
                                                   rhs=h0T[:, c, :], start=(c == 0), stop=(c == 7)), [r_win, r_h0T], [rps[bank]])

        if t == 0 and os.environ.get("K_DUMP"):
            outs.append(S.dma("sp", lambda e: e.dma_start(out=T["d_h0T"][:, :], in_=h0T.rearrange("p c t -> p (c t)")), [r_h0T], []))
            outs.append(S.dma("sp", lambda e: e.dma_start(out=T["d_h0"][:, :], in_=h0.rearrange("p c t -> p (c t)")), [r_h0], []))
            outs.append(S.dma("sp", lambda e: e.dma_start(out=T["d_win"][:, :], in_=win[:, 0, 0:1024]), [r_win], []))
        for g in range(4):
            w = POOL_W[g]
            i2 = g % 2
            ba = nxt_bank(); bb = nxt_bank()
            inproj(ba, 0, 0, 2 * g); inproj(ba, 0, 1, 2 * g + 1)
            inproj(bb, 0, 0, 8 + 2 * g); inproj(bb, 0, 1, 8 + 2 * g + 1)
            xe = xa_e[i2]; rxe = r_xae[i2]
            S.op("act", lambda e, bb=bb, i2=i2: e.activation(out=sga[i2], in_=psum[bb][:, :].rearrange("p (a t) -> p a t", a=2), func=AF.Silu),
                 [rps[bb]], [r_sga[i2]])
            S.op("act", lambda e, ba=ba, xe=xe: e.copy(out=xe[:, :, 15:15 + TT], in_=psum[ba][:, :].rearrange("p (a t) -> p a t", a=2)),
                 [rps[ba]], [rxe])
            S.op("act", lambda e, g=g, xe=xe: e.copy(out=xe[:, :, 0:15], in_=xa_h[:, 2 * g:2 * g + 2, :]), [r_xah], [rxe])
            if t == 0 and g == 0 and os.environ.get("K_DUMP"):
                outs.append(S.dma("sp", lambda e, xe=xe: e.dma_start(out=T["d_xae"][:, :], in_=xe.rearrange("p c t -> p (c t)")), [rxe], []))
            src, rsrc = xe, rxe
            L = 15 + TT
            for k in range(g + 1):
                sh = 1 << k
                dst, rdst = s_tmp[k % 2], r_stmp[k % 2]
                S.op("dve", lambda e, src=src, dst=dst, sh=sh: e.tensor_tensor(out=dst[:, :, sh:L], in0=src[:, :, sh:L], in1=src[:, :, 0:L - sh], op=ALU.add),
                     [rsrc], [rdst])
                src, rsrc = dst, rdst
            S.op("dve", lambda e, src=src, xe=xe, i2=i2, w=w: e.scalar_tensor_tensor(out=dT[i2], in0=src[:, :, 15:L], scalar=1.0 / w, in1=xe[:, :, 15:L],
                                                                                   op0=ALU.mult, op1=ALU.subtract), [rsrc, rxe], [r_dT[i2]])
            if t == 0:
                S.op("dve", lambda e, src=src, g=g: e.tensor_tensor(out=src[:, :, 15:30], in0=src[:, :, 15:30],
                                                                     in1=invc[:, g:g + 1, 0:15].broadcast_to([128, 2, 15]), op=ALU.mult), [rsrc, r_invc], [rsrc])
                S.op("dve", lambda e, src=src, xe=xe, i2=i2: e.tensor_tensor(out=dT[i2][:, :, 0:15], in0=src[:, :, 15:30], in1=xe[:, :, 15:30], op=ALU.subtract),
                     [rsrc, rxe], [r_dT[i2]])
            S.op("act", lambda e, g=g, xe=xe: e.copy(out=xa_h[:, 2 * g:2 * g + 2, :], in_=xe[:, :, TT:TT + 15]), [rxe], [r_xah])
            bc = nxt_bank()
            for ec in range(2):
                for cc in range(2):
                    S.op("pe", lambda e, ec=ec, cc=cc, g=g, i2=i2, bc=bc: e.matmul(psum[bc][:, ec * TT:(ec + 1) * TT], lhsT=wpool[:, g, cc, ec * 128:(ec + 1) * 128],
                                                                                  rhs=dT[i2][:, cc, :], start=(cc == 0), stop=(cc == 1)),
                         [r_wpool, r_dT[i2]], [rps[bc]])
            for ec in range(2):
                S.op("dve", lambda e, ec=ec, g=g, i2=i2, bc=bc: e.scalar_tensor_tensor(out=mixT[:, 2 * g + ec, :], in0=psum[bc][:, ec * TT:(ec + 1) * TT],
                                                                                      scalar=pvec[:, 2 * g + ec, 34:35], in1=sga[i2][:, ec, :],
                                                                                      op0=ALU.mult, op1=ALU.mult),
                     [rps[bc], r_pvec, r_sga[i2]], [r_mix[2 * g + ec]])

        for jg in (0, 4):
            for j in range(jg, jg + 4):
                i2 = j % 2
                i4 = j % 4
                bu = nxt_bank(); bg = nxt_bank()
                inproj(bu, 0, 0, 16 + j); inproj(bu, 0, 1, 24 + j)
                inproj(bg, 0, 0, 32 + j)
                ge = glu_e[i4]; rge = r_glue[i4]
                S.op("act", lambda e, bu=bu, i2=i2: e.activation(out=sig[i2], in_=psum[bu][:, TT:2 * TT], func=AF.Sigmoid), [rps[bu]], [r_sig[i2]])
                S.op("dve", lambda e, bu=bu, i2=i2, ge=ge: e.tensor_tensor(out=ge[:, 30:30 + TT], in0=psum[bu][:, 0:TT], in1=sig[i2], op=ALU.mult),
                     [rps[bu], r_sig[i2]], [rge])
                S.op("act", lambda e, j=j, ge=ge: e.copy(out=ge[:, 0:30], in_=glu_h[:, j, :]), [r_gluh], [rge])
                S.op("act", lambda e, bg=bg, j=j: e.activation(out=sgb[:, j, :], in_=psum[bg][:, 0:TT], func=AF.Silu), [rps[bg]], [r_sgb[j]])
            for k in range(31):
                for j in range(jg, jg + 4):
                    ge = glu_e[j % 4]; rge = r_glue[j % 4]
                    if k == 0:
                        S.op("dve", lambda e, j=j, ge=ge: e.tensor_scalar(out=cbuf[:, j, :], in0=ge[:, 0:TT], scalar1=pvec[:, j, 0:1], scalar2=pvec[:, j, 31:32],
                                                                          op0=ALU.mult, op1=ALU.add), [rge, r_pvec], [r_c[j]])
                    else:
                        S.op("dve", lambda e, j=j, ge=ge, k=k: e.scalar_tensor_tensor(out=cbuf[:, j, :], in0=ge[:, k:k + TT], scalar=pvec[:, j, k:k + 1], in1=cbuf[:, j, :],
                                                                                    op0=ALU.mult, op1=ALU.add), [rge, r_pvec, r_c[j]], [r_c[j]])
            for j in range(jg, jg + 4):
                ge = glu_e[j % 4]; rge = r_glue[j % 4]
                S.op("act", lambda e, j=j, ge=ge: e.copy(out=glu_h[:, j, :], in_=ge[:, TT:TT + 30]), [rge], [r_gluh])
                S.op("act", lambda e, j=j: e.activation(out=csq16[:, j, :], in_=cbuf[:, j, :], func=AF.Square), [r_c[j]], [r_csq16])
                S.op("act", lambda e, j=j: e.copy(out=cb16[:, j, :], in_=cbuf[:, j, :]), [r_c[j]], [r_cb16])
        bs = 4
        for j in range(8):
            S.op("pe", lambda e, j=j: e.matmul(psum[bs][:, 0:TT], lhsT=ones_b, rhs=cb16[:, j, :], start=(j == 0), stop=(j == 7)), [r_ones, r_cb16], [rps[bs]])
        for j in range(8):
            S.op("pe", lambda e, j=j: e.matmul(psum[5][:, 0:TT], lhsT=ones_b, rhs=csq16[:, j, :], start=(j == 0), stop=(j == 7)), [r_ones, r_csq16], [rps[5]])
        S.op("dve", lambda e: e.tensor_scalar(out=lnm, in0=psum[bs][:, 0:TT], scalar1=1.0 / D, scalar2=None, op0=ALU.mult), [rps[bs]], [r_lnm])
        S.op("dve", lambda e: e.tensor_tensor(out=lnr, in0=lnm, in1=lnm, op=ALU.mult), [r_lnm], [r_lnr])
        S.op("dve", lambda e: e.scalar_tensor_tensor(out=lnr, in0=psum[5][:, 0:TT], scalar=1.0 / D, in1=lnr, op0=ALU.mult, op1=ALU.subtract),
             [rps[5], r_lnr], [r_lnr])
        S.op("dve", lambda e: e.tensor_scalar(out=lnr, in0=lnr, scalar1=EPS, scalar2=None, op0=ALU.add), [r_lnr], [r_lnr])
        S.op("act", lambda e: e.activation(out=lnr, in_=lnr, func=AF.Sqrt), [r_lnr], [r_lnr])
        S.op("dve", lambda e: e.reciprocal(out=lnr, in_=lnr), [r_lnr], [r_lnr])
        for j in range(8):
            i2 = j % 2
            S.op("dve", lambda e, j=j, i2=i2: e.tensor_tensor(out=lnt[i2], in0=cbuf[:, j, :], in1=lnm, op=ALU.subtract), [r_c[j], r_lnm], [r_lnt[i2]])
            S.op("dve", lambda e, i2=i2: e.tensor_tensor(out=lnt[i2], in0=lnt[i2], in1=lnr, op=ALU.mult), [r_lnt[i2], r_lnr], [r_lnt[i2]])
            S.op("act", lambda e, j=j, i2=i2: e.activation(out=lnt[i2], in_=lnt[i2], func=AF.Silu, scale=pvec[:, j, 32:33], bias=pvec[:, j, 33:34]),
                 [r_lnt[i2], r_pvec], [r_lnt[i2]])
            S.op("dve", lambda e, j=j, i2=i2: e.tensor_tensor(out=mixT[:, 8 + j, :], in0=lnt[i2], in1=sgb[:, j, :], op=ALU.mult),
                 [r_lnt[i2], r_sgb[j]], [r_mix[8 + j]])
        for blk in range(NB):
            for half in range(2):
                bo = 5 + (blk * 2 + half) % 2
                for kc in range(16):
                    S.op("pe", lambda e, blk=blk, half=half, kc=kc, bo=bo: e.matmul(psum[bo][:, :], lhsT=mixT[:, kc, blk * 128:(blk + 1) * 128],
                                                                                  rhs=wout[:, kc, half * 512:(half + 1) * 512], start=(kc == 0), stop=(kc == 15)),
                         [r_mix[kc], r_wout], [rps[bo]])
                S.op("dve", lambda e, blk=blk, half=half, bo=bo: e.tensor_tensor(out=xt[:, blk, half * 512:(half + 1) * 512], in0=psum[bo][:, :],
                                                                               in1=xt[:, blk, half * 512:(half + 1) * 512], op=ALU.add), [rps[bo], r_xt], [r_xt])
        S.dma("sp", lambda e, tok0=tok0: e.dma_start(out=T["x1_scr"][tok0:tok0 + TT, :].rearrange("(k p) d -> p k d", p=128), in_=xt), [r_xt], [r_x1[t]])

    stt = cbuf.rearrange("p c t -> p (c t)")[:, 0:D]
    for c in range(4):
        S.op("pe", lambda e, c=c: e.transpose(out=psum[0][0:15, c * 128:(c + 1) * 128], in_=xa_h[:, c, :], identity=ident_f), [r_xah, r_identf], [rps[0]])
    for c in range(4):
        S.op("pe", lambda e, c=c: e.transpose(out=psum[1][0:30, c * 128:(c + 1) * 128], in_=glu_h[:, c, :], identity=ident_f), [r_gluh, r_identf], [rps[1]])
        S.op("pe", lambda e, c=c: e.transpose(out=psum[2][0:30, c * 128:(c + 1) * 128], in_=glu_h[:, 4 + c, :], identity=ident_f), [r_gluh, r_identf], [rps[2]])
    S.op("dve", lambda e: e.tensor_copy(out=stt[0:15, 0:512], in_=psum[0][0:15, 0:512]), [rps[0]], r_c[0:4])
    outs.append(S.dma("sp", lambda e: e.dma_start(out=T["pool_st"][:, 0:512], in_=stt[0:15, 0:512]), r_c[0:4], []))
    for c in range(4):
        S.op("pe", lambda e, c=c: e.transpose(out=psum[3][0:15, c * 128:(c + 1) * 128], in_=xa_h[:, 4 + c, :], identity=ident_f), [r_xah, r_identf], [rps[3]])
    S.op("dve", lambda e: e.tensor_copy(out=stt[0:15, 512:1024], in_=psum[3][0:15, 0:512]), [rps[3]], r_c[0:4])
    outs.append(S.dma("sp", lambda e: e.dma_start(out=T["pool_st"][:, 512:1024], in_=stt[0:15, 512:1024]), r_c[0:4], []))
    stc = stt
    S.op("dve", lambda e: e.tensor_copy(out=stc[0:30, 0:512], in_=psum[1][0:30, 0:512]), [rps[1]], r_c[0:4])
    S.op("dve", lambda e: e.tensor_copy(out=stc[0:30, 512:1024], in_=psum[2][0:30, 0:512]), [rps[2]], r_c[0:4])
    outs.append(S.dma("sp", lambda e: e.dma_start(out=T["conv_st"][:, :], in_=stc[0:30, :]), r_c[0:4], []))

    S.barrier()
    A.reset(mP)
    S.dma("sp", lambda e: e.dma_start(out=nw, in_=T["norm_w_c"][0:1, :].partition_broadcast(128)), [], [r_nw])
    wc = A.alloc([8, 4 * D], BF16); r_wc = S.res()
    woc = A.alloc([8, D], BF16); r_woc = S.res()
    mW2 = A.mark()
    stage2 = [A.alloc([D], F32) for _ in range(4)]; r_stage2 = [S.res() for _ in range(4)]
    cast_load(wc, r_wc, T["w_in_c"], 8, 4 * D, stage2, r_stage2)
    cast_load(woc, r_woc, T["w_out_c"], 8, D, stage2, r_stage2)
    S.barrier()
    A.reset(mW2)
    xt2 = A.alloc([NB, D], F32); r_xt2 = S.res()
    h1 = A.alloc([NB, D], BF16); r_h1 = S.res()
    h1T = A.alloc([8, TT], BF16); r_h1T = S.res()
    stg = [A.alloc([D], F32) for _ in range(2)]; r_stg = [S.res() for _ in range(2)]
    sq = A.alloc([D], F32); r_sq = S.res()
    nb16 = A.alloc([D], BF16); r_nb16 = S.res()
    ktile = A.alloc([8, TT], BF16); r_ktile = S.res()
    vtile = A.alloc([8, NB, 129], BF16); r_vtile = S.res()
    QT = A.alloc([8, TT], BF16); r_QT = S.res()
    sg = A.alloc([NB, D], BF16); r_sg = S.res()
    KCH = 4 * TT
    kth = [A.alloc([KCH], BF16) for _ in range(3)]; r_kth = [S.res() for _ in range(3)]
    vh = [A.alloc([KCH // 128, 129], BF16) for _ in range(3)]; r_vh = [S.res() for _ in range(3)]
    PT = [A.alloc([2, TT], BF16) for _ in range(3)]; r_PT = [S.res() for _ in range(3)]
    og = A.alloc([NB, D], BF16); r_og = S.res()
    ogT = A.alloc([8, TT], BF16); r_ogT = S.res()
    o1 = A.alloc([128], F32); r_o1 = S.res()
    o2 = A.alloc([128], F32); r_o2 = S.res()
    sm2 = A.alloc([16], F32); r_sm2 = S.res()
    rs16 = A.alloc([16], F32); r_rs16 = S.res()
    print("phase2 arena bytes:", A.off)
    S.op("dve", lambda e: e.memset(vtile, 1.0), [], [r_vtile])

    r_kscr = [S.res() for _ in range(NT)]
    r_vscr = [S.res() for _ in range(NT)]
    nload = [0]
    npt = [0]
    pj = [0]
    ownr = A.alloc([16], I32); r_ownr = S.res()
    amask = A.alloc([2, 2, TT], BF16); r_amask = S.res()
    S.dma("sp", lambda e: e.dma_start(out=ownr, in_=T["own_rows"][:, :]), [], [r_ownr])
    S.dma("pool", lambda e: e.dma_start(out=amask, in_=T["amask"][:, :].rearrange("p (a b q) -> p a b q", a=2, b=2)), [], [r_amask])
    xq = A.alloc([NB, D], F32); r_xq = S.res()
    print("phase2 arena bytes (final):", A.off)

    def proj_bank():
        b = 6 + pj[0] % 2
        pj[0] += 1
        return b

    def head_norm(src_stage, r_src, wcol0, np_=128):
        src_ = src_stage[0:np_]
        S.op("pool", lambda e: e.tensor_tensor(out=sq[0:np_], in0=src_, in1=src_, op=ALU.mult), [r_src], [r_sq])
        S.op("dve", lambda e: e.tensor_reduce(out=rs16[0:np_], in_=sq[0:np_].rearrange("p (g d) -> p g d", d=64), axis=AX.X, op=ALU.add), [r_sq], [r_rs16])
        rms_rstd(rs16[0:np_], 64, [r_rs16])
        v3 = src_.rearrange("p (g d) -> p g d", d=64)
        S.op("dve", lambda e: e.tensor_tensor(out=v3, in0=v3, in1=rs16[0:np_].unsqueeze(2).broadcast_to([np_, 16, 64]), op=ALU.mult), [r_src, r_rs16], [r_src])
        S.op("pool", lambda e: e.tensor_tensor(out=v3, in0=v3, in1=qkw[0:np_, wcol0:wcol0 + 64].unsqueeze(1).broadcast_to([np_, 16, 64]), op=ALU.mult),
             [r_src, r_qkw], [r_src])

    def proj(hT, r_hT, blk, col0, half):
        b = proj_bank()
        for c in range(8):
            S.op("pe", lambda e, c=c, b=b: e.matmul(psum[b][:, :], lhsT=hT[:, c, blk * 128:(blk + 1) * 128],
                                                    rhs=wc[:, c, col0 + half * 512: col0 + (half + 1) * 512], start=(c == 0), stop=(c == 7)),
                 [r_hT, r_wc], [rps[b]])
        return b

    def to_featmajor(dst, r_dst, blk):
        pb = ps_bf(7)
        for h in range(8):
            S.op("pe", lambda e, h=h, pb=pb: e.transpose(out=pb[:, h * 128:(h + 1) * 128], in_=nb16[:, h * 128:(h + 1) * 128], identity=ident_b),
                 [r_nb16, r_identb], [rps[7]])
        S.op("dve", lambda e, pb=pb, blk=blk: e.tensor_copy(out=dst[:, :, blk * 128:(blk + 1) * 128], in_=pb.rearrange("p (h t) -> p h t", h=8)),
             [rps[7]], [r_dst])

    for t in range(DBG["nt2"]):
        tok0 = t * TT
        S.dma("sp", lambda e, tok0=tok0: e.dma_start(out=xt2, in_=T["x1_scr"][tok0:tok0 + TT, :].rearrange("(k p) d -> p k d", p=128)), [r_x1[t]], [r_xt2])
        rmsnorm_tile(xt2, r_xt2, h1, r_h1, h1T, r_h1T, NB, TT)
        for blk in range(NB):
            row = tok0 + blk * 128
            ks, rks = stg[0], r_stg[0]
            for half in range(2):
                b = proj(h1T, r_h1T, blk, D, half)
                S.op("act", lambda e, b=b, half=half: e.copy(out=ks[:, half * 512:(half + 1) * 512], in_=psum[b][:, :]), [rps[b]], [rks])
            head_norm(ks, rks, 64)
            outs.append(S.dma("sp", lambda e, row=row: e.dma_start(out=T["k_all"][row:row + 128, :], in_=ks), [rks], []))
            S.op("act", lambda e: e.copy(out=nb16, in_=ks), [rks], [r_nb16])
            to_featmajor(ktile, r_ktile, blk)
            vs, rvs = stg[1], r_stg[1]
            for half in range(2):
                b = proj(h1T, r_h1T, blk, 2 * D, half)
                S.op("act", lambda e, b=b, half=half: e.copy(out=vs[:, half * 512:(half + 1) * 512], in_=psum[b][:, :]), [rps[b]], [rvs])
                S.op("dve", lambda e, b=b, half=half, blk=blk: e.tensor_copy(out=vtile[:, half * 4:(half + 1) * 4, blk, 0:128],
                                                                          in_=psum[b][:, :].rearrange("p (h d) -> p h d", h=4)), [rps[b]], [r_vtile])
            outs.append(S.dma("sp", lambda e, row=row: e.dma_start(out=T["v_all"][row:row + 128, :], in_=vs), [rvs], []))
        S.dma("sp", lambda e, tok0=tok0: e.dma_start(out=T["kt_scr"][:, :, tok0:tok0 + TT].rearrange("h p t -> p h t"), in_=ktile), [r_ktile], [r_kscr[t]])
        S.dma("sp", lambda e, t=t: e.dma_start(out=T["v_scr"][:, :, t * NB:(t + 1) * NB, :].rearrange("h p k d -> p h (k d)"), in_=vtile.rearrange("p h k d -> p h (k d)")), [r_vtile], [r_vscr[t]])
        if t % 2 == 0 or not DBG["attn"]:
            continue
        s_ = t // 2
        for blk in range(NB):
            S.dma("pool", lambda e, blk=blk, s_=s_: e.indirect_dma_start(out=xq[:, blk, :], out_offset=None, in_=T["x1_scr"][:, :],
                                                                     in_offset=bass.IndirectOffsetOnAxis(ap=ownr[:, s_ * 2 + blk:s_ * 2 + blk + 1], axis=0)),
                  [r_ownr, r_x1[t - 1], r_x1[t]], [r_xq])
        rmsnorm_tile(xq, r_xq, h1, r_h1, h1T, r_h1T, NB, TT)
        for blk in range(NB):
            qs, rqs = stg[0], r_stg[0]
            for half in range(2):
                b = proj(h1T, r_h1T, blk, 0, half)
                S.op("act", lambda e, b=b, half=half: e.copy(out=qs[:, half * 512:(half + 1) * 512], in_=psum[b][:, :]), [rps[b]], [rqs])
            head_norm(qs, rqs, 0)
            S.op("act", lambda e: e.copy(out=nb16, in_=qs), [rqs], [r_nb16])
            to_featmajor(QT, r_QT, blk)
            for half in range(2):
                b = proj(h1T, r_h1T, blk, 3 * D, half)
                S.op("act", lambda e, b=b, half=half, blk=blk: e.activation(out=sg[:, blk, half * 512:(half + 1) * 512], in_=psum[b][:, :], func=AF.Silu),
                     [rps[b]], [r_sg])
        nkb = (t + 1) * NB
        for h in range(8):
            ob = [0, 1]
            for qb in range(NB):
                S.op("dve", lambda e, b=ob[qb]: e.memset(psum[b][:, 0:258], 0.0), [], [rps[ob[qb]]])
            pend_pv = []
            for ch0 in range(0, nkb, KCH // 128):
                nb_ch = min(KCH // 128, nkb - ch0)
                li = nload[0] % 3
                nload[0] += 1
                tiles_in = sorted(set((ch0 + i) // NB for i in range(nb_ch)))
                S.dma("sp", lambda e, h=h, li=li, ch0=ch0, nb_ch=nb_ch: e.dma_start(out=kth[li][:, 0:nb_ch * 128],
                                                                                   in_=T["kt_scr"][h, :, ch0 * 128:(ch0 + nb_ch) * 128]),
                      [r_kscr[x] for x in tiles_in], [r_kth[li]])
                S.dma("sp", lambda e, h=h, li=li, ch0=ch0, nb_ch=nb_ch: e.dma_start(out=vh[li][:, 0:nb_ch, :],
                                                                                   in_=T["v_scr"][h, :, ch0:ch0 + nb_ch, :]),
                      [r_vscr[x] for x in tiles_in], [r_vh[li]])
                for i in range(nb_ch):
                    kb = ch0 + i
                    jd = kb - (nkb - 4)
                    sb_ = 2 + 2 * (kb % 2)
                    pi_ = npt[0] % 3
                    npt[0] += 1
                    for c in range(2):
                        S.op("pe", lambda e, c=c, li=li, i=i, h=h, sb_=sb_: e.matmul(
                            psum[sb_ + c][:, 0:TT], lhsT=kth[li][c * 64:(c + 1) * 64, i * 128:(i + 1) * 128],
                            rhs=QT[c * 64:(c + 1) * 64, h, :], start=True, stop=True), [r_kth[li], r_QT], [rps[sb_ + c]])
                    while len(pend_pv) > 0:
                        pend_pv.pop(0)()
                    pt = PT[pi_]; rpt = r_PT[pi_]
                    S.op("act", lambda e, pt=pt, sb_=sb_: e.activation(out=pt, in_=psum_all[:, sb_ * 512:(sb_ + 2) * 512].rearrange("p (c x) -> p c x", c=2)[:, :, 0:TT],
                                                                     func=AF.Exp, scale=0.125, bias=lam[:, 1:2]), [rps[sb_], rps[sb_ + 1], r_lam], [rpt])
                    if jd >= 0:
                        S.op("dve", lambda e, pt=pt, jd=jd: e.tensor_tensor(out=pt, in0=pt, in1=amask[:, jd // 2, jd % 2, :].unsqueeze(1).broadcast_to([128, 2, TT]),
                                                                           op=ALU.mult), [rpt, r_amask], [rpt])
                    def do_pv(pt=pt, rpt=rpt, li=li, i=i):
                        for qb in range(NB):
                            for c in range(2):
                                S.op("pe", lambda e, c=c, qb=qb, b=ob[qb]: e.matmul(
                                    psum[b][:, c * 129:(c + 1) * 129], lhsT=pt[:, c, qb * 128:(qb + 1) * 128], rhs=vh[li][:, i, :],
                                    start=False, stop=False, skip_group_check=True), [rpt, r_vh[li]], [rps[b]])
                    pend_pv.append(do_pv)
            while len(pend_pv) > 0:
                pend_pv.pop(0)()
            for qb in range(NB):
                b = ob[qb]
                O = psum[b][:, 0:258].rearrange("p (c d) -> p c d", c=2)
                S.op("dve", lambda e, O=O: e.reciprocal(out=sm2[:, 0:2], in_=O[:, :, 128]), [rps[b]], [r_sm2])
                S.op("dve", lambda e: e.tensor_tensor(out=sm2[:, 1:2], in0=sm2[:, 1:2], in1=lam[:, 0:1], op=ALU.mult), [r_sm2, r_lam], [r_sm2])
                S.op("act", lambda e, O=O: e.activation(out=o1, in_=O[:, 0, 0:128], func=AF.Copy, scale=sm2[:, 0:1]), [rps[b], r_sm2], [r_o1])
                S.op("dve", lambda e, O=O: e.scalar_tensor_tensor(out=o2, in0=O[:, 1, 0:128], scalar=sm2[:, 1:2], in1=o1, op0=ALU.mult, op1=ALU.add),
                     [rps[b], r_sm2, r_o1], [r_o2])
                S.op("act", lambda e: e.activation(out=junk[:, 0:128], in_=o2, func=AF.Square, accum_out=sm2[:, 2:3]), [r_o2], [r_junk, r_sm2])
                rms_rstd(sm2[:, 2:3], 128, [r_sm2])
                S.op("dve", lambda e: e.scalar_tensor_tensor(out=o2, in0=o2, scalar=sm2[:, 2:3], in1=sublnw, op0=ALU.mult, op1=ALU.mult),
                     [r_o2, r_sm2, r_subw], [r_o2])
                S.op("dve", lambda e, qb=qb, h=h: e.tensor_tensor(out=og[:, qb, h * 128:(h + 1) * 128], in0=o2, in1=sg[:, qb, h * 128:(h + 1) * 128], op=ALU.mult),
                     [r_o2, r_sg], [r_og])
        for blk in range(NB):
            pb = ps_bf(7)
            for h in range(8):
                S.op("pe", lambda e, h=h, pb=pb, blk=blk: e.transpose(out=pb[:, h * 128:(h + 1) * 128], in_=og[:, blk, h * 128:(h + 1) * 128], identity=ident_b),
                     [r_og, r_identb], [rps[7]])
            S.op("act", lambda e, pb=pb, blk=blk: e.copy(out=ogT[:, :, blk * 128:(blk + 1) * 128], in_=pb.rearrange("p (h t) -> p h t", h=8)),
                 [rps[7]], [r_ogT])
        for blk in range(NB):
            for half in range(2):
                b = proj_bank()
                for kc in range(8):
                    S.op("pe", lambda e, blk=blk, half=half, kc=kc, b=b: e.matmul(psum[b][:, :], lhsT=ogT[:, kc, blk * 128:(blk + 1) * 128],
                                                                                rhs=woc[:, kc, half * 512:(half + 1) * 512], start=(kc == 0), stop=(kc == 7)),
                         [r_ogT, r_woc], [rps[b]])
                S.op("dve", lambda e, blk=blk, half=half, b=b: e.tensor_tensor(out=xq[:, blk, half * 512:(half + 1) * 512], in0=psum[b][:, :],
                                                                             in1=xq[:, blk, half * 512:(half + 1) * 512], op=ALU.add), [rps[b], r_xq], [r_xq])
        outs.append(S.dma("sp", lambda e, s_=s_: e.dma_start(out=T["y_own"][s_ * TT:(s_ + 1) * TT, :].rearrange("(k p) d -> p k d", p=128), in_=xq), [r_xq], []))

    def sample_l1():
        NS = 16
        ktn = A.alloc([8, NS], BF16); r_ktn = S.res()
        QTs = A.alloc([8, NS], BF16); r_QTs = S.res()
        vnew = A.alloc([8, 129], BF16); r_vnew = S.res()
        kpg = [A.alloc([D], BF16) for _ in range(2)]; r_kpg = [S.res() for _ in range(2)]
        vpg = [A.alloc([8, 129], BF16) for _ in range(2)]; r_vpg = [S.res() for _ in range(2)]
        ktp = [A.alloc([8, 128], BF16) for _ in range(2)]; r_ktp = [S.res() for _ in range(2)]
        vfl = [A.alloc([D], BF16) for _ in range(2)]; r_vfl = [S.res() for _ in range(2)]
        pte = [[A.alloc([8, 2, NS], BF16) for _ in range(2)] for _ in range(4)]
        r_pte = [[S.res() for _ in range(2)] for _ in range(4)]
        ptn = A.alloc([8, 2, NS], BF16); r_ptn = S.res()
        pti = A.alloc([256], I32); r_pti = S.res()
        ptf = A.alloc([256], F32); r_ptf = S.res()
        idx = A.alloc([256], I32); r_idx = S.res()
        smf = A.alloc([NS], F32); r_smf = S.res()
        smb = A.alloc([NS], BF16); r_smb = S.res()
        rr = A.alloc([16], F32); r_rr = S.res()
        print("sample-L1 arena bytes:", A.off)
        O1 = xq.rearrange("p a d -> p (a d)")[:, 0:1032]; r_O1 = r_xq
        O2 = xt2.rearrange("p a d -> p (a d)")[:, 0:1032]; r_O2 = r_xt2

        S.op("pool", lambda e: e.memset(vnew, 1.0), [], [r_vnew])
        for i in range(2):
            S.op("pool", lambda e, i=i: e.memset(vpg[i], 1.0), [], [r_vpg[i]])
        for e_ in range(4):
            for i in range(2):
                S.op("pool", lambda e, e_=e_, i=i: e.memset(pte[e_][i], 0.0), [], [r_pte[e_][i]])
        S.dma("sp", lambda e: e.dma_start(out=pti, in_=T["pt"][0:1, :].partition_broadcast(128)), [], [r_pti])
        S.op("dve", lambda e: e.tensor_copy(out=ptf, in_=pti), [r_pti], [r_ptf])
        S.op("dve", lambda e: e.scalar_tensor_tensor(out=ptf, in0=ptf, scalar=128.0, in1=iot[:, 0:1].broadcast_to([128, 256]), op0=ALU.mult, op1=ALU.subtract),
             [r_ptf, r_iot], [r_ptf])
        S.op("dve", lambda e: e.tensor_copy(out=idx, in_=ptf), [r_ptf], [r_idx])
        S.dma("sp", lambda e: e.dma_start(out=smf[0:NS, :], in_=T["smask"][:, :]), [], [r_smf])
        S.op("dve", lambda e: e.tensor_copy(out=smb[0:NS, :], in_=smf[0:NS, :]), [r_smf], [r_smb])

        ss = small[0:NS, 9:10]
        S.op("act", lambda e: e.activation(out=junk[0:NS, :], in_=x1s[0:NS, :], func=AF.Square, accum_out=ss), [r_x1s], [r_junk, r_small])
        rms_rstd(ss, D, [r_small])
        S.op("dve", lambda e: e.scalar_tensor_tensor(out=h1[0:NS, 0, :], in0=x1s[0:NS, :], scalar=ss, in1=nw[0:NS, :], op0=ALU.mult, op1=ALU.mult),
             [r_x1s, r_small, r_nw], [r_h1])
        pb = ps_bf(7)
        for c in range(8):
            S.op("pe", lambda e, c=c: e.transpose(out=pb[:, c * NS:(c + 1) * NS], in_=h1[0:NS, 0, c * 128:(c + 1) * 128], identity=ident_b[0:NS, 0:NS]),
                 [r_h1, r_identb], [rps[7]])
        S.op("act", lambda e: e.copy(out=h1T[:, :, 0:NS], in_=pb[:, 0:8 * NS].rearrange("p (c t) -> p c t", c=8)), [rps[7]], [r_h1T])

        def sproj(col0, half):
            b = proj_bank()
            for c in range(8):
                S.op("pe", lambda e, c=c, b=b: e.matmul(psum[b][0:NS, :], lhsT=h1T[:, c, 0:NS], rhs=wc[:, c, col0 + half * 512: col0 + (half + 1) * 512],
                                                        start=(c == 0), stop=(c == 7)), [r_h1T, r_wc], [rps[b]])
            return b

        def heads_T(dst, r_dst):
            pb_ = ps_bf(7)
            for h in range(8):
                S.op("pe", lambda e, h=h: e.transpose(out=pb_[:, h * NS:(h + 1) * NS], in_=nb16[0:NS, h * 128:(h + 1) * 128], identity=ident_b[0:NS, 0:NS]),
                     [r_nb16, r_identb], [rps[7]])
            S.op("dve", lambda e: e.tensor_copy(out=dst, in_=pb_[:, 0:8 * NS].rearrange("p (h t) -> p h t", h=8)), [rps[7]], [r_dst])

        ks, rks = stg[0], r_stg[0]
        vs, rvs = stg[1], r_stg[1]
        for half in range(2):
            b = sproj(D, half)
            S.op("act", lambda e, b=b, half=half: e.copy(out=ks[0:NS, half * 512:(half + 1) * 512], in_=psum[b][0:NS, :]), [rps[b]], [rks])
        head_norm(ks, rks, 64, NS)
        outs.append(S.dma("sp", lambda e: e.dma_start(out=T["ks_out"][:, :], in_=ks[0:NS, :]), [rks], []))
        S.op("act", lambda e: e.copy(out=nb16[0:NS, :], in_=ks[0:NS, :]), [rks], [r_nb16])
        heads_T(ktn, r_ktn)
        for half in range(2):
            b = sproj(2 * D, half)
            S.op("act", lambda e, b=b, half=half: e.copy(out=vs[0:NS, half * 512:(half + 1) * 512], in_=psum[b][0:NS, :]), [rps[b]], [rvs])
        S.op("dve", lambda e: e.tensor_copy(out=vnew[0:NS, :, 0:128], in_=vs[0:NS, :].rearrange("p (h d) -> p h d", h=8)), [rvs], [r_vnew])
        outs.append(S.dma("sp", lambda e: e.dma_start(out=T["vs_out"][:, :], in_=vs[0:NS, :]), [rvs], []))
        for half in range(2):
            b = sproj(0, half)
            S.op("act", lambda e, b=b, half=half: e.copy(out=ks[0:NS, half * 512:(half + 1) * 512], in_=psum[b][0:NS, :]), [rps[b]], [rks])
        head_norm(ks, rks, 0, NS)
        S.op("act", lambda e: e.copy(out=nb16[0:NS, :], in_=ks[0:NS, :]), [rks], [r_nb16])
        heads_T(QTs, r_QTs)
        for half in range(2):
            b = sproj(3 * D, half)
            S.op("act", lambda e, b=b, half=half: e.activation(out=sg[0:NS, 0, half * 512:(half + 1) * 512], in_=psum[b][0:NS, :], func=AF.Silu), [rps[b]], [r_sg])

        for b in range(3):
            S.op("dve", lambda e, b=b: e.memset(psum[b][0:32, :], 0.0), [], [rps[b]])

        def pv(P2d_of_h, rP, vt, r_vt, np_):
            for h in range(8):
                S.op("pe", lambda e, h=h: e.matmul(psum[h // 3][0:32, (h % 3) * 129:(h % 3 + 1) * 129], lhsT=P2d_of_h(h), rhs=vt[0:np_, h, :],
                                                   start=False, stop=False, skip_group_check=True), [rP, r_vt], [rps[h // 3]])

        npages = DBG.get("npg", NPAGE)
        ck = T["cache_k"]
        cv3 = T["cache_v"].rearrange("r (h d) -> r h d", h=8)
        def stage_a(e_, j):
            if True:
                if True:
                    gi = e_ * NPAGE + j
                    bi = gi % 2
                    S.dma("pool", lambda e, gi=gi, bi=bi: e.indirect_dma_start(out=kpg[bi], out_offset=None, in_=ck[:, :],
                                                                            in_offset=bass.IndirectOffsetOnAxis(ap=idx[:, gi:gi + 1], axis=0)), [r_idx], [r_kpg[bi]])
                    S.dma("pool", lambda e, gi=gi, bi=bi: e.indirect_dma_start(out=vfl[bi], out_offset=None, in_=T["cache_v"][:, :],
                                                                            in_offset=bass.IndirectOffsetOnAxis(ap=idx[:, gi:gi + 1], axis=0)), [r_idx], [r_vfl[bi]])
                    S.op("act" if gi % 2 else "dve",
                         (lambda e, bi=bi: e.copy(out=vpg[bi][:, :, 0:128], in_=vfl[bi].rearrange("p (h d) -> p h d", h=8))) if gi % 2 else
                         (lambda e, bi=bi: e.tensor_copy(out=vpg[bi][:, :, 0:128], in_=vfl[bi].rearrange("p (h d) -> p h d", h=8))), [r_vfl[bi]], [r_vpg[bi]])
                    pb7 = ps_bf(7)
                    for h in range(8):
                        S.op("pe", lambda e, h=h, bi=bi: e.transpose(out=pb7[:, h * 128:(h + 1) * 128], in_=kpg[bi][:, h * 128:(h + 1) * 128], identity=ident_b),
                             [r_kpg[bi], r_identb], [rps[7]])
                    S.op("dve" if gi % 2 else "act",
                         (lambda e, bi=bi: e.tensor_copy(out=ktp[bi], in_=pb7.rearrange("p (h t) -> p h t", h=8))) if gi % 2 else
                         (lambda e, bi=bi: e.copy(out=ktp[bi], in_=pb7.rearrange("p (h t) -> p h t", h=8))), [rps[7]], [r_ktp[bi]])

        def stage_b(e_, j):
            if True:
                if True:
                    gi = e_ * NPAGE + j
                    bi = gi % 2
                    sbk = 3 + 2 * (gi % 2)
                    for h in range(8):
                        for c in range(2):
                            S.op("pe", lambda e, h=h, c=c, bi=bi, sbk=sbk, e_=e_: e.matmul(psum[sbk + c][:, h * 4:(h + 1) * 4], lhsT=ktp[bi][c * 64:(c + 1) * 64, h, :],
                                                                                        rhs=QTs[c * 64:(c + 1) * 64, h, e_ * 4:(e_ + 1) * 4], start=True, stop=True),
                                 [r_ktp[bi], r_QTs], [rps[sbk + c]])
                    P = pte[e_][bi]; rP = r_pte[e_][bi]
                    for c in range(2):
                        S.op("act", lambda e, c=c, P=P, sbk=sbk, e_=e_: e.activation(out=P[:, :, c, e_ * 4:(e_ + 1) * 4], in_=psum[sbk + c][:, 0:32].rearrange("p (h q) -> p h q", h=8),
                                                                                 func=AF.Exp, scale=0.125, bias=lam[:, 1:2]), [rps[sbk + c], r_lam], [rP])
                    pv(lambda h, P=P: P[:, h].rearrange("p c q -> p (c q)"), rP, vpg[bi], r_vpg[bi], 128)

        pages = [(e_, j) for e_ in range(4) for j in range(npages)]
        for n_, (e_, j) in enumerate(pages):
            stage_a(e_, j)
            if n_ >= 1:
                stage_b(*pages[n_ - 1])
        if pages:
            stage_b(*pages[-1])
        for h in range(8):
            for c in range(2):
                S.op("pe", lambda e, h=h, c=c: e.matmul(psum[3 + c][0:NS, h * NS:(h + 1) * NS], lhsT=ktn[c * 64:(c + 1) * 64, h, :], rhs=QTs[c * 64:(c + 1) * 64, h, :],
                                                        start=True, stop=True), [r_ktn, r_QTs], [rps[3 + c]])
        for c in range(2):
            S.op("act", lambda e, c=c: e.activation(out=ptn[0:NS, :, c, :], in_=psum[3 + c][0:NS, 0:8 * NS].rearrange("p (h q) -> p h q", h=8),
                                                    func=AF.Exp, scale=0.125, bias=lam[0:NS, 1:2]), [rps[3 + c], r_lam], [r_ptn])
            S.op("dve", lambda e, c=c: e.tensor_tensor(out=ptn[0:NS, :, c, :], in0=ptn[0:NS, :, c, :], in1=smb[0:NS, :].unsqueeze(1).broadcast_to([NS, 8, NS]), op=ALU.mult),
                 [r_ptn, r_smb], [r_ptn])
        pv(lambda h: ptn[0:NS, h].rearrange("p c q -> p (c q)"), r_ptn, vnew, r_vnew, NS)

        for b in range(3):
            n = 387 if b < 2 else 258
            S.op("act" if b % 2 else "dve",
                 (lambda e, b=b, n=n: e.copy(out=O1[0:32, b * 387:b * 387 + n], in_=psum[b][0:32, 0:n])) if b % 2 else
                 (lambda e, b=b, n=n: e.tensor_copy(out=O1[0:32, b * 387:b * 387 + n], in_=psum[b][0:32, 0:n])), [rps[b]], [r_O1])
        for i, (c0, n) in enumerate(((0, 512), (512, 512), (1024, 8))):
            S.op("pe", lambda e, i=i, c0=c0, n=n: e.matmul(psum[3 + i][0:NS, 0:n], lhsT=ident_f[0:32, 16:32], rhs=O1[0:32, c0:c0 + n], start=True, stop=True),
                 [r_O1, r_identf], [rps[3 + i]])
            S.op("dve", lambda e, i=i, c0=c0, n=n: e.tensor_copy(out=O2[0:NS, c0:c0 + n], in_=psum[3 + i][0:NS, 0:n]), [rps[3 + i]], [r_O2])
        O1v = O1[0:NS, :].rearrange("p (h d) -> p h d", h=8)
        O2v = O2[0:NS, :].rearrange("p (h d) -> p h d", h=8)
        S.op("dve", lambda e: e.reciprocal(out=rr[0:NS, 0:8], in_=O1v[:, :, 128]), [r_O1], [r_rr])
        S.op("dve", lambda e: e.reciprocal(out=rr[0:NS, 8:16], in_=O2v[:, :, 128]), [r_O2], [r_rr])
        S.op("dve", lambda e: e.tensor_scalar(out=rr[0:NS, 8:16], in0=rr[0:NS, 8:16], scalar1=lam[0:NS, 0:1], scalar2=None, op0=ALU.mult), [r_rr, r_lam], [r_rr])
        o_a = ks[0:NS, :].rearrange("p (h d) -> p h d", h=8)
        o_b = vs[0:NS, :].rearrange("p (h d) -> p h d", h=8)
        S.op("dve", lambda e: e.tensor_tensor(out=o_a, in0=O1v[:, :, 0:128], in1=rr[0:NS, 0:8].unsqueeze(2).broadcast_to([NS, 8, 128]), op=ALU.mult), [r_O1, r_rr], [rks])
        S.op("dve", lambda e: e.tensor_tensor(out=o_b, in0=O2v[:, :, 0:128], in1=rr[0:NS, 8:16].unsqueeze(2).broadcast_to([NS, 8, 128]), op=ALU.mult), [r_O2, r_rr], [rvs])
        S.op("dve", lambda e: e.tensor_tensor(out=o_a, in0=o_a, in1=o_b, op=ALU.add), [rks, rvs], [rks])
        S.op("pool", lambda e: e.tensor_tensor(out=sq[0:NS], in0=ks[0:NS], in1=ks[0:NS], op=ALU.mult), [rks], [r_sq])
        S.op("dve", lambda e: e.tensor_reduce(out=rs16[0:NS, 0:8], in_=sq[0:NS].rearrange("p (h d) -> p h d", h=8), axis=AX.X, op=ALU.add), [r_sq], [r_rs16])
        rms_rstd(rs16[0:NS, 0:8], 128, [r_rs16])
        S.op("dve", lambda e: e.tensor_tensor(out=o_a, in0=o_a, in1=rs16[0:NS, 0:8].unsqueeze(2).broadcast_to([NS, 8, 128]), op=ALU.mult), [rks, r_rs16], [rks])
        S.op("dve", lambda e: e.tensor_tensor(out=o_a, in0=o_a, in1=sublnw[0:NS, :].unsqueeze(1).broadcast_to([NS, 8, 128]), op=ALU.mult), [rks, r_subw], [rks])
        S.op("dve", lambda e: e.tensor_tensor(out=og[0:NS, 0, :], in0=ks[0:NS, :], in1=sg[0:NS, 0, :], op=ALU.mult), [rks, r_sg], [r_og])
        pb_ = ps_bf(7)
        for h in range(8):
            S.op("pe", lambda e, h=h: e.transpose(out=pb_[:, h * NS:(h + 1) * NS], in_=og[0:NS, 0, h * 128:(h + 1) * 128], identity=ident_b[0:NS, 0:NS]),
                 [r_og, r_identb], [rps[7]])
        S.op("act", lambda e: e.copy(out=ogT[:, :, 0:NS], in_=pb_[:, 0:8 * NS].rearrange("p (h t) -> p h t", h=8)), [rps[7]], [r_ogT])
        for half in range(2):
            b = 5 + half
            for kc in range(8):
                S.op("pe", lambda e, half=half, kc=kc, b=b: e.matmul(psum[b][0:NS, :], lhsT=ogT[:, kc, 0:NS], rhs=woc[:, kc, half * 512:(half + 1) * 512],
                                                                   start=(kc == 0), stop=(kc == 7)), [r_ogT, r_woc], [rps[b]])
            S.op("dve", lambda e, half=half, b=b: e.tensor_tensor(out=vs[0:NS, half * 512:(half + 1) * 512], in0=psum[b][0:NS, :],
                                                                in1=x1s[0:NS, half * 512:(half + 1) * 512], op=ALU.add), [rps[b], r_x1s], [rvs])
        outs.append(S.dma("sp", lambda e: e.dma_start(out=T["ys_out"][:, :], in_=vs[0:NS, :]), [rvs], []))

    if DBG["st"]:
        sample_l1()

    S.emit(nc, final_waits=outs)
    st.close()
    return nc


_CACHE = {}
_HOOK = [None]


def _get_nc():
    if "nc" not in _CACHE:
        nc, S, T = build_nc()
        emit_program(nc, S, T)
        _CACHE["nc"] = nc
    return _CACHE["nc"]


def kernel(x_prompt, x_sample, state_pool, state_conv, cache_k, cache_v, page_table,
           norm_w_ab, w_in_ab, pool_w, pool_scale, conv_w, conv_b, conv_ln_w, conv_ln_b, w_out_ab,
           norm_w_c, w_in_c, q_norm_w, k_norm_w, lambda_q1, lambda_k1, lambda_q2, lambda_k2,
           subln_w, w_out_c):
    f = lambda a: np.ascontiguousarray(np.asarray(a, dtype=np.float32))
    x_prompt = f(x_prompt)
    pvec_src = np.ascontiguousarray(np.concatenate([f(conv_w)[0], f(conv_b), f(conv_ln_w), f(conv_ln_b), f(pool_scale)], axis=0))
    qk_w = np.ascontiguousarray(np.concatenate([f(q_norm_w), f(k_norm_w)], axis=1))
    lam_v = np.ascontiguousarray(np.concatenate([f(lambda_q1), f(lambda_k1), f(lambda_q2), f(lambda_k2)], axis=1))
    shared = {
        "norm_w_ab": f(norm_w_ab), "w_in_ab": f(w_in_ab)[0], "pool_w": f(pool_w)[0], "pvec_src": pvec_src,
        "w_out_ab": f(w_out_ab)[0], "norm_w_c": f(norm_w_c), "w_in_c": f(w_in_c)[0], "qk_w": qk_w, "lam_v": lam_v,
        "subln_w": f(subln_w), "w_out_c": f(w_out_c)[0],
    }
    p = np.arange(128)
    x_sample = f(x_sample); state_pool = f(state_pool); state_conv = f(state_conv)
    ck2 = f(cache_k).reshape(-1, D)
    cv2 = f(cache_v).reshape(-1, D)
    kk = np.arange(16)
    smask = ((kk[:, None] // 4 == kk[None, :] // 4) & (kk[:, None] % 4 <= kk[None, :] % 4)).astype(np.float32)
    in_maps = []
    for c in range(8):
        b, hf = c // 2, c % 2
        own_rows = np.zeros((128, 16), np.int32)
        for s_ in range(8):
            for blk in range(2):
                own_rows[:, s_ * 2 + blk] = (2 * s_ + hf) * TT + blk * 128 + p
        am = np.zeros((128, 2, 2, TT), np.float32)
        q = np.arange(TT)
        for kt2 in range(2):
            for kblk in range(2):
                am[:, kt2, kblk, :] = ((kt2 - hf) * TT + kblk * 128 + p[:, None] <= q[None, :]).astype(np.float32)
        m = dict(shared)
        m["xs"] = np.ascontiguousarray(x_sample[4 * c:4 * c + 4].reshape(16, D))
        m["spool"] = np.ascontiguousarray(state_pool[0, 4 * c:4 * c + 4])
        m["sconv"] = np.ascontiguousarray(state_conv[0, 4 * c:4 * c + 4])
        m["pt"] = np.ascontiguousarray(np.asarray(page_table, dtype=np.int32)[4 * c:4 * c + 4].reshape(1, 256))
        m["smask"] = smask
        m["cache_k"] = ck2
        m["cache_v"] = cv2
        m["xp"] = x_prompt[b]
        m["own_rows"] = own_rows
        m["amask"] = am.reshape(128, 4 * TT)
        in_maps.append(m)
    nc = _get_nc()
    res = run_bass_kernel_spmd(nc, in_maps, core_ids=list(range(8)))
    R = res.results
    if _HOOK[0] is not None:
        _HOOK[0](R)
    y_prompt = np.zeros((NBATCH, SEQ, D), np.float32)
    k_p = np.zeros((1, NBATCH, SEQ, 8, 128), np.float32)
    v_p = np.zeros((1, NBATCH, SEQ, 8, 128), np.float32)
    pool_p = np.zeros((1, NBATCH, 15, D), np.float32)
    conv_p = np.zeros((1, NBATCH, 30, D), np.float32)
    for c in range(8):
        b, hf = c // 2, c % 2
        yo = R[c]["y_own"].reshape(8, TT, D)
        for s_ in range(8):
            t = 2 * s_ + hf
            y_prompt[b, t * TT:(t + 1) * TT] = yo[s_]
        if hf == 0:
            k_p[0, b] = R[c]["k_all"].reshape(SEQ, 8, 128)
            v_p[0, b] = R[c]["v_all"].reshape(SEQ, 8, 128)
            pool_p[0, b] = R[c]["pool_st"]
            conv_p[0, b] = R[c]["conv_st"]
    ys = np.zeros((32, 4, D), np.float32)
    pool_s = np.zeros((1, 32, 15, D), np.float32)
    conv_s = np.zeros((1, 32, 30, D), np.float32)
    k_s = np.zeros((1, 32, 4, 8, 128), np.float32)
    v_s = np.zeros((1, 32, 4, 8, 128), np.float32)
    for c in range(8):
        ys[4 * c:4 * c + 4] = R[c]["ys_out"].reshape(4, 4, D)
        pool_s[0, 4 * c:4 * c + 4] = R[c]["pool_s_out"]
        conv_s[0, 4 * c:4 * c + 4] = R[c]["conv_s_out"]
        k_s[0, 4 * c:4 * c + 4] = R[c]["ks_out"].reshape(4, 4, 8, 128)
        v_s[0, 4 * c:4 * c + 4] = R[c]["vs_out"].reshape(4, 4, 8, 128)
    return (y_prompt, ys, pool_p, conv_p, k_p, v_p, pool_s, conv_s, k_s, v_s)
```
